# Optimizing a Trainium2 kernel written in Bass

```python
import math
import jax, jax.numpy as jnp
from jax import lax
import numpy as np

D_MODEL = 2048
BATCH = 8
SEQ = 2048
DEPTH = 2

N_MIXERS = 2
N_RET_LAYERS = (DEPTH + 1) // 2
N_MOBA_LAYERS = DEPTH // 2
DEEPNORM_ALPHA = (2.0 * DEPTH) ** 0.25
DEEPNORM_BETA = (8.0 * DEPTH) ** -0.25
LN_EPS = 1e-5
ADA_SCALE = 0.1

RET_HEADS = 8
RET_DK = D_MODEL // RET_HEADS
RET_DV = 2 * D_MODEL // RET_HEADS
RET_QK = RET_HEADS * RET_DK
RET_V = RET_HEADS * RET_DV
RET_GATE = RET_V
RET_IN = 2 * RET_QK + RET_V + RET_GATE
RET_CHUNK = 128
RET_THETA = 10000.0
RET_GN_EPS = 1e-5

MOBA_HEADS = 16
MOBA_DH = D_MODEL // MOBA_HEADS
MOBA_W = MOBA_HEADS * MOBA_DH
MOBA_IN = 4 * MOBA_W
MOBA_BLOCK = 256
MOBA_TOPK = 3
MOBA_QCHUNK = 8
ROPE_THETA = 500000.0
ROPE_DIMS = MOBA_DH // 4
NEG = -1e30

kernel_name = "hybrid_retention_moba_adaln_deepnorm"


def _rope(x, pos, n_rot, theta):
    half = n_rot // 2
    inv = theta ** (-jnp.arange(half, dtype=jnp.float32) * 2.0 / n_rot)
    ang = pos.astype(jnp.float32)[..., None] * inv
    cos = jnp.cos(ang)[:, :, None, :].astype(x.dtype)
    sin = jnp.sin(ang)[:, :, None, :].astype(x.dtype)
    x1 = x[..., :half]
    x2 = x[..., half:n_rot]
    return jnp.concatenate([x1 * cos - x2 * sin, x2 * cos + x1 * sin, x[..., n_rot:]], axis=-1)


def _layernorm(x, g, b):
    xf = x.astype(jnp.float32)
    mu = jnp.mean(xf, axis=-1, keepdims=True)
    var = jnp.mean(jnp.square(xf - mu), axis=-1, keepdims=True)
    y = (xf - mu) * lax.rsqrt(var + LN_EPS)
    return (y * g.astype(jnp.float32) + b.astype(jnp.float32)).astype(x.dtype)


def _retention(h, w_in, w_out, positions):
    B, S, _ = h.shape
    proj = h @ w_in
    q = proj[..., :RET_QK].reshape(B, S, RET_HEADS, RET_DK)
    k = proj[..., RET_QK:2 * RET_QK].reshape(B, S, RET_HEADS, RET_DK)
    v = proj[..., 2 * RET_QK:2 * RET_QK + RET_V].reshape(B, S, RET_HEADS, RET_DV)
    g = proj[..., 2 * RET_QK + RET_V:]
    q = _rope(q, positions, RET_DK, RET_THETA)
    k = _rope(k, positions, RET_DK, RET_THETA) * (RET_DK ** -0.5)

    n_chunks = S // RET_CHUNK
    def to_chunks(t):
        return t.reshape(B, n_chunks, RET_CHUNK, RET_HEADS, t.shape[-1]).transpose(1, 0, 3, 2, 4)
    qc, kc, vc = to_chunks(q), to_chunks(k), to_chunks(v)

    log_gamma = jnp.log(1.0 - 2.0 ** (-5.0 - jnp.arange(RET_HEADS, dtype=jnp.float32)))
    n = jnp.arange(RET_CHUNK, dtype=jnp.float32)
    diff = n[:, None] - n[None, :]
    inner_decay = jnp.where(diff >= 0, jnp.exp(jnp.maximum(diff, 0.0) * log_gamma[:, None, None]), 0.0).astype(h.dtype)
    q_decay = jnp.exp((n + 1.0) * log_gamma[:, None]).astype(h.dtype)
    k_decay = jnp.exp((RET_CHUNK - 1.0 - n) * log_gamma[:, None]).astype(h.dtype)
    chunk_decay = jnp.exp(RET_CHUNK * log_gamma).astype(h.dtype)

    def step(state, xs):
        qi, ki, vi = xs
        scores = jnp.einsum('bhnd,bhmd->bhnm', qi, ki) * inner_decay
        inner = jnp.einsum('bhnm,bhmv->bhnv', scores, vi)
        cross = jnp.einsum('bhnd,bhdv->bhnv', qi, state) * q_decay[None, :, :, None]
        new_state = state * chunk_decay[None, :, None, None] + jnp.einsum(
            'bhmd,bhmv->bhdv', ki * k_decay[None, :, :, None], vi)
        return new_state, inner + cross

    state0 = jnp.zeros((B, RET_HEADS, RET_DK, RET_DV), dtype=qc.dtype)
    _, o = lax.scan(step, state0, (qc, kc, vc))
    o = o.transpose(1, 0, 3, 2, 4).reshape(B, S, RET_HEADS, RET_DV)
    of = o.astype(jnp.float32)
    mu = jnp.mean(of, axis=-1, keepdims=True)
    var = jnp.mean(jnp.square(of - mu), axis=-1, keepdims=True)
    o = ((of - mu) * lax.rsqrt(var + RET_GN_EPS)).astype(h.dtype).reshape(B, S, RET_V)
    return (o * jax.nn.silu(g)) @ w_out


def _moba(h, w_in, w_out, positions):
    B, S, _ = h.shape
    proj = h @ w_in
    q = proj[..., :MOBA_W].reshape(B, S, MOBA_HEADS, MOBA_DH)
    k = proj[..., MOBA_W:2 * MOBA_W].reshape(B, S, MOBA_HEADS, MOBA_DH)
    v = proj[..., 2 * MOBA_W:3 * MOBA_W].reshape(B, S, MOBA_HEADS, MOBA_DH)
    g = proj[..., 3 * MOBA_W:]
    q = _rope(q, positions, ROPE_DIMS, ROPE_THETA).transpose(0, 2, 1, 3)
    k = _rope(k, positions, ROPE_DIMS, ROPE_THETA).transpose(0, 2, 1, 3)
    v = v.transpose(0, 2, 1, 3)

    n_blocks = -(-S // MOBA_BLOCK)
    pad = n_blocks * MOBA_BLOCK - S
    kb = jnp.pad(k, ((0, 0), (0, 0), (0, pad), (0, 0))).reshape(B, MOBA_HEADS, n_blocks, MOBA_BLOCK, MOBA_DH)
    vb = jnp.pad(v, ((0, 0), (0, 0), (0, pad), (0, 0))).reshape(B, MOBA_HEADS, n_blocks, MOBA_BLOCK, MOBA_DH)
    k_mean = jnp.mean(kb, axis=3)

    t = jnp.arange(S)
    blk_t = t // MOBA_BLOCK
    gate = jnp.einsum('bhtd,bhnd->bhtn', q, k_mean).astype(jnp.float32)
    past = jnp.arange(n_blocks)[None, :] < blk_t[:, None]
    gate = jnp.where(past, gate, NEG)
    topk = min(MOBA_TOPK, n_blocks)
    _, top_idx = lax.top_k(gate, topk)
    valid = top_idx < blk_t[None, None, :, None]

    n_q = S // MOBA_QCHUNK
    def qchunks(a):
        a = a.reshape(B, MOBA_HEADS, n_q, MOBA_QCHUNK, *a.shape[3:])
        return jnp.moveaxis(a, 2, 0)
    qs, idxs, valids = qchunks(q), qchunks(top_idx), qchunks(valid)
    scale = MOBA_DH ** -0.5
    bi = jnp.arange(B)[:, None, None, None]
    hi = jnp.arange(MOBA_HEADS)[None, :, None, None]

    def chunk_fn(args):
        ci, qc, idxc, validc = args
        tq = ci * MOBA_QCHUNK + jnp.arange(MOBA_QCHUNK)
        own = (ci * MOBA_QCHUNK) // MOBA_BLOCK
        k_own = lax.dynamic_index_in_dim(kb, own, axis=2, keepdims=False)
        v_own = lax.dynamic_index_in_dim(vb, own, axis=2, keepdims=False)
        kpos_own = own * MOBA_BLOCK + jnp.arange(MOBA_BLOCK)
        s_own = jnp.einsum('bhqd,bhkd->bhqk', qc, k_own).astype(jnp.float32) * scale
        s_own = jnp.where(kpos_own[None, :] <= tq[:, None], s_own, NEG)
        kg = kb[bi, hi, idxc]
        vg = vb[bi, hi, idxc]
        s_sel = jnp.einsum('bhqd,bhqnkd->bhqnk', qc, kg).astype(jnp.float32) * scale
        s_sel = jnp.where(validc[..., None], s_sel, NEG)
        logits = jnp.concatenate([s_sel.reshape(*s_sel.shape[:3], -1), s_own], axis=-1)
        p = jax.nn.softmax(logits, axis=-1).astype(qc.dtype)
        n_sel = topk * MOBA_BLOCK
        p_sel = p[..., :n_sel].reshape(s_sel.shape)
        p_own = p[..., n_sel:]
        return (jnp.einsum('bhqnk,bhqnkd->bhqd', p_sel, vg)
                + jnp.einsum('bhqk,bhkd->bhqd', p_own, v_own))

    o = lax.map(chunk_fn, (jnp.arange(n_q), qs, idxs, valids))
    o = o.transpose(1, 0, 3, 2, 4).reshape(B, S, MOBA_W)
    return (o * jax.nn.silu(g)) @ w_out


def setup_inputs(seed: int = 0) -> dict:
    key = jax.random.key(seed)
    ks = jax.random.split(key, 16)
    D = D_MODEL
    x = jax.random.normal(ks[0], (BATCH, SEQ, D), jnp.float32)
    c = jax.random.normal(ks[1], (BATCH, D), jnp.float32)
    offs = jax.random.randint(ks[2], (BATCH, 1), 0, 4096, dtype=jnp.int32)
    positions = (offs + jnp.arange(SEQ, dtype=jnp.int32)[None, :]).astype(jnp.int32)

    ret_w_in = jax.random.normal(ks[3], (N_RET_LAYERS, D, RET_IN), jnp.float32) * D ** -0.5
    col = jnp.arange(RET_IN)
    v_cols = (col >= 2 * RET_QK) & (col < 2 * RET_QK + RET_V)
    ret_w_in = ret_w_in * jnp.where(v_cols, DEEPNORM_BETA, 1.0).astype(jnp.float32)
    ret_w_out = jax.random.normal(ks[4], (N_RET_LAYERS, RET_V, D), jnp.float32) * (RET_V ** -0.5 * DEEPNORM_BETA)

    moba_w_in = jax.random.normal(ks[5], (N_MOBA_LAYERS, D, MOBA_IN), jnp.float32) * D ** -0.5
    mcol = jnp.arange(MOBA_IN)
    mv_cols = (mcol >= 2 * MOBA_W) & (mcol < 3 * MOBA_W)
    moba_w_in = moba_w_in * jnp.where(mv_cols, DEEPNORM_BETA, 1.0).astype(jnp.float32)
    moba_w_out = jax.random.normal(ks[6], (N_MOBA_LAYERS, MOBA_W, D), jnp.float32) * (MOBA_W ** -0.5 * DEEPNORM_BETA)

    w_ada = jax.random.normal(ks[7], (DEPTH, D, 3 * D), jnp.float32) * (ADA_SCALE * D ** -0.5)
    b_ada = 0.02 * jax.random.normal(ks[8], (DEPTH, 3 * D), jnp.float32)
    b_ada = b_ada + jnp.concatenate([jnp.zeros((2 * D,)), jnp.ones((D,))]).astype(jnp.float32)[None, :]
    ln_g = 1.0 + 0.02 * jax.random.normal(ks[9], (DEPTH, D), jnp.float32)
    ln_b = 0.02 * jax.random.normal(ks[10], (DEPTH, D), jnp.float32)
    return {"x": x, "c": c, "positions": positions,
            "ret_w_in": ret_w_in, "ret_w_out": ret_w_out,
            "moba_w_in": moba_w_in, "moba_w_out": moba_w_out,
            "w_ada": w_ada, "b_ada": b_ada, "ln_g": ln_g, "ln_b": ln_b}


def reference(x, c, positions, ret_w_in, ret_w_out, moba_w_in, moba_w_out, w_ada, b_ada, ln_g, ln_b):
    D = x.shape[-1]
    for layer in range(DEPTH):
        mod = c @ w_ada[layer] + b_ada[layer]
        shift = mod[:, None, :D]
        scale = mod[:, None, D:2 * D]
        gate = mod[:, None, 2 * D:]
        h = x * (1.0 + scale) + shift
        if layer % N_MIXERS == 0:
            y = _retention(h, ret_w_in[layer // N_MIXERS], ret_w_out[layer // N_MIXERS], positions)
        else:
            y = _moba(h, moba_w_in[layer // N_MIXERS], moba_w_out[layer // N_MIXERS], positions)
        x = _layernorm(DEEPNORM_ALPHA * x + gate * y, ln_g[layer], ln_b[layer])
    return x
```

```python
import contextlib
import math
import numpy as np
import ml_dtypes
import concourse.bass as bass
import concourse.mybir as mybir
from concourse.bass_utils import run_bass_kernel_spmd

F32 = mybir.dt.float32
BF16 = mybir.dt.bfloat16
I32 = mybir.dt.int32
AF = mybir.ActivationFunctionType
ALU = mybir.AluOpType
AX = mybir.AxisListType

D = 2048
SEQ = 2048
NT = 16
KC = 16
ALPHA = float((2.0 * 2) ** 0.25)
LN_EPS = 1e-5
GN_EPS = 1e-5
BIG = 30000.0
TWO_PI_LO = 6.28318
COMPUTE = ("pe", "dve", "act", "pool")


class Sched:
    def __init__(self, nc, stack):
        self.nc = nc
        self.stack = stack
        self.sems = {}
        self.cum = {}
        self.known = {e: {} for e in ("pe", "dve", "act", "pool", "sp")}
        self.begin()
        for e in COMPUTE:
            self._sem("eng_" + e)

    def begin(self):
        self.q = {e: [] for e in ("pe", "dve", "act", "pool", "sp")}
        self.res = {}

    def _sem(self, key):
        if key not in self.sems:
            self.sems[key] = self.stack.enter_context(self.nc.semaphore(key))
            self.cum[key] = 0
        return self.sems[key]

    def _need(self, eng, tok, needs, kind):
        if tok is None:
            return
        key, val, teng = tok
        if teng == eng:
            if eng == "pe" or kind != "raw":
                return
        if self.known[eng].get(key, 0) >= val:
            return
        if needs.get(key, 0) < val:
            needs[key] = val

    def op(self, eng, fn, reads=(), writes=(), dma_sem=None):
        needs = {}
        for r in reads:
            st = self.res.get(r)
            if st is None:
                continue
            self._need(eng, st[0], needs, "raw")
            if r.startswith("ps"):
                for t in st[1].values():
                    self._need(eng, t, needs, "war")
        for w in writes:
            st = self.res.get(w)
            if st is None:
                continue
            self._need(eng, st[0], needs, "waw")
            for t in st[1].values():
                self._need(eng, t, needs, "war")
        for k, v in needs.items():
            self.known[eng][k] = v
        if dma_sem is not None:
            self._sem(dma_sem)
            self.cum[dma_sem] += 16
            tok = (dma_sem, self.cum[dma_sem], "dma")
            inc = (dma_sem, 16)
        else:
            key = "eng_" + eng
            self.cum[key] += 1
            tok = (key, self.cum[key], eng)
            inc = (key, 1)
        self.q[eng].append((list(needs.items()), fn, inc))
        for r in reads:
            st = self.res.setdefault(r, [None, {}])
            st[1][tok[0]] = tok
        for w in writes:
            self.res[w] = [tok, {}]
        return tok

    def finish(self):
        nc = self.nc
        waits = [(k, v) for k, v in self.cum.items() if not k.startswith("eng_") and v > 0]
        self.q["sp"].append((waits, None, None))
        handles = {"pe": "tensor", "dve": "vector", "act": "scalar", "pool": "gpsimd", "sp": "sync"}
        with nc.Block() as block:
            for e, hname in handles.items():
                queue = self.q[e]
                if not queue:
                    continue

                def body(engine, queue=queue):
                    for wl, fn, inc in queue:
                        for k, v in wl:
                            engine.wait_ge(self.sems[k], v)
                        if fn is not None:
                            fn(engine).then_inc(self.sems[inc[0]], inc[1])

                getattr(block, hname)(body)
        self.begin()


def host_consts():
    c = {}
    c["ident_f"] = np.eye(128, dtype=np.float32)
    c["ident_b"] = np.eye(128, dtype=np.float32).astype(ml_dtypes.bfloat16)
    lg = np.log(np.float32(1.0) - np.float32(2.0) ** (-5.0 - np.arange(8, dtype=np.float32))).astype(np.float32)
    n = np.arange(128, dtype=np.float32)
    diff = n[:, None] - n[None, :]
    inner = np.where(diff >= 0, np.exp(np.maximum(diff, 0.0)[None] * lg[:, None, None]), 0.0).astype(np.float32)
    c["ret_mask"] = np.ascontiguousarray(inner.transpose(2, 0, 1) / 16.0).astype(np.float32)
    qdec = np.exp((n[None, :] + 1.0) * lg[:, None]).astype(np.float32)
    c["ret_qdec"] = np.ascontiguousarray(np.broadcast_to(qdec[None], (128, 8, 128))).astype(np.float32)
    kdec = np.exp((127.0 - n[None, :]) * lg[:, None]).astype(np.float32)
    c["ret_kdec"] = np.ascontiguousarray(kdec.T / 16.0).astype(np.float32)
    cdec = [float(np.exp(np.float32(128.0) * lg[h])) for h in range(8)]
    i128 = np.arange(128, dtype=np.float64)
    c["ret_inv"] = ((10000.0 ** (-(i128 * 2.0 / 256.0))).astype(np.float32).astype(np.float64) / (2 * math.pi)).astype(np.float32).reshape(128, 1)
    i16 = np.arange(16, dtype=np.float64)
    minv = ((500000.0 ** (-(i16 * 2.0 / 32.0))).astype(np.float32).astype(np.float64) / (2 * math.pi)).astype(np.float32)
    c["moba_inv"] = np.ascontiguousarray(np.broadcast_to(minv[None], (128, 16))).astype(np.float32)
    m = np.arange(128)
    c["tri"] = (m[:, None] <= m[None, :]).astype(np.float32).astype(ml_dtypes.bfloat16)
    bi = np.zeros((128, 8, 128), np.float32)
    for b in range(8):
        bi[b, b, :] = 1.0
    c["blockind"] = bi.astype(ml_dtypes.bfloat16)
    negm = np.zeros((128, 8, 8), np.float32)
    for i8 in range(8):
        B = (8 + i8) // 2
        negm[:, i8, B:] = -1e30
    c["negmask"] = negm
    sbi = np.zeros((128, 16, 32), np.float32)
    for i in range(16):
        B = i // 2
        sbi[:, i, B + 1:8] = -BIG
    c["selb_init"] = sbi.astype(ml_dtypes.bfloat16)
    sst = np.zeros((128, 1024), np.float32)
    for t in range(8):
        B = t // 2
        sst[B + 1:8, t * 128:(t + 1) * 128] = -BIG
    c["selbT_static"] = sst.astype(ml_dtypes.bfloat16)
    return c, cdec


CONST_SPECS = [("ident_f", [128, 128], F32), ("ident_b", [128, 128], BF16), ("ret_mask", [128, 8, 128], F32),
               ("ret_qdec", [128, 8, 128], F32), ("ret_kdec", [128, 8], F32), ("ret_inv", [128, 1], F32),
               ("moba_inv", [128, 16], F32), ("tri", [128, 128], BF16), ("blockind", [128, 8, 128], BF16),
               ("negmask", [128, 8, 8], F32), ("selb_init", [128, 16, 32], BF16), ("selbT_static", [128, 1024], BF16)]


def build(layers=(0, 1), cdec=None):
    nc = bass.Bass("TRN2", target_bir_lowering=False)
    dr = {}

    def din(name, shape, dt):
        dr[name] = nc.dram_tensor(name, shape, dt, kind="ExternalInput").ap()

    din("x", [SEQ, D], F32)
    din("cT", [128, 16], F32)
    din("posb", [128, SEQ], I32)
    din("post", [128, 16], I32)
    din("w_ada", [2, D, 3 * D], F32)
    din("b_fm", [128, 2, 48], F32)
    din("b_gate", [128, 2, D], F32)
    din("lng", [128, 2, D], F32)
    din("lnb", [128, 2, D], F32)
    din("ret_w_in", [D, 12288], F32)
    din("ret_w_out", [4096, D], F32)
    din("moba_w_in", [D, 8192], F32)
    din("moba_w_out", [D, D], F32)
    for name, shape, dt in CONST_SPECS:
        din(name, shape, dt)
    out_d = nc.dram_tensor("out", [SEQ, D], F32, kind="ExternalOutput").ap()
    yT_d = [nc.dram_tensor("yT0", [4096, SEQ], BF16, kind="Internal").ap(),
            nc.dram_tensor("yT1", [2048, SEQ], BF16, kind="Internal").ap()]
    x1_d = nc.dram_tensor("x1s", [SEQ, D], F32, kind="Internal").ap()
    gate_d = nc.dram_tensor("gates", [2, 128, D], F32, kind="Internal").ap()

    with contextlib.ExitStack() as outer:
        S = Sched(nc, outer)

        uniq = {"n": 0}

        def sb(stack, name, shape, dt):
            uniq["n"] += 1
            return stack.enter_context(nc.sbuf_tensor("s%d_%s" % (uniq["n"], name), shape, dt))

        ps = [outer.enter_context(nc.psum_tensor(f"psb{i}", [128, 512], F32)) for i in range(8)]
        psb = [p[:].bitcast(BF16) for p in ps]
        PS = [f"ps{i}" for i in range(8)]

        ident_f = sb(outer, "ident_f", [128, 128], F32)
        ident_b = sb(outer, "ident_b", [128, 128], BF16)
        mod_fm = sb(outer, "mod_fm", [128, 2, 48], F32)
        s1p = sb(outer, "s1p", [128, 2, 16], F32)
        eps_t = sb(outer, "eps_t", [128, 1], F32)

        def mm(out, lhsT, rhs, start, stop, reads, writes, skip=False):
            S.op("pe", lambda e: e.matmul(out, lhsT=lhsT, rhs=rhs, start=start, stop=stop, skip_group_check=skip),
                 reads, writes)

        def tr(out, in_, ident, reads, writes):
            S.op("pe", lambda e: e.transpose(out=out, in_=in_, identity=ident), reads, writes)

        def act(out, in_, func, reads, writes, scale=1.0, bias=0.0):
            S.op("act", lambda e: e.activation(out=out, in_=in_, func=func, bias=bias, scale=scale), reads, writes)

        def ts(eng, out, in0, s1, s2, op0, op1, reads, writes):
            if op1 is None:
                S.op(eng, lambda e: e.tensor_scalar(out=out, in0=in0, scalar1=s1, scalar2=None, op0=op0), reads, writes)
            else:
                S.op(eng, lambda e: e.tensor_scalar(out=out, in0=in0, scalar1=s1, scalar2=s2, op0=op0, op1=op1),
                     reads, writes)

        def tt(eng, out, in0, in1, op, reads, writes):
            S.op(eng, lambda e: e.tensor_tensor(out=out, in0=in0, in1=in1, op=op), reads, writes)

        def cp(eng, out, in_, reads, writes):
            S.op(eng, lambda e: e.tensor_copy(out=out, in_=in_), reads, writes)

        def dma(q, out, in_, reads, writes, sem, **kw):
            S.op(q, lambda e: e.dma_start(out=out, in_=in_, **kw), reads, writes, dma_sem=sem)

        dma_sem_names = (["d_c%d" % i for i in range(8)] + ["d_wa0", "d_wa1", "d_wr0", "d_wr1", "d_wr2", "d_xs0", "d_xs1",
                         "d_yst0", "d_yst1", "d_yTb", "d_wo0", "d_wo1", "d_xt0", "d_xt1", "d_zo0", "d_zo1", "d_g", "d_pos"])
        for nme in dma_sem_names:
            S._sem(nme)
        allsems = list(S.sems.values())

        def clear_block():
            with nc.Block() as blk:
                @blk.gpsimd
                def _(g):
                    for s_ in allsems:
                        g.sem_clear(s_)

        clear_block()

        def phase0():
            with contextlib.ExitStack() as ph:
                wa = [sb(ph, f"wa{i}", [128, 3 * D], BF16) for i in range(2)]
                c_f = sb(ph, "c_f", [128, 16], F32)
                c_b = sb(ph, "c_b", [128, 16], BF16)
                ones_b = sb(ph, "ones_b", [128, 128], BF16)
                c_rep = sb(ph, "c_rep", [128, 16, 128], BF16)
                bfm = sb(ph, "bfm", [128, 2, 48], F32)
                bgate = sb(ph, "bgate", [128, 2, D], F32)
                grow = sb(ph, "grow0", [128, D], F32)
                dma("sp", ident_f[:], dr["ident_f"][:, :], [], ["ident_f"], "d_c0")
                dma("sp", ident_b[:], dr["ident_b"][:, :], [], ["ident_b"], "d_c1")
                dma("sp", c_f[:], dr["cT"][:, :], [], ["c_f"], "d_c2")
                dma("sp", bfm[:], dr["b_fm"][:, :, :], [], ["bfm"], "d_c3")
                dma("sp", bgate[:], dr["b_gate"][:, :, :], [], ["bgate"], "d_c4")
                S.op("dve", lambda e: e.memset(eps_t[:], LN_EPS), [], ["eps_t"])
                S.op("dve", lambda e: e.memset(ones_b[:], 1.0), [], ["ones_b"])
                cp("dve", c_b[:], c_f[:], ["c_f"], ["c_b"])
                for ch in range(16):
                    ts("dve", c_rep[:, ch, :], ones_b[:], c_f[:, ch:ch + 1], None, ALU.mult, None,
                       ["ones_b", "c_f"], ["c_rep"])
                cnt = 0
                for l in range(2):
                    for ch in range(16):
                        b = cnt % 2
                        cnt += 1
                        dma("pool", wa[b][:], dr["w_ada"][l, ch * 128:(ch + 1) * 128, :], [], [f"wa{b}"], f"d_wa{b}",
                            max_dma_last_dim=8192)
                        for jt in range(48):
                            mm(ps[0][:, jt:jt + 1], wa[b][:, jt * 128:(jt + 1) * 128], c_b[:, ch:ch + 1],
                               (ch == 0 and jt == 0), (ch == 15), [f"wa{b}", "c_b"], [PS[0]], skip=True)
                        for n_ in range(4):
                            mm(ps[1 + n_][:, :], c_rep[:, ch, :], wa[b][:, 4096 + n_ * 512:4096 + (n_ + 1) * 512],
                               ch == 0, ch == 15, [f"wa{b}", "c_rep"], [PS[1 + n_]])
                    tt("dve", mod_fm[:, l, :], ps[0][:, 0:48], bfm[:, l, :], ALU.add, [PS[0], "bfm"], ["mod_fm"])
                    ts("dve", s1p[:, l, :], mod_fm[:, l, 16:32], 1.0, None, ALU.add, None, ["mod_fm"], ["s1p"])
                    for n_ in range(4):
                        tt("dve", grow[:, n_ * 512:(n_ + 1) * 512], ps[1 + n_][:, :], bgate[:, l, n_ * 512:(n_ + 1) * 512],
                           ALU.add, [PS[1 + n_], "bgate"], ["grow"])
                    dma("sp", gate_d[l, :, :], grow[:], ["grow"], ["gate_d"], "d_g")
                S.finish()

        def phase1(l, xsrc, hT):
            with contextlib.ExitStack() as ph:
                xs = [sb(ph, f"xs{i}", [128, 4, D], F32) for i in range(2)]
                for g in range(4):
                    b = g % 2
                    dma("sp", xs[b][:], xsrc[g * 512:(g + 1) * 512, :].rearrange("(t p) d -> p t d", p=128),
                        [], [f"xs{b}"], f"d_xs{b}")
                    for c in range(16):
                        bk = c % 4
                        for t in range(4):
                            tr(ps[bk][:, t * 128:(t + 1) * 128], xs[b][:, t, c * 128:(c + 1) * 128], ident_f[:],
                               [f"xs{b}", "ident_f"], [PS[bk]])
                        act(hT[:, c, g * 512:(g + 1) * 512], ps[bk][:, :], AF.Identity, [PS[bk], "s1p", "mod_fm"], ["hT"],
                            scale=s1p[:, l, c:c + 1], bias=mod_fm[:, l, c:c + 1])
                S.finish()

        def phase2_ret(hT):
            w_in = dr["ret_w_in"].rearrange("(c p) n -> p c n", p=128)
            yT = yT_d[0].rearrange("(f p) t -> p f t", p=128)
            with contextlib.ExitStack() as ph:
                wr = [sb(ph, f"wr{i}", [128, 16, 256], BF16) for i in range(3)]
                cosT = sb(ph, "cosT", [128, SEQ], F32)
                sinT = sb(ph, "sinT", [128, SEQ], F32)
                pos_i = sb(ph, "pos_i", [128, 512], I32)
                ta = sb(ph, "ta", [128, 512], F32)
                tb_ = sb(ph, "tb_", [128, 512], F32)
                ti = sb(ph, "ti", [128, 512], I32)
                qT = sb(ph, "qT", [128, 2, SEQ], BF16)
                kT = sb(ph, "kT", [128, 2, SEQ], BF16)
                v = sb(ph, "v", [128, 16, 512], BF16)
                gT = sb(ph, "gT", [128, 4, SEQ], BF16)
                r = [sb(ph, f"r{i}", [128, 512], F32) for i in range(4)]
                sT = [sb(ph, f"sT{i}", [128, 128], BF16) for i in range(2)]
                qd = [sb(ph, f"qd{i}", [128, 2, 128], BF16) for i in range(2)]
                kd = [sb(ph, f"kd{i}", [128, 256], BF16) for i in range(2)]
                st_f = sb(ph, "st_f", [128, 2, 512], F32)
                st_b = sb(ph, "st_b", [128, 2, 512], BF16)
                on_ = [sb(ph, f"on{i}", [128, 512], BF16) for i in range(2)]
                stats = sb(ph, "stats", [128, 6], F32)
                mv = sb(ph, "mv", [128, 2], F32)
                lnv = sb(ph, "lnv", [128, 1], F32)
                rstd = sb(ph, "rstd", [128, 1], F32)
                yst = [sb(ph, f"yst{i}", [128, 4, 512], BF16) for i in range(2)]
                mask = sb(ph, "mask", [128, 8, 128], F32)
                qdec = sb(ph, "qdec", [128, 8, 128], F32)
                kdec = sb(ph, "kdec", [128, 8], F32)
                inv = sb(ph, "inv", [128, 1], F32)
                dma("sp", mask[:], dr["ret_mask"][:, :, :], [], ["mask"], "d_c0")
                dma("sp", qdec[:], dr["ret_qdec"][:, :, :], [], ["qdec"], "d_c1")
                dma("sp", kdec[:], dr["ret_kdec"][:, :], [], ["kdec"], "d_c2")
                dma("sp", inv[:], dr["ret_inv"][:, :], [], ["inv"], "d_c3")

                pieces = []
                for h in range(8):
                    pieces += [("q", h, h * 256), ("k", h, 2048 + h * 256), ("v0", h, 4096 + h * 512),
                               ("v1", h, 4096 + h * 512 + 256), ("g0", h, 8192 + h * 512), ("g1", h, 8192 + h * 512 + 256)]
                state = {"next": 0}

                def prefetch(upto):
                    while state["next"] <= upto and state["next"] < len(pieces):
                        i = state["next"]
                        b = i % 3
                        col = pieces[i][2]
                        dma("pool", wr[b][:], w_in[:, :, col:col + 256], [], [f"wr{b}"], f"d_wr{b}")
                        state["next"] += 1

                prefetch(1)

                for pc in range(4):
                    sl = slice(pc * 512, (pc + 1) * 512)
                    dma("sp", pos_i[:], dr["posb"][:, sl], [], ["pos_i"], "d_pos")
                    cp("dve", ta[:], pos_i[:], ["pos_i"], ["ta"])
                    ts("dve", ta[:], ta[:], inv[:, 0:1], None, ALU.mult, None, ["ta", "inv"], ["ta"])
                    cp("dve", ti[:], ta[:], ["ta"], ["ti"])
                    cp("dve", tb_[:], ti[:], ["ti"], ["tb_"])
                    tt("dve", ta[:], ta[:], tb_[:], ALU.subtract, ["ta", "tb_"], ["ta"])
                    ts("dve", tb_[:], ta[:], 0.5, None, ALU.is_gt, None, ["ta"], ["tb_"])
                    tt("dve", ta[:], ta[:], tb_[:], ALU.subtract, ["ta", "tb_"], ["ta"])
                    act(sinT[:, sl], ta[:], AF.Sin, ["ta"], ["sinT"], scale=TWO_PI_LO)
                    ts("dve", tb_[:], ta[:], 0.25, None, ALU.add, None, ["ta"], ["tb_"])
                    ts("dve", ta[:], tb_[:], 0.5, None, ALU.is_gt, None, ["tb_", "sinT"], ["ta"])
                    tt("dve", tb_[:], tb_[:], ta[:], ALU.subtract, ["ta", "tb_"], ["tb_"])
                    act(cosT[:, sl], tb_[:], AF.Sin, ["tb_"], ["cosT"], scale=TWO_PI_LO)

                bankc = {"n": 0}

                def next_bank():
                    b = bankc["n"] % 3
                    bankc["n"] += 1
                    return b

                def proj_fm(wb, half, tbk, bk):
                    for kc in range(KC):
                        mm(ps[bk][:, :], wr[wb][:, kc, half * 128:(half + 1) * 128], hT[:, kc, tbk * 512:(tbk + 1) * 512],
                           kc == 0, kc == KC - 1, [f"wr{wb}", "hT"], [PS[bk]])

                pi = 0
                for h in range(8):
                    for dst, dname in ((qT, "qT"), (kT, "kT")):
                        wb = pi % 3
                        prefetch(pi + 2)
                        for tbk in range(4):
                            sl = slice(tbk * 512, (tbk + 1) * 512)
                            bA = next_bank()
                            proj_fm(wb, 0, tbk, bA)
                            bB = next_bank()
                            proj_fm(wb, 1, tbk, bB)
                            tt("dve", r[0][:], ps[bA][:, :], cosT[:, sl], ALU.mult, [PS[bA], "cosT"], ["r0"])
                            tt("dve", r[1][:], ps[bB][:, :], sinT[:, sl], ALU.mult, [PS[bB], "sinT"], ["r1"])
                            tt("dve", r[2][:], ps[bB][:, :], cosT[:, sl], ALU.mult, [PS[bB], "cosT"], ["r2"])
                            tt("dve", r[3][:], ps[bA][:, :], sinT[:, sl], ALU.mult, [PS[bA], "sinT"], ["r3"])
                            tt("pool", dst[:, 0, sl], r[0][:], r[1][:], ALU.subtract, ["r0", "r1"], [dname])
                            tt("pool", dst[:, 1, sl], r[2][:], r[3][:], ALU.add, ["r2", "r3"], [dname])
                        pi += 1
                    for vp in range(2):
                        wb = pi % 3
                        prefetch(pi + 2)
                        for t in range(NT):
                            bk = next_bank()
                            for kc in range(KC):
                                mm(ps[bk][:, 0:256], hT[:, kc, t * 128:(t + 1) * 128], wr[wb][:, kc, :],
                                   kc == 0, kc == KC - 1, [f"wr{wb}", "hT"], [PS[bk]])
                            act(v[:, t, vp * 256:(vp + 1) * 256], ps[bk][:, 0:256], AF.Identity, [PS[bk]], ["v"])
                        pi += 1
                    for gp in range(2):
                        wb = pi % 3
                        prefetch(pi + 2)
                        for half in range(2):
                            for tbk in range(4):
                                bk = next_bank()
                                proj_fm(wb, half, tbk, bk)
                                act(gT[:, gp * 2 + half, tbk * 512:(tbk + 1) * 512], ps[bk][:, :], AF.Silu, [PS[bk]], ["gT"])
                        pi += 1
                    for c in range(16):
                        cs = slice(c * 128, (c + 1) * 128)
                        bf = c % 2
                        if c < 15:
                            for half in range(2):
                                tr(psb[3][:, half * 128:(half + 1) * 128], kT[:, half, cs], ident_b[:],
                                   ["kT", "ident_b"], [PS[3]])
                            act(kd[bf][:], psb[3][:, 0:256], AF.Identity, [PS[3], "kdec"], [f"kd{bf}"],
                                scale=kdec[:, h:h + 1])
                        for half in range(2):
                            mm(ps[4][:, 0:128], kT[:, half, cs], qT[:, half, cs], half == 0, half == 1,
                               ["kT", "qT"], [PS[4]])
                        tt("dve", sT[bf][:], ps[4][:, 0:128], mask[:, h, :], ALU.mult, [PS[4], "mask"], [f"sT{bf}"])
                        if c > 0:
                            for half in range(2):
                                tt("pool", qd[bf][:, half, :], qT[:, half, cs], qdec[:, h, :], ALU.mult,
                                   ["qT", "qdec"], [f"qd{bf}"])
                        mm(ps[5][:, :], sT[bf][:], v[:, c, :], True, c == 0, [f"sT{bf}", "v"], [PS[5]])
                        if c > 0:
                            for half in range(2):
                                mm(ps[5][:, :], qd[bf][:, half, :], st_b[:, half, :], False, half == 1,
                                   [f"qd{bf}", "st_b"], [PS[5]])
                        if c < 15:
                            for half in range(2):
                                bk = 6 + half
                                mm(ps[bk][:, :], kd[bf][:, half * 128:(half + 1) * 128], v[:, c, :], True, True,
                                   [f"kd{bf}", "v"], [PS[bk]])
                                if c == 0:
                                    cp("dve", st_f[:, half, :], ps[bk][:, :], [PS[bk]], [f"st_f{half}"])
                                else:
                                    S.op("dve", lambda e, half=half, bk=bk, cd=cdec[h]: e.scalar_tensor_tensor(
                                        out=st_f[:, half, :], in0=st_f[:, half, :], scalar=cd, in1=ps[bk][:, :],
                                        op0=ALU.mult, op1=ALU.add), [PS[bk], f"st_f{half}"], [f"st_f{half}"])
                                act(st_b[:, half, :], st_f[:, half, :], AF.Identity, [f"st_f{half}"], ["st_b"])
                        S.op("dve", lambda e: e.bn_stats(out=stats[:], in_=ps[5][:, :]), [PS[5]], ["stats"])
                        S.op("dve", lambda e: e.bn_aggr(out=mv[:], in_=stats[:]), ["stats"], ["mv"])
                        act(lnv[:], mv[:, 1:2], AF.Ln, ["mv", "eps_t"], ["lnv"], bias=eps_t[:, 0:1])
                        act(rstd[:], lnv[:], AF.Exp, ["lnv"], ["rstd"], scale=-0.5)
                        ts("dve", on_[bf][:], ps[5][:, :], mv[:, 0:1], rstd[:, 0:1], ALU.subtract, ALU.mult,
                           [PS[5], "mv", "rstd"], [f"on{bf}"])
                        for fc in range(4):
                            tr(psb[0][:, fc * 128:(fc + 1) * 128], on_[bf][:, fc * 128:(fc + 1) * 128], ident_b[:],
                               [f"on{bf}", "ident_b"], [PS[0]])
                        yb = (c // 4) % 2
                        tt("dve", yst[yb][:, :, (c % 4) * 128:(c % 4 + 1) * 128],
                           psb[0][:, 0:512].rearrange("p (f t) -> p f t", t=128), gT[:, :, cs], ALU.mult,
                           [PS[0], "gT"], [f"yst{yb}"])
                        if c % 4 == 3:
                            tbk = c // 4
                            dma("sp", yT[:, h * 4:(h + 1) * 4, tbk * 512:(tbk + 1) * 512], yst[yb][:],
                                [f"yst{yb}"], ["yT_d"], f"d_yst{yb}")
                S.finish()

        def phase2_moba(hT):
            w_in = dr["moba_w_in"].rearrange("(c p) n -> p c n", p=128)
            yT = yT_d[1].rearrange("(f p) t -> p f t", p=128)
            scale = 128.0 ** -0.5
            with contextlib.ExitStack() as ph:
                wr = [sb(ph, f"wr{i}", [128, 16, 256], BF16) for i in range(3)]
                qtm = [sb(ph, f"qtm{i}", [128, 256], BF16) for i in range(2)]
                qT = sb(ph, "qT", [128, 2, SEQ], BF16)
                kT = sb(ph, "kT", [128, 2, SEQ], BF16)
                vext = sb(ph, "vext", [128, 16, 2, 130], BF16)
                gT = sb(ph, "gT", [128, 2, SEQ], BF16)
                PT = [sb(ph, f"PT{i}", [128, 16, 512], BF16) for i in range(2)]
                pos_i = sb(ph, "pos_i", [128, 16], I32)
                posf = sb(ph, "posf", [128, 16], F32)
                minv = sb(ph, "minv", [128, 16], F32)
                ta = sb(ph, "ta", [128, 16, 16], F32)
                tb_ = sb(ph, "tb_", [128, 16, 16], F32)
                ti = sb(ph, "ti", [128, 16, 16], I32)
                cos2 = sb(ph, "cos2", [128, 16, 2, 16], F32)
                sin2 = sb(ph, "sin2", [128, 16, 2, 16], F32)
                r = [sb(ph, f"r{i}", [128, 2, 16], F32) for i in range(4)]
                kmf = sb(ph, "kmf", [128, 2, 8], F32)
                kmb = sb(ph, "kmb", [128, 2, 8], BF16)
                gm = sb(ph, "gm", [128, 8, 8], F32)
                top8 = sb(ph, "top8", [128, 8, 8], F32)
                negmask = sb(ph, "negmask", [128, 8, 8], F32)
                selb = sb(ph, "selb", [128, 16, 32], BF16)
                selbT = sb(ph, "selbT", [128, SEQ], BF16)
                blockind = sb(ph, "blockind", [128, 8, 128], BF16)
                tri = sb(ph, "tri", [128, 128], BF16)
                o_n = [sb(ph, f"o_n{i}", [128, 4, 128], BF16) for i in range(2)]
                rden = sb(ph, "rden", [128, 1], F32)
                yst = [sb(ph, f"yst{i}", [128, 512], BF16) for i in range(2)]
                dma("sp", negmask[:], dr["negmask"][:, :, :], [], ["negmask"], "d_c0")
                dma("sp", selb[:], dr["selb_init"][:, :, :], [], ["selb"], "d_c1")
                dma("sp", blockind[:], dr["blockind"][:, :, :], [], ["blockind"], "d_c2")
                dma("sp", tri[:], dr["tri"][:, :], [], ["tri"], "d_c3")
                dma("sp", minv[:], dr["moba_inv"][:, :], [], ["minv"], "d_c4")
                dma("sp", pos_i[:], dr["post"][:, :], [], ["pos_i"], "d_c5")
                S.op("dve", lambda e: e.memset(selbT[:, 1024:2048], 0.0), [], ["selbT_d"])
                dma("sp", selbT[:, 0:1024], dr["selbT_static"][:, :], [], ["selbT_s"], "d_c6")
                S.op("dve", lambda e: e.memset(vext[:, :, :, 128:130], 1.0), [], ["vext1"])

                pieces = []
                for g in range(8):
                    pieces += [("q", g, g * 256), ("k", g, 2048 + g * 256), ("v", g, 4096 + g * 256), ("g", g, 6144 + g * 256)]
                state = {"next": 0}

                def prefetch(upto):
                    while state["next"] <= upto and state["next"] < len(pieces):
                        i = state["next"]
                        b = i % 3
                        col = pieces[i][2]
                        dma("pool", wr[b][:], w_in[:, :, col:col + 256], [], [f"wr{b}"], f"d_wr{b}")
                        state["next"] += 1

                prefetch(1)

                cp("dve", posf[:], pos_i[:], ["pos_i"], ["posf"])
                for t in range(16):
                    ts("dve", ta[:, t, :], minv[:], posf[:, t:t + 1], None, ALU.mult, None, ["minv", "posf"], ["ta"])
                cp("dve", ti[:], ta[:], ["ta"], ["ti"])
                cp("dve", tb_[:], ti[:], ["ti"], ["tb_"])
                tt("dve", ta[:], ta[:], tb_[:], ALU.subtract, ["ta", "tb_"], ["ta"])
                ts("dve", tb_[:], ta[:], 0.5, None, ALU.is_gt, None, ["ta"], ["tb_"])
                tt("dve", ta[:], ta[:], tb_[:], ALU.subtract, ["ta", "tb_"], ["ta"])
                for hd in range(2):
                    act(sin2[:, :, hd, :], ta[:], AF.Sin, ["ta"], ["sin2"], scale=TWO_PI_LO)
                ts("dve", tb_[:], ta[:], 0.25, None, ALU.add, None, ["ta"], ["tb_"])
                ts("dve", ta[:], tb_[:], 0.5, None, ALU.is_gt, None, ["tb_", "sin2"], ["ta"])
                tt("dve", tb_[:], tb_[:], ta[:], ALU.subtract, ["ta", "tb_"], ["tb_"])
                for hd in range(2):
                    act(cos2[:, :, hd, :], tb_[:], AF.Sin, ["tb_"], ["cos2"], scale=TWO_PI_LO)

                bankc = {"n": 0, "att": 0, "po": 0}

                def next_bank():
                    b = bankc["n"] % 2
                    bankc["n"] += 1
                    return b

                pi = 0
                for g in range(8):
                    for dst, dname in ((qT, "qT"), (kT, "kT")):
                        wb = pi % 3
                        prefetch(pi + 2)
                        for t in range(NT):
                            bk = next_bank()
                            qb_ = t % 2
                            for kc in range(KC):
                                mm(ps[bk][:, 0:256], hT[:, kc, t * 128:(t + 1) * 128], wr[wb][:, kc, :],
                                   kc == 0, kc == KC - 1, [f"wr{wb}", "hT"], [PS[bk]])
                            psv = ps[bk][:, 0:256].rearrange("p (h d) -> p h d", d=128)
                            qv = qtm[qb_][:].rearrange("p (h d) -> p h d", d=128)
                            act(qv[:, :, 32:128], psv[:, :, 32:128], AF.Identity, [PS[bk]], [f"qtm{qb_}"])
                            tt("dve", r[0][:], psv[:, :, 0:16], cos2[:, t, :, :], ALU.mult, [PS[bk], "cos2"], ["r0"])
                            tt("dve", r[1][:], psv[:, :, 16:32], sin2[:, t, :, :], ALU.mult, [PS[bk], "sin2"], ["r1"])
                            tt("dve", r[2][:], psv[:, :, 16:32], cos2[:, t, :, :], ALU.mult, [PS[bk], "cos2"], ["r2"])
                            tt("dve", r[3][:], psv[:, :, 0:16], sin2[:, t, :, :], ALU.mult, [PS[bk], "sin2"], ["r3"])
                            tt("pool", qv[:, :, 0:16], r[0][:], r[1][:], ALU.subtract, ["r0", "r1"], [f"qtm{qb_}"])
                            tt("pool", qv[:, :, 16:32], r[2][:], r[3][:], ALU.add, ["r2", "r3"], [f"qtm{qb_}"])
                            for hd in range(2):
                                tr(psb[3][:, hd * 128:(hd + 1) * 128], qtm[qb_][:, hd * 128:(hd + 1) * 128], ident_b[:],
                                   [f"qtm{qb_}", "ident_b"], [PS[3]])
                            act(dst[:, :, t * 128:(t + 1) * 128], psb[3][:, 0:256].rearrange("p (h d) -> p h d", d=128),
                                AF.Identity, [PS[3]], [dname])
                        pi += 1
                    wb = pi % 3
                    prefetch(pi + 2)
                    for t in range(NT):
                        bk = next_bank()
                        for kc in range(KC):
                            mm(ps[bk][:, 0:256], hT[:, kc, t * 128:(t + 1) * 128], wr[wb][:, kc, :],
                               kc == 0, kc == KC - 1, [f"wr{wb}", "hT"], [PS[bk]])
                        act(vext[:, t, :, 0:128], ps[bk][:, 0:256].rearrange("p (h d) -> p h d", d=128), AF.Identity,
                            [PS[bk]], ["vext"])
                    pi += 1
                    wb = pi % 3
                    prefetch(pi + 2)
                    for hd in range(2):
                        for tbk in range(4):
                            bk = next_bank()
                            for kc in range(KC):
                                mm(ps[bk][:, :], wr[wb][:, kc, hd * 128:(hd + 1) * 128], hT[:, kc, tbk * 512:(tbk + 1) * 512],
                                   kc == 0, kc == KC - 1, [f"wr{wb}", "hT"], [PS[bk]])
                            act(gT[:, hd, tbk * 512:(tbk + 1) * 512], ps[bk][:, :], AF.Silu, [PS[bk]], ["gT"])
                    pi += 1
                    S.op("dve", lambda e: e.tensor_reduce(out=kmf[:], in_=kT[:].rearrange("p h (n k) -> p h n k", k=256),
                                                          axis=AX.X, op=ALU.add), ["kT"], ["kmf"])
                    ts("dve", kmb[:], kmf[:], 1.0 / 256.0, None, ALU.mult, None, ["kmf"], ["kmb"])
                    for hd in range(2):
                        for i8 in range(8):
                            i = 8 + i8
                            mm(ps[4][:, i8 * 8:(i8 + 1) * 8], qT[:, hd, i * 128:(i + 1) * 128], kmb[:, hd, :], True, True,
                               ["qT", "kmb"], [PS[4]])
                        tt("dve", gm[:].rearrange("p a b -> p (a b)"), ps[4][:, 0:64], negmask[:].rearrange("p a b -> p (a b)"),
                           ALU.add, [PS[4], "negmask"], ["gm"])
                        for i8 in range(8):
                            S.op("dve", lambda e, i8=i8: e.max(out=top8[:, i8, :], in_=gm[:, i8, :]), ["gm"], ["top8"])
                        for i8 in range(8):
                            i = 8 + i8
                            B = i // 2
                            ts("dve", selb[:, i, 0:B], gm[:, i8, 0:B], top8[:, i8, 2:3], -BIG, ALU.is_lt, ALU.mult,
                               ["gm", "top8"], ["selb"])
                        for i8 in range(8):
                            tr(psb[3][0:32, i8 * 128:(i8 + 1) * 128], selb[:, 8 + i8, :], ident_b[:], ["selb", "ident_b"], [PS[3]])
                        cp("dve", selbT[0:32, 1024:2048], psb[3][0:32, 0:1024], [PS[3]], ["selbT_d"])
                        for qb in range(4):
                            pb = bankc["att"] % 2
                            bankc["att"] += 1
                            qs = slice(qb * 512, (qb + 1) * 512)
                            nkt = 4 * (qb + 1)
                            for j in range(nkt):
                                bk = 5 + (j % 2)
                                mm(ps[bk][:, :], kT[:, hd, j * 128:(j + 1) * 128], qT[:, hd, qs], True, False,
                                   ["kT", "qT"], [PS[bk]])
                                mm(ps[bk][:, :], blockind[:, j // 2, :], selbT[:, qs], False, True,
                                   ["blockind", "selbT_s", "selbT_d"], [PS[bk]])
                                act(PT[pb][:, j, :], ps[bk][:, :], AF.Exp, [PS[bk]], [f"PT{pb}"], scale=scale)
                                if j >= 4 * qb:
                                    il = j - 4 * qb
                                    tt("pool", PT[pb][:, j, il * 128:(il + 1) * 128], PT[pb][:, j, il * 128:(il + 1) * 128],
                                       tri[:], ALU.mult, [f"PT{pb}", "tri"], [f"PT{pb}"])
                            ob = qb % 2
                            for il in range(4):
                                i = 4 * qb + il
                                pk = 7 if (bankc["po"] % 2 == 0) else 0
                                bankc["po"] += 1
                                for j in range(i + 1):
                                    mm(ps[pk][:, 0:129], PT[pb][:, j, il * 128:(il + 1) * 128], vext[:, j, hd, 0:129],
                                       j == 0, j == i, [f"PT{pb}", "vext", "vext1"], [PS[pk]])
                                S.op("dve", lambda e, pk=pk: e.reciprocal(out=rden[:], in_=ps[pk][:, 128:129]), [PS[pk]], ["rden"])
                                ts("dve", o_n[ob][:, il, :], ps[pk][:, 0:128], rden[:, 0:1], None, ALU.mult, None,
                                   [PS[pk], "rden"], [f"o_n{ob}"])
                                tr(psb[2][:, il * 128:(il + 1) * 128], o_n[ob][:, il, :], ident_b[:], [f"o_n{ob}", "ident_b"], [PS[2]])
                            tt("dve", yst[ob][:], psb[2][:, 0:512], gT[:, hd, qs], ALU.mult, [PS[2], "gT"], [f"yst{ob}"])
                            hg = g * 2 + hd
                            dma("sp", yT[:, hg, qs], yst[ob][:], [f"yst{ob}"], ["yT_d"], f"d_yst{ob}")
                S.finish()

        def phase3(l, xsrc, dst, w_out, kco):
            wv = w_out.rearrange("(c p) n -> p c n", p=128)
            yT = yT_d[l].rearrange("(f p) t -> p f t", p=128)
            with contextlib.ExitStack() as ph:
                yTb = sb(ph, "yTb", [128, kco, 512], BF16)
                wo = [sb(ph, f"wo{i}", [128, kco, 512], BF16) for i in range(2)]
                z = sb(ph, "z", [128, 4, D], F32)
                xt = [sb(ph, f"xt{i}", [128, D], F32) for i in range(2)]
                zo = [sb(ph, f"zo{i}", [128, D], F32) for i in range(2)]
                grow = sb(ph, "grow", [128, D], F32)
                lng_t = sb(ph, "lng_t", [128, D], F32)
                lnb_t = sb(ph, "lnb_t", [128, D], F32)
                stats = sb(ph, "stats", [128, 4, 6], F32)
                mv = sb(ph, "mv", [128, 2], F32)
                lnv = sb(ph, "lnv", [128, 1], F32)
                rstd = sb(ph, "rstd", [128, 1], F32)
                nmr = sb(ph, "nmr", [128, 1], F32)
                dma("sp", grow[:], gate_d[l, :, :], [], ["grow"], "d_c0")
                dma("sp", lng_t[:], dr["lng"][:, l, :], [], ["lng_t"], "d_c1")
                dma("sp", lnb_t[:], dr["lnb"][:, l, :], [], ["lnb_t"], "d_c2")
                wcnt = 0
                xcnt = 0
                bcnt = 0
                for tbk in range(4):
                    dma("sp", yTb[:], yT[:, :, tbk * 512:(tbk + 1) * 512], [], ["yTb"], "d_yTb")
                    for j in range(4):
                        wb = wcnt % 2
                        wcnt += 1
                        dma("pool", wo[wb][:], wv[:, :, j * 512:(j + 1) * 512], [], [f"wo{wb}"], f"d_wo{wb}")
                        for t in range(4):
                            bk = bcnt % 4
                            bcnt += 1
                            for kc in range(kco):
                                mm(ps[bk][:, :], yTb[:, kc, t * 128:(t + 1) * 128], wo[wb][:, kc, :], kc == 0, kc == kco - 1,
                                   ["yTb", f"wo{wb}"], [PS[bk]])
                            tt("dve", z[:, t, j * 512:(j + 1) * 512], ps[bk][:, :], grow[:, j * 512:(j + 1) * 512], ALU.mult,
                               [PS[bk], "grow"], [f"z{t}"])
                    for t in range(4):
                        xb = xcnt % 2
                        xcnt += 1
                        row = (tbk * 4 + t) * 128
                        dma("sp", xt[xb][:], xsrc[row:row + 128, :], [], [f"xt{xb}"], f"d_xt{xb}")
                        S.op("dve", lambda e, t=t, xb=xb: e.scalar_tensor_tensor(
                            out=z[:, t, :], in0=xt[xb][:], scalar=ALPHA, in1=z[:, t, :], op0=ALU.mult, op1=ALU.add),
                            [f"xt{xb}", f"z{t}"], [f"z{t}"])
                        for q4 in range(4):
                            S.op("dve", lambda e, t=t, q4=q4: e.bn_stats(out=stats[:, q4, :], in_=z[:, t, q4 * 512:(q4 + 1) * 512]),
                                 [f"z{t}"], ["stats"])
                        S.op("dve", lambda e: e.bn_aggr(out=mv[:], in_=stats[:].rearrange("p a b -> p (a b)")), ["stats"], ["mv"])
                        act(lnv[:], mv[:, 1:2], AF.Ln, ["mv", "eps_t"], ["lnv"], bias=eps_t[:, 0:1])
                        act(rstd[:], lnv[:], AF.Exp, ["lnv"], ["rstd"], scale=-0.5)
                        ts("dve", nmr[:], mv[:, 0:1], rstd[:, 0:1], -1.0, ALU.mult, ALU.mult, ["mv", "rstd"], ["nmr"])
                        act(zo[xb][:], z[:, t, :], AF.Identity, [f"z{t}", "rstd", "nmr"], [f"zo{xb}"],
                            scale=rstd[:, 0:1], bias=nmr[:, 0:1])
                        tt("pool", zo[xb][:], zo[xb][:], lng_t[:], ALU.mult, [f"zo{xb}", "lng_t"], [f"zo{xb}"])
                        tt("pool", zo[xb][:], zo[xb][:], lnb_t[:], ALU.add, [f"zo{xb}", "lnb_t"], [f"zo{xb}"])
                        dma("sp", dst[row:row + 128, :], zo[xb][:], [f"zo{xb}"], ["dst_d"], f"d_zo{xb}")
                S.finish()

        phase0()
        for l in layers:
            xsrc = dr["x"] if l == layers[0] else x1_d
            dst = out_d if l == layers[-1] else x1_d
            with contextlib.ExitStack() as lay:
                hT = sb(lay, "hT", [128, KC, SEQ], BF16)
                phase1(l, xsrc, hT)
                if l == 0:
                    phase2_ret(hT)
                else:
                    phase2_moba(hT)
            if l == 0:
                phase3(0, xsrc, dst, dr["ret_w_out"], 32)
            else:
                phase3(1, xsrc, dst, dr["moba_w_out"], 16)
        clear_block()
    return nc


_CACHE = {}


def make_in_maps(inputs, consts, xs_override=None):
    x = np.ascontiguousarray(inputs["x"], dtype=np.float32)
    c = np.asarray(inputs["c"], dtype=np.float32)
    pos = np.asarray(inputs["positions"], dtype=np.int32)
    b_ada = np.asarray(inputs["b_ada"], dtype=np.float32)
    b_fm = np.ascontiguousarray(b_ada.reshape(2, 48, 128).transpose(2, 0, 1))
    b_gate = np.ascontiguousarray(np.broadcast_to(b_ada[None, :, 4096:], (128, 2, 2048)))
    lng = np.ascontiguousarray(np.broadcast_to(np.asarray(inputs["ln_g"], np.float32)[None], (128, 2, 2048)))
    lnb = np.ascontiguousarray(np.broadcast_to(np.asarray(inputs["ln_b"], np.float32)[None], (128, 2, 2048)))
    shared = dict(w_ada=np.ascontiguousarray(inputs["w_ada"], dtype=np.float32), b_fm=b_fm, b_gate=b_gate, lng=lng, lnb=lnb,
                  ret_w_in=np.ascontiguousarray(inputs["ret_w_in"][0], dtype=np.float32),
                  ret_w_out=np.ascontiguousarray(inputs["ret_w_out"][0], dtype=np.float32),
                  moba_w_in=np.ascontiguousarray(inputs["moba_w_in"][0], dtype=np.float32),
                  moba_w_out=np.ascontiguousarray(inputs["moba_w_out"][0], dtype=np.float32))
    shared.update(consts)
    maps = []
    for b in range(x.shape[0]):
        m = dict(shared)
        m["x"] = x[b] if xs_override is None else np.ascontiguousarray(xs_override[b], dtype=np.float32)
        m["cT"] = np.ascontiguousarray(c[b].reshape(16, 128).T)
        m["posb"] = np.ascontiguousarray(np.broadcast_to(pos[b][None, :], (128, SEQ)))
        m["post"] = np.ascontiguousarray(pos[b].reshape(16, 128).T)
        maps.append(m)
    return maps


def run_layers(inputs, layers, xs_override=None, trace=False):
    consts, cdec = host_consts()
    key = tuple(layers)
    if key not in _CACHE:
        _CACHE[key] = build(layers=layers, cdec=cdec)
    nc = _CACHE[key]
    maps = make_in_maps(inputs, consts, xs_override)
    n = len(maps)
    res = run_bass_kernel_spmd(nc, maps, core_ids=list(range(n)), trace=trace)
    out = np.stack([np.asarray(r["out"], dtype=np.float32) for r in res.results], axis=0)
    return out, res


def kernel(**inputs):
    out, _ = run_layers(inputs, (0, 1))
    return out
```

```python
import contextlib
import math
import numpy as np
import ml_dtypes
import concourse.bass as bass
import concourse.mybir as mybir
from concourse.bass_utils import run_bass_kernel_spmd

F32 = mybir.dt.float32
BF16 = mybir.dt.bfloat16
I32 = mybir.dt.int32
AF = mybir.ActivationFunctionType
ALU = mybir.AluOpType
AX = mybir.AxisListType

D = 2048
SEQ = 2048
NT = 16
KC = 16
ALPHA = float((2.0 * 2) ** 0.25)
LN_EPS = 1e-5
GN_EPS = 1e-5
BIG = 30000.0
TWO_PI_LO = 6.28318
COMPUTE = ("pe", "dve", "act", "pool")


class Sched:
    def __init__(self, nc, stack):
        self.nc = nc
        self.stack = stack
        self.sems = {}
        self.cum = {}
        self.known = {e: {} for e in ("pe", "dve", "act", "pool", "sp")}
        self.begin()
        for e in COMPUTE:
            self._sem("eng_" + e)

    def begin(self):
        self.q = {e: [] for e in ("pe", "dve", "act", "pool", "sp")}
        self.res = {}

    def _sem(self, key):
        if key not in self.sems:
            assert not getattr(self, "frozen", False), "semaphore %s not pre-declared" % key
            self.sems[key] = self.stack.enter_context(self.nc.semaphore(key))
            self.cum[key] = 0
        return self.sems[key]

    def _need(self, eng, tok, needs, kind):
        if tok is None:
            return
        key, val, teng = tok
        if teng == eng:
            if eng == "pe" or kind != "raw":
                return
        if self.known[eng].get(key, 0) >= val:
            return
        if needs.get(key, 0) < val:
            needs[key] = val

    def op(self, eng, fn, reads=(), writes=(), dma_sem=None):
        needs = {}
        for r in reads:
            st = self.res.get(r)
            if st is None:
                continue
            self._need(eng, st[0], needs, "raw")
            if r.startswith("ps"):
                for t in st[1].values():
                    self._need(eng, t, needs, "war")
        for w in writes:
            st = self.res.get(w)
            if st is None:
                continue
            self._need(eng, st[0], needs, "waw")
            for t in st[1].values():
                self._need(eng, t, needs, "war")
        for k, v in needs.items():
            self.known[eng][k] = v
        if dma_sem is not None:
            self._sem(dma_sem)
            self.cum[dma_sem] += 16
            tok = (dma_sem, self.cum[dma_sem], "dma")
            inc = (dma_sem, 16)
        else:
            key = "eng_" + eng
            self.cum[key] += 1
            tok = (key, self.cum[key], eng)
            inc = (key, 1)
        self.q[eng].append((list(needs.items()), fn, inc))
        for r in reads:
            st = self.res.setdefault(r, [None, {}])
            st[1][tok[0]] = tok
        for w in writes:
            self.res[w] = [tok, {}]
        return tok

    def finish(self):
        nc = self.nc
        waits = [(k, v) for k, v in self.cum.items() if not k.startswith("eng_") and v > 0]
        self.q["sp"].append((waits, None, None))
        handles = {"pe": "tensor", "dve": "vector", "act": "scalar", "pool": "gpsimd", "sp": "sync"}
        with nc.Block() as block:
            for e, hname in handles.items():
                queue = self.q[e]
                if not queue:
                    continue

                def body(engine, queue=queue):
                    for wl, fn, inc in queue:
                        for k, v in wl:
                            engine.wait_ge(self.sems[k], v)
                        if fn is not None:
                            fn(engine).then_inc(self.sems[inc[0]], inc[1])

                getattr(block, hname)(body)
        self.begin()


def host_consts():
    c = {}
    c["ident_f"] = np.eye(128, dtype=np.float32)
    c["ident_b"] = np.eye(128, dtype=np.float32).astype(ml_dtypes.bfloat16)
    lg = np.log(np.float32(1.0) - np.float32(2.0) ** (-5.0 - np.arange(8, dtype=np.float32))).astype(np.float32)
    n = np.arange(128, dtype=np.float32)
    diff = n[:, None] - n[None, :]
    inner = np.where(diff >= 0, np.exp(np.maximum(diff, 0.0)[None] * lg[:, None, None]), 0.0).astype(np.float32)
    c["ret_mask"] = np.ascontiguousarray(inner.transpose(2, 0, 1) / 16.0).astype(np.float32)
    qdec = np.exp((n[None, :] + 1.0) * lg[:, None]).astype(np.float32)
    c["ret_qdec"] = np.ascontiguousarray(np.broadcast_to(qdec[None], (128, 8, 128))).astype(np.float32)
    kdec = np.exp((127.0 - n[None, :]) * lg[:, None]).astype(np.float32)
    c["ret_kdec"] = np.ascontiguousarray(kdec.T / 16.0).astype(np.float32)
    cdec = [float(np.exp(np.float32(128.0) * lg[h])) for h in range(8)]
    i128 = np.arange(128, dtype=np.float64)
    c["ret_inv"] = ((10000.0 ** (-(i128 * 2.0 / 256.0))).astype(np.float32).astype(np.float64) / (2 * math.pi)).astype(np.float32).reshape(128, 1)
    i16 = np.arange(16, dtype=np.float64)
    minv = ((500000.0 ** (-(i16 * 2.0 / 32.0))).astype(np.float32).astype(np.float64) / (2 * math.pi)).astype(np.float32)
    c["moba_inv"] = np.ascontiguousarray(np.broadcast_to(minv[None], (128, 16))).astype(np.float32)
    m = np.arange(128)
    c["tri"] = (m[:, None] <= m[None, :]).astype(np.float32).astype(ml_dtypes.bfloat16)
    bi = np.zeros((128, 8, 128), np.float32)
    for b in range(8):
        bi[b, b, :] = 1.0
    c["blockind"] = bi.astype(ml_dtypes.bfloat16)
    negm = np.zeros((128, 8, 8), np.float32)
    for i8 in range(8):
        B = (8 + i8) // 2
        negm[:, i8, B:] = -1e30
    c["negmask"] = negm
    sbi = np.zeros((128, 16, 32), np.float32)
    for i in range(16):
        B = i // 2
        sbi[:, i, B + 1:8] = -BIG
    c["selb_init"] = sbi.astype(ml_dtypes.bfloat16)
    sst = np.zeros((128, 1024), np.float32)
    for t in range(8):
        B = t // 2
        sst[B + 1:8, t * 128:(t + 1) * 128] = -BIG
    c["selbT_static"] = sst.astype(ml_dtypes.bfloat16)
    return c, cdec


CONST_SPECS = [("ident_f", [128, 128], F32), ("ident_b", [128, 128], BF16), ("ret_mask", [128, 8, 128], F32),
               ("ret_qdec", [128, 8, 128], F32), ("ret_kdec", [128, 8], F32), ("ret_inv", [128, 1], F32),
               ("moba_inv", [128, 16], F32), ("tri", [128, 128], BF16), ("blockind", [128, 8, 128], BF16),
               ("negmask", [128, 8, 8], F32), ("selb_init", [128, 16, 32], BF16), ("selbT_static", [128, 1024], BF16)]


def build(layers=(0, 1), cdec=None):
    nc = bass.Bass("TRN2", target_bir_lowering=False)
    dr = {}

    def din(name, shape, dt):
        dr[name] = nc.dram_tensor(name, shape, dt, kind="ExternalInput").ap()

    din("x", [SEQ, D], F32)
    din("cT", [128, 16], F32)
    din("posb", [128, SEQ], I32)
    din("post", [128, 16], I32)
    din("w_ada", [2, D, 3 * D], F32)
    din("b_fm", [128, 2, 48], F32)
    din("b_gate", [128, 2, D], F32)
    din("lng", [128, 2, D], F32)
    din("lnb", [128, 2, D], F32)
    din("ret_w_in", [D, 12288], F32)
    din("ret_w_out", [4096, D], F32)
    din("moba_w_in", [D, 8192], F32)
    din("moba_w_out", [D, D], F32)
    for name, shape, dt in CONST_SPECS:
        din(name, shape, dt)
    out_d = nc.dram_tensor("out", [SEQ, D], F32, kind="ExternalOutput").ap()
    yT_d = [nc.dram_tensor("yT0", [4096, SEQ], BF16, kind="Internal").ap(),
            nc.dram_tensor("yT1", [2048, SEQ], BF16, kind="Internal").ap()]
    x1_d = nc.dram_tensor("x1s", [SEQ, D], F32, kind="Internal").ap()
    gate_d = nc.dram_tensor("gates", [2, 128, D], F32, kind="Internal").ap()

    with contextlib.ExitStack() as outer:
        S = Sched(nc, outer)

        uniq = {"n": 0}

        def sb(stack, name, shape, dt):
            uniq["n"] += 1
            return stack.enter_context(nc.sbuf_tensor("s%d_%s" % (uniq["n"], name), shape, dt))

        ps = [outer.enter_context(nc.psum_tensor(f"psb{i}", [128, 512], F32)) for i in range(8)]
        psb = [p[:].bitcast(BF16) for p in ps]
        PS = [f"ps{i}" for i in range(8)]

        ident_f = sb(outer, "ident_f", [128, 128], F32)
        ident_b = sb(outer, "ident_b", [128, 128], BF16)
        mod_fm = sb(outer, "mod_fm", [128, 2, 48], F32)
        s1p = sb(outer, "s1p", [128, 2, 16], F32)
        eps_t = sb(outer, "eps_t", [128, 1], F32)

        def mm(out, lhsT, rhs, start, stop, reads, writes, skip=False):
            S.op("pe", lambda e: e.matmul(out, lhsT=lhsT, rhs=rhs, start=start, stop=stop, skip_group_check=skip),
                 reads, writes)

        def tr(out, in_, ident, reads, writes):
            S.op("pe", lambda e: e.transpose(out=out, in_=in_, identity=ident), reads, writes)

        def act(out, in_, func, reads, writes, scale=1.0, bias=0.0):
            S.op("act", lambda e: e.activation(out=out, in_=in_, func=func, bias=bias, scale=scale), reads, writes)

        def ts(eng, out, in0, s1, s2, op0, op1, reads, writes):
            if op1 is None:
                S.op(eng, lambda e: e.tensor_scalar(out=out, in0=in0, scalar1=s1, scalar2=None, op0=op0), reads, writes)
            else:
                S.op(eng, lambda e: e.tensor_scalar(out=out, in0=in0, scalar1=s1, scalar2=s2, op0=op0, op1=op1),
                     reads, writes)

        def tt(eng, out, in0, in1, op, reads, writes):
            S.op(eng, lambda e: e.tensor_tensor(out=out, in0=in0, in1=in1, op=op), reads, writes)

        def cp(eng, out, in_, reads, writes):
            S.op(eng, lambda e: e.tensor_copy(out=out, in_=in_), reads, writes)

        def dma(q, out, in_, reads, writes, sem, **kw):
            S.op(q, lambda e: e.dma_start(out=out, in_=in_, **kw), reads, writes, dma_sem=sem)

        dma_sem_names = (["d_c%d" % i for i in range(8)] + ["d_wa0", "d_wa1", "d_wr0", "d_wr1", "d_wr2", "d_xs0", "d_xs1",
                         "d_yst0", "d_yst1", "d_yTb", "d_zo0_0", "d_zo0_1", "d_zo0_2", "d_zo0_3", "d_zo1_0", "d_zo1_1", "d_zo1_2", "d_zo1_3", "d_wo0", "d_wo1", "d_xt0", "d_xt1", "d_zo0", "d_zo1", "d_g", "d_pos"])
        for nme in dma_sem_names:
            S._sem(nme)
        allsems = list(S.sems.values())
        S.frozen = True

        def clear_block():
            with nc.Block() as blk:
                @blk.gpsimd
                def _(g):
                    for s_ in allsems:
                        g.sem_clear(s_)

        clear_block()

        def phase0():
            with contextlib.ExitStack() as ph:
                wa = [sb(ph, f"wa{i}", [128, 3 * D], BF16) for i in range(2)]
                c_f = sb(ph, "c_f", [128, 16], F32)
                c_b = sb(ph, "c_b", [128, 16], BF16)
                ones_b = sb(ph, "ones_b", [128, 128], BF16)
                c_rep = sb(ph, "c_rep", [128, 16, 128], BF16)
                bfm = sb(ph, "bfm", [128, 2, 48], F32)
                bgate = sb(ph, "bgate", [128, 2, D], F32)
                grow = sb(ph, "grow0", [128, D], F32)
                dma("sp", ident_f[:], dr["ident_f"][:, :], [], ["ident_f"], "d_c0")
                dma("sp", ident_b[:], dr["ident_b"][:, :], [], ["ident_b"], "d_c1")
                dma("sp", c_f[:], dr["cT"][:, :], [], ["c_f"], "d_c2")
                dma("sp", bfm[:], dr["b_fm"][:, :, :], [], ["bfm"], "d_c3")
                dma("sp", bgate[:], dr["b_gate"][:, :, :], [], ["bgate"], "d_c4")
                S.op("dve", lambda e: e.memset(eps_t[:], LN_EPS), [], ["eps_t"])
                S.op("dve", lambda e: e.memset(ones_b[:], 1.0), [], ["ones_b"])
                cp("dve", c_b[:], c_f[:], ["c_f"], ["c_b"])
                for ch in range(16):
                    ts("dve", c_rep[:, ch, :], ones_b[:], c_f[:, ch:ch + 1], None, ALU.mult, None,
                       ["ones_b", "c_f"], ["c_rep"])
                cnt = 0
                for l in range(2):
                    for ch in range(16):
                        b = cnt % 2
                        cnt += 1
                        dma("pool", wa[b][:], dr["w_ada"][l, ch * 128:(ch + 1) * 128, :], [], [f"wa{b}"], f"d_wa{b}",
                            max_dma_last_dim=8192)
                        for jt in range(48):
                            mm(ps[0][:, jt:jt + 1], wa[b][:, jt * 128:(jt + 1) * 128], c_b[:, ch:ch + 1],
                               (ch == 0 and jt == 0), (ch == 15), [f"wa{b}", "c_b"], [PS[0]], skip=True)
                        for n_ in range(4):
                            mm(ps[1 + n_][:, :], c_rep[:, ch, :], wa[b][:, 4096 + n_ * 512:4096 + (n_ + 1) * 512],
                               ch == 0, ch == 15, [f"wa{b}", "c_rep"], [PS[1 + n_]])
                    tt("dve", mod_fm[:, l, :], ps[0][:, 0:48], bfm[:, l, :], ALU.add, [PS[0], "bfm"], ["mod_fm"])
                    ts("dve", s1p[:, l, :], mod_fm[:, l, 16:32], 1.0, None, ALU.add, None, ["mod_fm"], ["s1p"])
                    for n_ in range(4):
                        tt("dve", grow[:, n_ * 512:(n_ + 1) * 512], ps[1 + n_][:, :], bgate[:, l, n_ * 512:(n_ + 1) * 512],
                           ALU.add, [PS[1 + n_], "bgate"], ["grow"])
                    dma("sp", gate_d[l, :, :], grow[:], ["grow"], ["gate_d"], "d_g")
                S.finish()

        def phase1(l, xsrc, hT):
            with contextlib.ExitStack() as ph:
                xs = [sb(ph, f"xs{i}", [128, 4, D], F32) for i in range(2)]
                for g in range(4):
                    b = g % 2
                    dma("sp", xs[b][:], xsrc[g * 512:(g + 1) * 512, :].rearrange("(t p) d -> p t d", p=128),
                        [], [f"xs{b}"], f"d_xs{b}")
                    for c in range(16):
                        bk = c % 4
                        for t in range(4):
                            tr(ps[bk][:, t * 128:(t + 1) * 128], xs[b][:, t, c * 128:(c + 1) * 128], ident_f[:],
                               [f"xs{b}", "ident_f"], [PS[bk]])
                        act(hT[:, c, g * 512:(g + 1) * 512], ps[bk][:, :], AF.Identity, [PS[bk], "s1p", "mod_fm"], ["hT"],
                            scale=s1p[:, l, c:c + 1], bias=mod_fm[:, l, c:c + 1])
                S.finish()

        def phase2_ret(hT):
            w_in = dr["ret_w_in"].rearrange("(c p) n -> p c n", p=128)
            yT = yT_d[0].rearrange("(f p) t -> p f t", p=128)
            with contextlib.ExitStack() as ph:
                wr = [sb(ph, f"wr{i}", [128, 16, 256], BF16) for i in range(3)]
                cosT = sb(ph, "cosT", [128, SEQ], F32)
                sinT = sb(ph, "sinT", [128, SEQ], F32)
                pos_i = sb(ph, "pos_i", [128, 512], I32)
                ta = sb(ph, "ta", [128, 512], F32)
                tb_ = sb(ph, "tb_", [128, 512], F32)
                ti = sb(ph, "ti", [128, 512], I32)
                qT = sb(ph, "qT", [128, 2, SEQ], BF16)
                kT = sb(ph, "kT", [128, 2, SEQ], BF16)
                v = sb(ph, "v", [128, 16, 512], BF16)
                gT = sb(ph, "gT", [128, 4, SEQ], BF16)
                r = [sb(ph, f"r{i}", [128, 512], F32) for i in range(4)]
                sT = [sb(ph, f"sT{i}", [128, 128], BF16) for i in range(2)]
                qd = [sb(ph, f"qd{i}", [128, 2, 128], BF16) for i in range(2)]
                kd = [sb(ph, f"kd{i}", [128, 256], BF16) for i in range(2)]
                st_f = sb(ph, "st_f", [128, 2, 512], F32)
                st_b = sb(ph, "st_b", [128, 2, 512], BF16)
                on_ = [sb(ph, f"on{i}", [128, 512], BF16) for i in range(2)]
                stats = sb(ph, "stats", [128, 6], F32)
                mv = sb(ph, "mv", [128, 2], F32)
                lnv = sb(ph, "lnv", [128, 1], F32)
                rstd = sb(ph, "rstd", [128, 1], F32)
                yst = [sb(ph, f"yst{i}", [128, 4, 512], BF16) for i in range(2)]
                mask = sb(ph, "mask", [128, 8, 128], F32)
                qdec = sb(ph, "qdec", [128, 8, 128], F32)
                kdec = sb(ph, "kdec", [128, 8], F32)
                inv = sb(ph, "inv", [128, 1], F32)
                dma("sp", mask[:], dr["ret_mask"][:, :, :], [], ["mask"], "d_c0")
                dma("sp", qdec[:], dr["ret_qdec"][:, :, :], [], ["qdec"], "d_c1")
                dma("sp", kdec[:], dr["ret_kdec"][:, :], [], ["kdec"], "d_c2")
                dma("sp", inv[:], dr["ret_inv"][:, :], [], ["inv"], "d_c3")

                pieces = []
                for h in range(8):
                    pieces += [("q", h, h * 256), ("k", h, 2048 + h * 256), ("v0", h, 4096 + h * 512),
                               ("v1", h, 4096 + h * 512 + 256), ("g0", h, 8192 + h * 512), ("g1", h, 8192 + h * 512 + 256)]
                state = {"next": 0}

                def prefetch(upto):
                    while state["next"] <= upto and state["next"] < len(pieces):
                        i = state["next"]
                        b = i % 3
                        col = pieces[i][2]
                        dma("pool", wr[b][:], w_in[:, :, col:col + 256], [], [f"wr{b}"], f"d_wr{b}")
                        state["next"] += 1

                prefetch(1)

                for pc in range(4):
                    sl = slice(pc * 512, (pc + 1) * 512)
                    dma("sp", pos_i[:], dr["posb"][:, sl], [], ["pos_i"], "d_pos")
                    cp("dve", ta[:], pos_i[:], ["pos_i"], ["ta"])
                    ts("dve", ta[:], ta[:], inv[:, 0:1], None, ALU.mult, None, ["ta", "inv"], ["ta"])
                    cp("dve", ti[:], ta[:], ["ta"], ["ti"])
                    cp("dve", tb_[:], ti[:], ["ti"], ["tb_"])
                    tt("dve", ta[:], ta[:], tb_[:], ALU.subtract, ["ta", "tb_"], ["ta"])
                    ts("dve", tb_[:], ta[:], 0.5, None, ALU.is_gt, None, ["ta"], ["tb_"])
                    tt("dve", ta[:], ta[:], tb_[:], ALU.subtract, ["ta", "tb_"], ["ta"])
                    act(sinT[:, sl], ta[:], AF.Sin, ["ta"], ["sinT"], scale=TWO_PI_LO)
                    ts("dve", tb_[:], ta[:], 0.25, None, ALU.add, None, ["ta"], ["tb_"])
                    ts("dve", ta[:], tb_[:], 0.5, None, ALU.is_gt, None, ["tb_", "sinT"], ["ta"])
                    tt("dve", tb_[:], tb_[:], ta[:], ALU.subtract, ["ta", "tb_"], ["tb_"])
                    act(cosT[:, sl], tb_[:], AF.Sin, ["tb_"], ["cosT"], scale=TWO_PI_LO)

                bankc = {"n": 0}
                pend = []

                def flush(keep):
                    while len(pend) > keep:
                        pend.pop(0)()

                def next_bank():
                    b = bankc["n"] % 3
                    bankc["n"] += 1
                    return b

                def proj_fm(wb, half, tbk, bk):
                    for kc in range(KC):
                        mm(ps[bk][:, :], wr[wb][:, kc, half * 128:(half + 1) * 128], hT[:, kc, tbk * 512:(tbk + 1) * 512],
                           kc == 0, kc == KC - 1, [f"wr{wb}", "hT"], [PS[bk]])

                pi = 0
                for h in range(8):
                    for dst, dname in ((qT, "qT"), (kT, "kT")):
                        wb = pi % 3
                        prefetch(pi + 2)
                        for tbk in range(4):
                            sl = slice(tbk * 512, (tbk + 1) * 512)
                            bA = next_bank()
                            proj_fm(wb, 0, tbk, bA)
                            bB = next_bank()
                            proj_fm(wb, 1, tbk, bB)
                            tt("dve", r[0][:], ps[bA][:, :], cosT[:, sl], ALU.mult, [PS[bA], "cosT"], ["r0"])
                            tt("dve", r[1][:], ps[bB][:, :], sinT[:, sl], ALU.mult, [PS[bB], "sinT"], ["r1"])
                            tt("dve", r[2][:], ps[bB][:, :], cosT[:, sl], ALU.mult, [PS[bB], "cosT"], ["r2"])
                            tt("dve", r[3][:], ps[bA][:, :], sinT[:, sl], ALU.mult, [PS[bA], "sinT"], ["r3"])
                            tt("pool", dst[:, 0, sl], r[0][:], r[1][:], ALU.subtract, ["r0", "r1"], [dname])
                            tt("pool", dst[:, 1, sl], r[2][:], r[3][:], ALU.add, ["r2", "r3"], [dname])
                        pi += 1
                    for vp in range(2):
                        wb = pi % 3
                        prefetch(pi + 2)
                        for t in range(NT):
                            bk = next_bank()
                            for kc in range(KC):
                                mm(ps[bk][:, 0:256], hT[:, kc, t * 128:(t + 1) * 128], wr[wb][:, kc, :],
                                   kc == 0, kc == KC - 1, [f"wr{wb}", "hT"], [PS[bk]])
                            act(v[:, t, vp * 256:(vp + 1) * 256], ps[bk][:, 0:256], AF.Identity, [PS[bk]], ["v"])
                        pi += 1
                    wb_g = [pi % 3, (pi + 1) % 3]
                    prefetch(pi + 2)
                    pi += 2
                    ggroups = [(tbk, gp, half) for tbk in range(4) for gp in range(2) for half in range(2)]
                    gstate = {"n": 0}

                    def emit_g(idx):
                        tbk, gp, half = ggroups[idx]
                        bk = 1 + (gstate["n"] % 2)
                        gstate["n"] += 1
                        wbg = wb_g[gp]
                        for kc in range(KC):
                            mm(ps[bk][:, :], wr[wbg][:, kc, half * 128:(half + 1) * 128], hT[:, kc, tbk * 512:(tbk + 1) * 512],
                               kc == 0, kc == KC - 1, [f"wr{wbg}", "hT"], [PS[bk]])
                        act(gT[:, gp * 2 + half, tbk * 512:(tbk + 1) * 512], ps[bk][:, :], AF.Silu, [PS[bk]], ["gT"])

                    for idx in range(4):
                        emit_g(idx)
                    for c in range(16):
                        cs = slice(c * 128, (c + 1) * 128)
                        bf = c % 2
                        if c < 15:
                            for half in range(2):
                                tr(psb[3][:, half * 128:(half + 1) * 128], kT[:, half, cs], ident_b[:],
                                   ["kT", "ident_b"], [PS[3]])
                            act(kd[bf][:], psb[3][:, 0:256], AF.Identity, [PS[3], "kdec"], [f"kd{bf}"],
                                scale=kdec[:, h:h + 1])
                        for half in range(2):
                            mm(ps[4][:, 0:128], kT[:, half, cs], qT[:, half, cs], half == 0, half == 1,
                               ["kT", "qT"], [PS[4]])
                        tt("dve", sT[bf][:], ps[4][:, 0:128], mask[:, h, :], ALU.mult, [PS[4], "mask"], [f"sT{bf}"])
                        if c > 0:
                            for half in range(2):
                                tt("pool", qd[bf][:, half, :], qT[:, half, cs], qdec[:, h, :], ALU.mult,
                                   ["qT", "qdec"], [f"qd{bf}"])
                        if c < 12:
                            emit_g(4 + c)
                        mm(ps[5][:, :], sT[bf][:], v[:, c, :], True, c == 0, [f"sT{bf}", "v"], [PS[5]])
                        if c > 0:
                            for half in range(2):
                                mm(ps[5][:, :], qd[bf][:, half, :], st_b[:, half, :], False, half == 1,
                                   [f"qd{bf}", "st_b"], [PS[5]])
                        if c < 15:
                            for half in range(2):
                                bk = 6 + half
                                mm(ps[bk][:, :], kd[bf][:, half * 128:(half + 1) * 128], v[:, c, :], True, True,
                                   [f"kd{bf}", "v"], [PS[bk]])
                                if c == 0:
                                    cp("dve", st_f[:, half, :], ps[bk][:, :], [PS[bk]], [f"st_f{half}"])
                                else:
                                    S.op("dve", lambda e, half=half, bk=bk, cd=cdec[h]: e.scalar_tensor_tensor(
                                        out=st_f[:, half, :], in0=st_f[:, half, :], scalar=cd, in1=ps[bk][:, :],
                                        op0=ALU.mult, op1=ALU.add), [PS[bk], f"st_f{half}"], [f"st_f{half}"])
                                act(st_b[:, half, :], st_f[:, half, :], AF.Identity, [f"st_f{half}"], ["st_b"])
                        S.op("dve", lambda e: e.bn_stats(out=stats[:], in_=ps[5][:, :]), [PS[5]], ["stats"])
                        S.op("dve", lambda e: e.bn_aggr(out=mv[:], in_=stats[:]), ["stats"], ["mv"])
                        act(lnv[:], mv[:, 1:2], AF.Ln, ["mv", "eps_t"], ["lnv"], bias=eps_t[:, 0:1])
                        act(rstd[:], lnv[:], AF.Exp, ["lnv"], ["rstd"], scale=-0.5)
                        ts("dve", on_[bf][:], ps[5][:, :], mv[:, 0:1], rstd[:, 0:1], ALU.subtract, ALU.mult,
                           [PS[5], "mv", "rstd"], [f"on{bf}"])

                        def fin(c=c, cs=cs, bf=bf, h=h):
                            for fc in range(4):
                                tr(psb[0][:, fc * 128:(fc + 1) * 128], on_[bf][:, fc * 128:(fc + 1) * 128], ident_b[:],
                                   [f"on{bf}", "ident_b"], [PS[0]])
                            yb = (c // 4) % 2
                            tt("dve", yst[yb][:, :, (c % 4) * 128:(c % 4 + 1) * 128],
                               psb[0][:, 0:512].rearrange("p (f t) -> p f t", t=128), gT[:, :, cs], ALU.mult,
                               [PS[0], "gT"], [f"yst{yb}"])
                            if c % 4 == 3:
                                tbk = c // 4
                                dma("sp", yT[:, h * 4:(h + 1) * 4, tbk * 512:(tbk + 1) * 512], yst[yb][:],
                                    [f"yst{yb}"], ["yT_d"], f"d_yst{yb}")

                        pend.append(fin)
                        flush(1)
                    flush(0)
                S.finish()

        def phase2_moba(hT):
            w_in = dr["moba_w_in"].rearrange("(c p) n -> p c n", p=128)
            yT = yT_d[1].rearrange("(f p) t -> p f t", p=128)
            scale = 128.0 ** -0.5
            with contextlib.ExitStack() as ph:
                wr = [sb(ph, f"wr{i}", [128, 16, 256], BF16) for i in range(3)]
                qtm = [sb(ph, f"qtm{i}", [128, 256], BF16) for i in range(3)]
                qT = sb(ph, "qT", [128, 2, SEQ], BF16)
                kT = sb(ph, "kT", [128, 2, SEQ], BF16)
                vext = sb(ph, "vext", [128, 16, 2, 130], BF16)
                gT = sb(ph, "gT", [128, 2, SEQ], BF16)
                PT = [sb(ph, f"PT{i}", [128, 16, 512], BF16) for i in range(2)]
                pos_i = sb(ph, "pos_i", [128, 16], I32)
                posf = sb(ph, "posf", [128, 16], F32)
                minv = sb(ph, "minv", [128, 16], F32)
                ta = sb(ph, "ta", [128, 16, 16], F32)
                tb_ = sb(ph, "tb_", [128, 16, 16], F32)
                ti = sb(ph, "ti", [128, 16, 16], I32)
                cos2 = sb(ph, "cos2", [128, 16, 2, 16], F32)
                sin2 = sb(ph, "sin2", [128, 16, 2, 16], F32)
                r = [[sb(ph, f"r{s_}_{i}", [128, 2, 16], F32) for i in range(4)] for s_ in range(2)]
                kmf = sb(ph, "kmf", [128, 2, 8], F32)
                kmb = sb(ph, "kmb", [128, 2, 8], BF16)
                gm = [sb(ph, f"gm{i}", [128, 8, 8], F32) for i in range(2)]
                top8 = [sb(ph, f"top8{i}", [128, 8, 8], F32) for i in range(2)]
                negmask = sb(ph, "negmask", [128, 8, 8], F32)
                selb = [sb(ph, f"selb{i}", [128, 16, 32], BF16) for i in range(2)]
                selbT = [sb(ph, f"selbT{i}", [128, SEQ], BF16) for i in range(2)]
                blockind = sb(ph, "blockind", [128, 8, 128], BF16)
                tri = sb(ph, "tri", [128, 128], BF16)
                o_n = [sb(ph, f"o_n{i}", [128, 4, 128], BF16) for i in range(2)]
                rden = [sb(ph, f"rden{i}", [128, 1], F32) for i in range(2)]
                yst = [sb(ph, f"yst{i}", [128, 512], BF16) for i in range(2)]
                dma("sp", negmask[:], dr["negmask"][:, :, :], [], ["negmask"], "d_c0")
                dma("sp", blockind[:], dr["blockind"][:, :, :], [], ["blockind"], "d_c2")
                dma("sp", tri[:], dr["tri"][:, :], [], ["tri"], "d_c3")
                dma("sp", minv[:], dr["moba_inv"][:, :], [], ["minv"], "d_c4")
                dma("sp", pos_i[:], dr["post"][:, :], [], ["pos_i"], "d_c5")
                for hd in range(2):
                    dma("sp", selb[hd][:], dr["selb_init"][:, :, :], [], [f"selb{hd}"], "d_c1" if hd == 0 else "d_c7")
                    S.op("dve", lambda e, hd=hd: e.memset(selbT[hd][:, 1024:2048], 0.0), [], [f"selbT_d{hd}"])
                    dma("sp", selbT[hd][:, 0:1024], dr["selbT_static"][:, :], [], [f"selbT_s{hd}"], "d_c6" if hd == 0 else "d_pos")
                S.op("dve", lambda e: e.memset(vext[:, :, :, 128:130], 1.0), [], ["vext1"])

                pieces = []
                for g in range(8):
                    pieces += [("q", g, g * 256), ("k", g, 2048 + g * 256), ("v", g, 4096 + g * 256), ("g", g, 6144 + g * 256)]
                state = {"next": 0}

                def prefetch(upto):
                    while state["next"] <= upto and state["next"] < len(pieces):
                        i = state["next"]
                        b = i % 3
                        col = pieces[i][2]
                        dma("pool", wr[b][:], w_in[:, :, col:col + 256], [], [f"wr{b}"], f"d_wr{b}")
                        state["next"] += 1

                prefetch(1)

                cp("dve", posf[:], pos_i[:], ["pos_i"], ["posf"])
                for t in range(16):
                    ts("dve", ta[:, t, :], minv[:], posf[:, t:t + 1], None, ALU.mult, None, ["minv", "posf"], ["ta"])
                cp("dve", ti[:], ta[:], ["ta"], ["ti"])
                cp("dve", tb_[:], ti[:], ["ti"], ["tb_"])
                tt("dve", ta[:], ta[:], tb_[:], ALU.subtract, ["ta", "tb_"], ["ta"])
                ts("dve", tb_[:], ta[:], 0.5, None, ALU.is_gt, None, ["ta"], ["tb_"])
                tt("dve", ta[:], ta[:], tb_[:], ALU.subtract, ["ta", "tb_"], ["ta"])
                for hd in range(2):
                    act(sin2[:, :, hd, :], ta[:], AF.Sin, ["ta"], ["sin2"], scale=TWO_PI_LO)
                ts("dve", tb_[:], ta[:], 0.25, None, ALU.add, None, ["ta"], ["tb_"])
                ts("dve", ta[:], tb_[:], 0.5, None, ALU.is_gt, None, ["tb_", "sin2"], ["ta"])
                tt("dve", tb_[:], tb_[:], ta[:], ALU.subtract, ["ta", "tb_"], ["tb_"])
                for hd in range(2):
                    act(cos2[:, :, hd, :], tb_[:], AF.Sin, ["tb_"], ["cos2"], scale=TWO_PI_LO)

                bankc = {"n": 0, "att": 0, "po": 0, "t": 0}
                pend = []

                def flush(keep):
                    while len(pend) > keep:
                        pend.pop(0)()

                def next_bank():
                    b = bankc["n"] % 2
                    bankc["n"] += 1
                    return b

                pi = 0
                for g in range(8):
                    for dst, dname in ((qT, "qT"), (kT, "kT")):
                        wb = pi % 3
                        prefetch(pi + 2)
                        for t in range(NT):
                            bk = next_bank()
                            qb_ = bankc["t"] % 3
                            rs = r[bankc["t"] % 2]
                            rn = ["r%d_%d" % (bankc["t"] % 2, i_) for i_ in range(4)]
                            bankc["t"] += 1
                            for kc in range(KC):
                                mm(ps[bk][:, 0:256], hT[:, kc, t * 128:(t + 1) * 128], wr[wb][:, kc, :],
                                   kc == 0, kc == KC - 1, [f"wr{wb}", "hT"], [PS[bk]])
                            psv = ps[bk][:, 0:256].rearrange("p (h d) -> p h d", d=128)
                            qv = qtm[qb_][:].rearrange("p (h d) -> p h d", d=128)
                            act(qv[:, :, 32:128], psv[:, :, 32:128], AF.Identity, [PS[bk]], [f"qtm{qb_}"])
                            tt("dve", rs[0][:], psv[:, :, 0:16], cos2[:, t, :, :], ALU.mult, [PS[bk], "cos2"], [rn[0]])
                            tt("dve", rs[1][:], psv[:, :, 16:32], sin2[:, t, :, :], ALU.mult, [PS[bk], "sin2"], [rn[1]])
                            tt("dve", rs[2][:], psv[:, :, 16:32], cos2[:, t, :, :], ALU.mult, [PS[bk], "cos2"], [rn[2]])
                            tt("dve", rs[3][:], psv[:, :, 0:16], sin2[:, t, :, :], ALU.mult, [PS[bk], "sin2"], [rn[3]])
                            tt("pool", qv[:, :, 0:16], rs[0][:], rs[1][:], ALU.subtract, [rn[0], rn[1]], [f"qtm{qb_}"])
                            tt("pool", qv[:, :, 16:32], rs[2][:], rs[3][:], ALU.add, [rn[2], rn[3]], [f"qtm{qb_}"])

                            def fin(t=t, qb_=qb_, dst=dst, dname=dname):
                                for hd in range(2):
                                    tr(psb[3][:, hd * 128:(hd + 1) * 128], qtm[qb_][:, hd * 128:(hd + 1) * 128], ident_b[:],
                                       [f"qtm{qb_}", "ident_b"], [PS[3]])
                                act(dst[:, :, t * 128:(t + 1) * 128], psb[3][:, 0:256].rearrange("p (h d) -> p h d", d=128),
                                    AF.Identity, [PS[3]], [dname])

                            pend.append(fin)
                            flush(2)
                        pi += 1
                    wb = pi % 3
                    prefetch(pi + 2)
                    for t in range(NT):
                        bk = next_bank()
                        for kc in range(KC):
                            mm(ps[bk][:, 0:256], hT[:, kc, t * 128:(t + 1) * 128], wr[wb][:, kc, :],
                               kc == 0, kc == KC - 1, [f"wr{wb}", "hT"], [PS[bk]])
                        act(vext[:, t, :, 0:128], ps[bk][:, 0:256].rearrange("p (h d) -> p h d", d=128), AF.Identity,
                            [PS[bk]], ["vext"])
                        if t == 1:
                            flush(0)
                        if t == 3:
                            S.op("dve", lambda e: e.tensor_reduce(out=kmf[:], in_=kT[:].rearrange("p h (n k) -> p h n k", k=256),
                                                                  axis=AX.X, op=ALU.add), ["kT"], ["kmf"])
                            ts("dve", kmb[:], kmf[:], 1.0 / 256.0, None, ALU.mult, None, ["kmf"], ["kmb"])
                            for hd in range(2):
                                for i8 in range(8):
                                    i = 8 + i8
                                    mm(ps[4][:, hd * 64 + i8 * 8:hd * 64 + (i8 + 1) * 8], qT[:, hd, i * 128:(i + 1) * 128],
                                       kmb[:, hd, :], True, True, ["qT", "kmb"], [PS[4]])
                        if t == 5:
                            for hd in range(2):
                                tt("dve", gm[hd][:].rearrange("p a b -> p (a b)"), ps[4][:, hd * 64:(hd + 1) * 64],
                                   negmask[:].rearrange("p a b -> p (a b)"), ALU.add, [PS[4], "negmask"], [f"gm{hd}"])
                                for i8 in range(8):
                                    S.op("dve", lambda e, i8=i8, hd=hd: e.max(out=top8[hd][:, i8, :], in_=gm[hd][:, i8, :]),
                                         [f"gm{hd}"], [f"top8{hd}"])
                                for i8 in range(8):
                                    i = 8 + i8
                                    B = i // 2
                                    ts("dve", selb[hd][:, i, 0:B], gm[hd][:, i8, 0:B], top8[hd][:, i8, 2:3], -BIG, ALU.is_lt, ALU.mult,
                                       [f"gm{hd}", f"top8{hd}"], [f"selb{hd}"])
                        if t == 9 or t == 11:
                            hd = 0 if t == 9 else 1
                            for i8 in range(8):
                                tr(psb[3][0:32, i8 * 128:(i8 + 1) * 128], selb[hd][:, 8 + i8, :], ident_b[:],
                                   [f"selb{hd}", "ident_b"], [PS[3]])
                            cp("dve", selbT[hd][0:32, 1024:2048], psb[3][0:32, 0:1024], [PS[3]], [f"selbT_d{hd}"])
                    pi += 1
                    wb = pi % 3
                    prefetch(pi + 2)
                    for hd in range(2):
                        for tbk in range(4):
                            bk = next_bank()
                            for kc in range(KC):
                                mm(ps[bk][:, :], wr[wb][:, kc, hd * 128:(hd + 1) * 128], hT[:, kc, tbk * 512:(tbk + 1) * 512],
                                   kc == 0, kc == KC - 1, [f"wr{wb}", "hT"], [PS[bk]])
                            act(gT[:, hd, tbk * 512:(tbk + 1) * 512], ps[bk][:, :], AF.Silu, [PS[bk]], ["gT"])
                    pi += 1
                    for hd in range(2):
                        for qb in range(4):
                            pb = bankc["att"] % 2
                            bankc["att"] += 1
                            qs = slice(qb * 512, (qb + 1) * 512)
                            nkt = 4 * (qb + 1)
                            for j in range(nkt):
                                bk = 5 + (j % 2)
                                mm(ps[bk][:, :], kT[:, hd, j * 128:(j + 1) * 128], qT[:, hd, qs], True, False,
                                   ["kT", "qT"], [PS[bk]])
                                mm(ps[bk][:, :], blockind[:, j // 2, :], selbT[hd][:, qs], False, True,
                                   ["blockind", f"selbT_s{hd}", f"selbT_d{hd}"], [PS[bk]])
                                act(PT[pb][:, j, :], ps[bk][:, :], AF.Exp, [PS[bk]], [f"PT{pb}"], scale=scale)
                                if j >= 4 * qb:
                                    il = j - 4 * qb
                                    tt("pool", PT[pb][:, j, il * 128:(il + 1) * 128], PT[pb][:, j, il * 128:(il + 1) * 128],
                                       tri[:], ALU.mult, [f"PT{pb}", "tri"], [f"PT{pb}"])
                                if j == 1:
                                    flush(0)
                            ob = bankc["att"] % 2
                            for il in range(4):
                                i = 4 * qb + il
                                pk = 7 if (bankc["po"] % 2 == 0) else 0
                                rb = bankc["po"] % 2
                                bankc["po"] += 1
                                for j in range(i + 1):
                                    mm(ps[pk][:, 0:129], PT[pb][:, j, il * 128:(il + 1) * 128], vext[:, j, hd, 0:129],
                                       j == 0, j == i, [f"PT{pb}", "vext", "vext1"], [PS[pk]])
                                S.op("dve", lambda e, pk=pk, rb=rb: e.reciprocal(out=rden[rb][:], in_=ps[pk][:, 128:129]),
                                     [PS[pk]], [f"rden{rb}"])
                                ts("dve", o_n[ob][:, il, :], ps[pk][:, 0:128], rden[rb][:, 0:1], None, ALU.mult, None,
                                   [PS[pk], f"rden{rb}"], [f"o_n{ob}"])

                                def fin_tr(il=il, ob=ob):
                                    tr(psb[2][:, il * 128:(il + 1) * 128], o_n[ob][:, il, :], ident_b[:],
                                       [f"o_n{ob}", "ident_b"], [PS[2]])

                                pend.append(fin_tr)
                                flush(1)

                            def fin_y(ob=ob, hd=hd, qs=qs, hg=g * 2 + hd):
                                tt("dve", yst[ob][:], psb[2][:, 0:512], gT[:, hd, qs], ALU.mult, [PS[2], "gT"], [f"yst{ob}"])
                                dma("sp", yT[:, hg, qs], yst[ob][:], [f"yst{ob}"], ["yT_d"], f"d_yst{ob}")

                            pend.append(fin_y)
                    flush(0)
                S.finish()

        def phase3(l, xsrc, dst, w_out, kco):
            wv = w_out.rearrange("(c p) n -> p c n", p=128)
            yT = yT_d[l].rearrange("(f p) t -> p f t", p=128)
            with contextlib.ExitStack() as ph:
                yTb = [sb(ph, f"yTb{i}", [128, kco, 512], BF16) for i in range(2)]
                wo = [sb(ph, f"wo{i}", [128, kco, 256], BF16) for i in range(2)]
                z = [sb(ph, f"z{i}", [128, 4, D], F32) for i in range(2)]
                xt = [sb(ph, f"xt{i}", [128, D], F32) for i in range(2)]
                grow = sb(ph, "grow", [128, D], F32)
                lng_t = sb(ph, "lng_t", [128, D], F32)
                lnb_t = sb(ph, "lnb_t", [128, D], F32)
                stats = sb(ph, "stats", [128, 4, 6], F32)
                mv = sb(ph, "mv", [128, 2], F32)
                lnv = sb(ph, "lnv", [128, 1], F32)
                rstd = sb(ph, "rstd", [128, 1], F32)
                nmr = sb(ph, "nmr", [128, 1], F32)
                dma("sp", grow[:], gate_d[l, :, :], [], ["grow"], "d_c0")
                dma("sp", lng_t[:], dr["lng"][:, l, :], [], ["lng_t"], "d_c1")
                dma("sp", lnb_t[:], dr["lnb"][:, l, :], [], ["lnb_t"], "d_c2")
                cnt = {"w": 0, "x": 0, "b": 0}
                wlist = [(tbk, j) for tbk in range(4) for j in range(8)]

                def wload(idx):
                    if idx < len(wlist):
                        j = wlist[idx][1]
                        wb = idx % 2
                        dma("pool", wo[wb][:], wv[:, :, j * 256:(j + 1) * 256], [], [f"wo{wb}"], f"d_wo{wb}")

                def yload(tbk):
                    if tbk < 4:
                        dma("sp", yTb[tbk % 2][:], yT[:, :, tbk * 512:(tbk + 1) * 512], [], [f"yTb{tbk % 2}"],
                            "d_yTb" if tbk % 2 == 0 else "d_g")

                def ln_tile(tbk, t):
                    zb = tbk % 2
                    zn = f"z{zb}_{t}"
                    xb = cnt["x"] % 2
                    cnt["x"] += 1
                    row = (tbk * 4 + t) * 128
                    dma("sp", xt[xb][:], xsrc[row:row + 128, :], [], [f"xt{xb}"], f"d_xt{xb}")
                    S.op("dve", lambda e: e.scalar_tensor_tensor(
                        out=z[zb][:, t, :], in0=xt[xb][:], scalar=ALPHA, in1=z[zb][:, t, :], op0=ALU.mult, op1=ALU.add),
                        [f"xt{xb}", zn], [zn])
                    for q4 in range(4):
                        S.op("dve", lambda e, q4=q4: e.bn_stats(out=stats[:, q4, :], in_=z[zb][:, t, q4 * 512:(q4 + 1) * 512]),
                             [zn], ["stats"])
                    S.op("dve", lambda e: e.bn_aggr(out=mv[:], in_=stats[:].rearrange("p a b -> p (a b)")), ["stats"], ["mv"])
                    act(lnv[:], mv[:, 1:2], AF.Ln, ["mv", "eps_t"], ["lnv"], bias=eps_t[:, 0:1])
                    act(rstd[:], lnv[:], AF.Exp, ["lnv"], ["rstd"], scale=-0.5)
                    ts("dve", nmr[:], mv[:, 0:1], rstd[:, 0:1], -1.0, ALU.mult, ALU.mult, ["mv", "rstd"], ["nmr"])
                    act(z[zb][:, t, :], z[zb][:, t, :], AF.Identity, [zn, "rstd", "nmr"], [zn],
                        scale=rstd[:, 0:1], bias=nmr[:, 0:1])
                    tt("pool", z[zb][:, t, :], z[zb][:, t, :], lng_t[:], ALU.mult, [zn, "lng_t"], [zn])
                    tt("pool", z[zb][:, t, :], z[zb][:, t, :], lnb_t[:], ALU.add, [zn, "lnb_t"], [zn])
                    dma("sp", dst[row:row + 128, :], z[zb][:, t, :], [zn], ["dst_d"], f"d_zo{zb}_{t}")

                yload(0)
                wload(0)
                for tbk in range(4):
                    yload(tbk + 1)
                    zb = tbk % 2
                    for j in range(8):
                        idx = tbk * 8 + j
                        wb = idx % 2
                        wload(idx + 1)
                        for t in range(4):
                            bk = cnt["b"] % 4
                            cnt["b"] += 1
                            for kc in range(kco):
                                mm(ps[bk][:, 0:256], yTb[tbk % 2][:, kc, t * 128:(t + 1) * 128], wo[wb][:, kc, :], kc == 0, kc == kco - 1,
                                   [f"yTb{tbk % 2}", f"wo{wb}"], [PS[bk]])
                            tt("dve", z[zb][:, t, j * 256:(j + 1) * 256], ps[bk][:, 0:256], grow[:, j * 256:(j + 1) * 256], ALU.mult,
                               [PS[bk], "grow"], [f"z{zb}_{t}"])
                        if tbk > 0 and j % 2 == 1:
                            ln_tile(tbk - 1, j // 2)
                for t in range(4):
                    ln_tile(3, t)
                S.finish()

        phase0()
        for l in layers:
            xsrc = dr["x"] if l == layers[0] else x1_d
            dst = out_d if l == layers[-1] else x1_d
            with contextlib.ExitStack() as lay:
                hT = sb(lay, "hT", [128, KC, SEQ], BF16)
                phase1(l, xsrc, hT)
                if l == 0:
                    phase2_ret(hT)
                else:
                    phase2_moba(hT)
            if l == 0:
                phase3(0, xsrc, dst, dr["ret_w_out"], 32)
            else:
                phase3(1, xsrc, dst, dr["moba_w_out"], 16)
        clear_block()
    return nc


_CACHE = {}


def make_in_maps(inputs, consts, xs_override=None):
    x = np.ascontiguousarray(inputs["x"], dtype=np.float32)
    c = np.asarray(inputs["c"], dtype=np.float32)
    pos = np.asarray(inputs["positions"], dtype=np.int32)
    b_ada = np.asarray(inputs["b_ada"], dtype=np.float32)
    b_fm = np.ascontiguousarray(b_ada.reshape(2, 48, 128).transpose(2, 0, 1))
    b_gate = np.ascontiguousarray(np.broadcast_to(b_ada[None, :, 4096:], (128, 2, 2048)))
    lng = np.ascontiguousarray(np.broadcast_to(np.asarray(inputs["ln_g"], np.float32)[None], (128, 2, 2048)))
    lnb = np.ascontiguousarray(np.broadcast_to(np.asarray(inputs["ln_b"], np.float32)[None], (128, 2, 2048)))
    shared = dict(w_ada=np.ascontiguousarray(inputs["w_ada"], dtype=np.float32), b_fm=b_fm, b_gate=b_gate, lng=lng, lnb=lnb,
                  ret_w_in=np.ascontiguousarray(inputs["ret_w_in"][0], dtype=np.float32),
                  ret_w_out=np.ascontiguousarray(inputs["ret_w_out"][0], dtype=np.float32),
                  moba_w_in=np.ascontiguousarray(inputs["moba_w_in"][0], dtype=np.float32),
                  moba_w_out=np.ascontiguousarray(inputs["moba_w_out"][0], dtype=np.float32))
    shared.update(consts)
    maps = []
    for b in range(x.shape[0]):
        m = dict(shared)
        m["x"] = x[b] if xs_override is None else np.ascontiguousarray(xs_override[b], dtype=np.float32)
        m["cT"] = np.ascontiguousarray(c[b].reshape(16, 128).T)
        m["posb"] = np.ascontiguousarray(np.broadcast_to(pos[b][None, :], (128, SEQ)))
        m["post"] = np.ascontiguousarray(pos[b].reshape(16, 128).T)
        maps.append(m)
    return maps


def run_layers(inputs, layers, xs_override=None, trace=False):
    consts, cdec = host_consts()
    key = tuple(layers)
    if key not in _CACHE:
        _CACHE[key] = build(layers=layers, cdec=cdec)
    nc = _CACHE[key]
    maps = make_in_maps(inputs, consts, xs_override)
    n = len(maps)
    res = run_bass_kernel_spmd(nc, maps, core_ids=list(range(n)), trace=trace)
    out = np.stack([np.asarray(r["out"], dtype=np.float32) for r in res.results], axis=0)
    return out, res


def kernel(**inputs):
    out, _ = run_layers(inputs, (0, 1))
    return out
```

```python
import contextlib
import math
import numpy as np
import ml_dtypes
import concourse.bass as bass
import concourse.mybir as mybir
from concourse.bass_utils import run_bass_kernel_spmd

F32 = mybir.dt.float32
BF16 = mybir.dt.bfloat16
I32 = mybir.dt.int32
AF = mybir.ActivationFunctionType
ALU = mybir.AluOpType
AX = mybir.AxisListType

D = 2048
SEQ = 2048
NT = 16
KC = 16
ALPHA = float((2.0 * 2) ** 0.25)
LN_EPS = 1e-5
GN_EPS = 1e-5
BIG = 30000.0
TWO_PI_LO = 6.28318
COMPUTE = ("pe", "dve", "act", "pool")


class Sched:
    def __init__(self, nc, stack):
        self.nc = nc
        self.stack = stack
        self.sems = {}
        self.cum = {}
        self.known = {e: {} for e in ("pe", "dve", "act", "pool", "sp")}
        self.begin()
        for e in COMPUTE:
            self._sem("eng_" + e)

    def begin(self):
        self.q = {e: [] for e in ("pe", "dve", "act", "pool", "sp")}
        self.res = {}

    def _sem(self, key):
        if key not in self.sems:
            assert not getattr(self, "frozen", False), "semaphore %s not pre-declared" % key
            self.sems[key] = self.stack.enter_context(self.nc.semaphore(key))
            self.cum[key] = 0
        return self.sems[key]

    def _need(self, eng, tok, needs, kind):
        if tok is None:
            return
        key, val, teng = tok
        if teng == eng:
            if eng == "pe" or kind != "raw":
                return
        if self.known[eng].get(key, 0) >= val:
            return
        if needs.get(key, 0) < val:
            needs[key] = val

    def op(self, eng, fn, reads=(), writes=(), dma_sem=None):
        needs = {}
        for r in reads:
            st = self.res.get(r)
            if st is None:
                continue
            self._need(eng, st[0], needs, "raw")
            if r.startswith("ps"):
                for t in st[1].values():
                    self._need(eng, t, needs, "war")
        for w in writes:
            st = self.res.get(w)
            if st is None:
                continue
            self._need(eng, st[0], needs, "waw")
            for t in st[1].values():
                self._need(eng, t, needs, "war")
        for k, v in needs.items():
            self.known[eng][k] = v
        if dma_sem is not None:
            self._sem(dma_sem)
            self.cum[dma_sem] += 16
            tok = (dma_sem, self.cum[dma_sem], "dma")
            inc = (dma_sem, 16)
        else:
            key = "eng_" + eng
            self.cum[key] += 1
            tok = (key, self.cum[key], eng)
            inc = (key, 1)
        self.q[eng].append((list(needs.items()), fn, inc))
        for r in reads:
            st = self.res.setdefault(r, [None, {}])
            st[1][tok[0]] = tok
        for w in writes:
            self.res[w] = [tok, {}]
        return tok

    def finish(self):
        nc = self.nc
        waits = [(k, v) for k, v in self.cum.items() if not k.startswith("eng_") and v > 0]
        self.q["sp"].append((waits, None, None))
        handles = {"pe": "tensor", "dve": "vector", "act": "scalar", "pool": "gpsimd", "sp": "sync"}
        with nc.Block() as block:
            for e, hname in handles.items():
                queue = self.q[e]
                if not queue:
                    continue

                def body(engine, queue=queue):
                    for wl, fn, inc in queue:
                        for k, v in wl:
                            engine.wait_ge(self.sems[k], v)
                        if fn is not None:
                            fn(engine).then_inc(self.sems[inc[0]], inc[1])

                getattr(block, hname)(body)
        self.begin()


def host_consts():
    c = {}
    c["ident_f"] = np.eye(128, dtype=np.float32)
    c["ident_b"] = np.eye(128, dtype=np.float32).astype(ml_dtypes.bfloat16)
    lg = np.log(np.float32(1.0) - np.float32(2.0) ** (-5.0 - np.arange(8, dtype=np.float32))).astype(np.float32)
    n = np.arange(128, dtype=np.float32)
    diff = n[:, None] - n[None, :]
    inner = np.where(diff >= 0, np.exp(np.maximum(diff, 0.0)[None] * lg[:, None, None]), 0.0).astype(np.float32)
    c["ret_mask"] = np.ascontiguousarray(inner.transpose(2, 0, 1) / 16.0).astype(np.float32)
    qdec = np.exp((n[None, :] + 1.0) * lg[:, None]).astype(np.float32)
    c["ret_qdec"] = np.ascontiguousarray(np.broadcast_to(qdec[None], (128, 8, 128))).astype(np.float32)
    kdec = np.exp((127.0 - n[None, :]) * lg[:, None]).astype(np.float32)
    c["ret_kdec"] = np.ascontiguousarray(kdec.T / 16.0).astype(np.float32)
    cdec = [float(np.exp(np.float32(128.0) * lg[h])) for h in range(8)]
    i128 = np.arange(128, dtype=np.float64)
    c["ret_inv"] = ((10000.0 ** (-(i128 * 2.0 / 256.0))).astype(np.float32).astype(np.float64) / (2 * math.pi)).astype(np.float32).reshape(128, 1)
    i16 = np.arange(16, dtype=np.float64)
    minv = ((500000.0 ** (-(i16 * 2.0 / 32.0))).astype(np.float32).astype(np.float64) / (2 * math.pi)).astype(np.float32)
    c["moba_inv"] = np.ascontiguousarray(np.broadcast_to(minv[None], (128, 16))).astype(np.float32)
    m = np.arange(128)
    c["tri"] = (m[:, None] <= m[None, :]).astype(np.float32).astype(ml_dtypes.bfloat16)
    bi = np.zeros((128, 8, 128), np.float32)
    for b in range(8):
        bi[b, b, :] = 1.0
    c["blockind"] = bi.astype(ml_dtypes.bfloat16)
    negm = np.zeros((128, 8, 8), np.float32)
    for i8 in range(8):
        B = (8 + i8) // 2
        negm[:, i8, B:] = -1e30
    c["negmask"] = negm
    sbi = np.zeros((128, 16, 32), np.float32)
    for i in range(16):
        B = i // 2
        sbi[:, i, B + 1:8] = -BIG
    c["selb_init"] = sbi.astype(ml_dtypes.bfloat16)
    sst = np.zeros((128, 1024), np.float32)
    for t in range(8):
        B = t // 2
        sst[B + 1:8, t * 128:(t + 1) * 128] = -BIG
    c["selbT_static"] = sst.astype(ml_dtypes.bfloat16)
    return c, cdec


CONST_SPECS = [("ident_f", [128, 128], F32), ("ident_b", [128, 128], BF16), ("ret_mask", [128, 8, 128], F32),
               ("ret_qdec", [128, 8, 128], F32), ("ret_kdec", [128, 8], F32), ("ret_inv", [128, 1], F32),
               ("moba_inv", [128, 16], F32), ("tri", [128, 128], BF16), ("blockind", [128, 8, 128], BF16),
               ("negmask", [128, 8, 8], F32), ("selb_init", [128, 16, 32], BF16), ("selbT_static", [128, 1024], BF16)]


def build(layers=(0, 1), cdec=None):
    nc = bass.Bass("TRN2", target_bir_lowering=False)
    dr = {}

    def din(name, shape, dt):
        dr[name] = nc.dram_tensor(name, shape, dt, kind="ExternalInput").ap()

    din("x", [SEQ, D], F32)
    din("cT", [128, 16], F32)
    din("posb", [128, SEQ], I32)
    din("post", [128, 16], I32)
    din("w_ada", [2, D, 3 * D], F32)
    din("b_fm", [128, 2, 48], F32)
    din("b_gate", [128, 2, D], F32)
    din("lng", [128, 2, D], F32)
    din("lnb", [128, 2, D], F32)
    din("ret_w_in", [D, 12288], F32)
    din("ret_w_out", [4096, D], F32)
    din("moba_w_in", [D, 8192], F32)
    din("moba_w_out", [D, D], F32)
    for name, shape, dt in CONST_SPECS:
        din(name, shape, dt)
    out_d = nc.dram_tensor("out", [SEQ, D], F32, kind="ExternalOutput").ap()
    yT_d = [nc.dram_tensor("yT0", [16, 128, 32, 128], BF16, kind="Internal").ap(),
            nc.dram_tensor("yT1", [16, 128, 16, 128], BF16, kind="Internal").ap()]
    x1_d = nc.dram_tensor("x1s", [SEQ, D], F32, kind="Internal").ap()
    gate_d = nc.dram_tensor("gates", [2, 128, D], F32, kind="Internal").ap()

    with contextlib.ExitStack() as outer:
        S = Sched(nc, outer)

        uniq = {"n": 0}

        def sb(stack, name, shape, dt):
            uniq["n"] += 1
            return stack.enter_context(nc.sbuf_tensor("s%d_%s" % (uniq["n"], name), shape, dt))

        pall = outer.enter_context(nc.psum_tensor("pall", [128, 4096], F32))
        pall_b = pall[:].bitcast(BF16)
        ps = [pall[:, i * 512:(i + 1) * 512] for i in range(8)]
        psb = [pall_b[:, i * 1024:(i + 1) * 1024] for i in range(8)]
        PS = [f"ps{i}" for i in range(8)]

        ident_f = sb(outer, "ident_f", [128, 128], F32)
        ident_b = sb(outer, "ident_b", [128, 128], BF16)
        mod_fm = sb(outer, "mod_fm", [128, 2, 48], F32)
        s1p = sb(outer, "s1p", [128, 2, 16], F32)
        eps_t = sb(outer, "eps_t", [128, 1], F32)

        def mm(out, lhsT, rhs, start, stop, reads, writes, skip=False):
            S.op("pe", lambda e: e.matmul(out, lhsT=lhsT, rhs=rhs, start=start, stop=stop, skip_group_check=skip),
                 reads, writes)

        def tr(out, in_, ident, reads, writes):
            S.op("pe", lambda e: e.transpose(out=out, in_=in_, identity=ident), reads, writes)

        def act(out, in_, func, reads, writes, scale=1.0, bias=0.0):
            S.op("act", lambda e: e.activation(out=out, in_=in_, func=func, bias=bias, scale=scale), reads, writes)

        def ts(eng, out, in0, s1, s2, op0, op1, reads, writes):
            if op1 is None:
                S.op(eng, lambda e: e.tensor_scalar(out=out, in0=in0, scalar1=s1, scalar2=None, op0=op0), reads, writes)
            else:
                S.op(eng, lambda e: e.tensor_scalar(out=out, in0=in0, scalar1=s1, scalar2=s2, op0=op0, op1=op1),
                     reads, writes)

        def tt(eng, out, in0, in1, op, reads, writes):
            S.op(eng, lambda e: e.tensor_tensor(out=out, in0=in0, in1=in1, op=op), reads, writes)

        def cp(eng, out, in_, reads, writes):
            S.op(eng, lambda e: e.tensor_copy(out=out, in_=in_), reads, writes)

        def dma(q, out, in_, reads, writes, sem, **kw):
            S.op(q, lambda e: e.dma_start(out=out, in_=in_, **kw), reads, writes, dma_sem=sem)

        dma_sem_names = (["d_c%d" % i for i in range(8)] + ["d_wa0", "d_wa1", "d_wr0", "d_wr1", "d_wr2", "d_xs0", "d_xs1",
                         "d_yst0", "d_yst1", "d_yst2", "d_yst3", "d_yTb", "d_wk0", "d_wk1", "d_wk2", "d_wk3", "d_wk4", "d_wk5", "d_wk6", "d_wk7", "d_wk8", "d_wk9", "d_wk10", "d_wk11", "d_wk12", "d_wk13", "d_wk14", "d_wk15", "d_wk16", "d_wk17", "d_wk18", "d_wk19", "d_wk20", "d_wk21", "d_wk22", "d_wk23", "d_wk24", "d_wk25", "d_wk26", "d_wk27", "d_wk28", "d_wk29", "d_wk30", "d_wk31", "d_zo0_0", "d_zo0_1", "d_zo0_2", "d_zo0_3", "d_zo1_0", "d_zo1_1", "d_zo1_2", "d_zo1_3", "d_wo0", "d_wo1", "d_xt0", "d_xt1", "d_zo0", "d_zo1", "d_g", "d_pos"])
        for nme in dma_sem_names:
            S._sem(nme)
        allsems = list(S.sems.values())
        S.frozen = True

        def clear_block():
            with nc.Block() as blk:
                @blk.gpsimd
                def _(g):
                    for s_ in allsems:
                        g.sem_clear(s_)

        clear_block()

        def phase0():
            with contextlib.ExitStack() as ph:
                wa = [sb(ph, f"wa{i}", [128, 3 * D], BF16) for i in range(2)]
                c_f = sb(ph, "c_f", [128, 16], F32)
                c_b = sb(ph, "c_b", [128, 16], BF16)
                ones_b = sb(ph, "ones_b", [128, 128], BF16)
                c_rep = sb(ph, "c_rep", [128, 16, 128], BF16)
                bfm = sb(ph, "bfm", [128, 2, 48], F32)
                bgate = sb(ph, "bgate", [128, 2, D], F32)
                grow = sb(ph, "grow0", [128, D], F32)
                dma("sp", ident_f[:], dr["ident_f"][:, :], [], ["ident_f"], "d_c0")
                dma("sp", ident_b[:], dr["ident_b"][:, :], [], ["ident_b"], "d_c1")
                dma("sp", c_f[:], dr["cT"][:, :], [], ["c_f"], "d_c2")
                dma("sp", bfm[:], dr["b_fm"][:, :, :], [], ["bfm"], "d_c3")
                dma("sp", bgate[:], dr["b_gate"][:, :, :], [], ["bgate"], "d_c4")
                S.op("dve", lambda e: e.memset(eps_t[:], LN_EPS), [], ["eps_t"])
                S.op("dve", lambda e: e.memset(ones_b[:], 1.0), [], ["ones_b"])
                cp("dve", c_b[:], c_f[:], ["c_f"], ["c_b"])
                for ch in range(16):
                    ts("dve", c_rep[:, ch, :], ones_b[:], c_f[:, ch:ch + 1], None, ALU.mult, None,
                       ["ones_b", "c_f"], ["c_rep"])
                cnt = 0
                for l in range(2):
                    for ch in range(16):
                        b = cnt % 2
                        cnt += 1
                        dma("pool", wa[b][:], dr["w_ada"][l, ch * 128:(ch + 1) * 128, :], [], [f"wa{b}"], f"d_wa{b}",
                            max_dma_last_dim=8192)
                        for jt in range(48):
                            mm(ps[0][:, jt:jt + 1], wa[b][:, jt * 128:(jt + 1) * 128], c_b[:, ch:ch + 1],
                               (ch == 0 and jt == 0), (ch == 15), [f"wa{b}", "c_b"], [PS[0]], skip=True)
                        for n_ in range(4):
                            mm(ps[1 + n_][:, :], c_rep[:, ch, :], wa[b][:, 4096 + n_ * 512:4096 + (n_ + 1) * 512],
                               ch == 0, ch == 15, [f"wa{b}", "c_rep"], [PS[1 + n_]])
                    tt("dve", mod_fm[:, l, :], ps[0][:, 0:48], bfm[:, l, :], ALU.add, [PS[0], "bfm"], ["mod_fm"])
                    ts("dve", s1p[:, l, :], mod_fm[:, l, 16:32], 1.0, None, ALU.add, None, ["mod_fm"], ["s1p"])
                    for n_ in range(4):
                        tt("dve", grow[:, n_ * 512:(n_ + 1) * 512], ps[1 + n_][:, :], bgate[:, l, n_ * 512:(n_ + 1) * 512],
                           ALU.add, [PS[1 + n_], "bgate"], ["grow"])
                    dma("sp", gate_d[l, :, :], grow[:], ["grow"], ["gate_d"], "d_g")
                S.finish()

        def phase1(l, xsrc, hT):
            with contextlib.ExitStack() as ph:
                xs = [sb(ph, f"xs{i}", [128, 4, D], F32) for i in range(2)]
                for g in range(4):
                    b = g % 2
                    dma("sp", xs[b][:], xsrc[g * 512:(g + 1) * 512, :].rearrange("(t p) d -> p t d", p=128),
                        [], [f"xs{b}"], f"d_xs{b}")
                    for c in range(16):
                        bk = c % 4
                        for t in range(4):
                            tr(ps[bk][:, t * 128:(t + 1) * 128], xs[b][:, t, c * 128:(c + 1) * 128], ident_f[:],
                               [f"xs{b}", "ident_f"], [PS[bk]])
                        act(hT[:, c, g * 512:(g + 1) * 512], ps[bk][:, :], AF.Identity, [PS[bk], "s1p", "mod_fm"], ["hT"],
                            scale=s1p[:, l, c:c + 1], bias=mod_fm[:, l, c:c + 1])
                S.finish()

        def phase2_ret(hT):
            w_in = dr["ret_w_in"].rearrange("(c p) n -> p c n", p=128)
            yT = yT_d[0]
            with contextlib.ExitStack() as ph:
                wr = [sb(ph, f"wr{i}", [128, 16, 256], BF16) for i in range(3)]
                cosT = sb(ph, "cosT", [128, SEQ], F32)
                sinT = sb(ph, "sinT", [128, SEQ], F32)
                pos_i = sb(ph, "pos_i", [128, 512], I32)
                ta = sb(ph, "ta", [128, 512], F32)
                tb_ = sb(ph, "tb_", [128, 512], F32)
                ti = sb(ph, "ti", [128, 512], I32)
                qT = sb(ph, "qT", [128, 2, SEQ], BF16)
                kT = sb(ph, "kT", [128, 2, SEQ], BF16)
                v = sb(ph, "v", [128, 16, 512], BF16)
                gT = sb(ph, "gT", [128, 4, SEQ], BF16)
                r = [sb(ph, f"r{i}", [128, 512], F32) for i in range(4)]
                sT = [sb(ph, f"sT{i}", [128, 128], BF16) for i in range(2)]
                qd = [sb(ph, f"qd{i}", [128, 2, 128], BF16) for i in range(2)]
                kd = [sb(ph, f"kd{i}", [128, 256], BF16) for i in range(2)]
                st_f = sb(ph, "st_f", [128, 2, 512], F32)
                st_b = sb(ph, "st_b", [128, 2, 512], BF16)
                on_ = [sb(ph, f"on{i}", [128, 512], BF16) for i in range(2)]
                stats = sb(ph, "stats", [128, 6], F32)
                mv = sb(ph, "mv", [128, 2], F32)
                lnv = sb(ph, "lnv", [128, 1], F32)
                rstd = sb(ph, "rstd", [128, 1], F32)
                yst = [sb(ph, f"yst{i}", [128, 4, 128], BF16) for i in range(4)]
                mask = sb(ph, "mask", [128, 8, 128], F32)
                qdec = sb(ph, "qdec", [128, 8, 128], F32)
                kdec = sb(ph, "kdec", [128, 8], F32)
                inv = sb(ph, "inv", [128, 1], F32)
                dma("sp", mask[:], dr["ret_mask"][:, :, :], [], ["mask"], "d_c0")
                dma("sp", qdec[:], dr["ret_qdec"][:, :, :], [], ["qdec"], "d_c1")
                dma("sp", kdec[:], dr["ret_kdec"][:, :], [], ["kdec"], "d_c2")
                dma("sp", inv[:], dr["ret_inv"][:, :], [], ["inv"], "d_c3")

                pieces = []
                for h in range(8):
                    pieces += [("q", h, h * 256), ("k", h, 2048 + h * 256), ("v0", h, 4096 + h * 512),
                               ("v1", h, 4096 + h * 512 + 256), ("g0", h, 8192 + h * 512), ("g1", h, 8192 + h * 512 + 256)]
                state = {"next": 0}

                def prefetch(upto):
                    while state["next"] <= upto and state["next"] < len(pieces):
                        i = state["next"]
                        b = i % 3
                        col = pieces[i][2]
                        dma("pool", wr[b][:], w_in[:, :, col:col + 256], [], [f"wr{b}"], f"d_wr{b}")
                        state["next"] += 1

                prefetch(1)

                for pc in range(4):
                    sl = slice(pc * 512, (pc + 1) * 512)
                    dma("sp", pos_i[:], dr["posb"][:, sl], [], ["pos_i"], "d_pos")
                    cp("dve", ta[:], pos_i[:], ["pos_i"], ["ta"])
                    ts("dve", ta[:], ta[:], inv[:, 0:1], None, ALU.mult, None, ["ta", "inv"], ["ta"])
                    cp("dve", ti[:], ta[:], ["ta"], ["ti"])
                    cp("dve", tb_[:], ti[:], ["ti"], ["tb_"])
                    tt("dve", ta[:], ta[:], tb_[:], ALU.subtract, ["ta", "tb_"], ["ta"])
                    ts("dve", tb_[:], ta[:], 0.5, None, ALU.is_gt, None, ["ta"], ["tb_"])
                    tt("dve", ta[:], ta[:], tb_[:], ALU.subtract, ["ta", "tb_"], ["ta"])
                    act(sinT[:, sl], ta[:], AF.Sin, ["ta"], ["sinT"], scale=TWO_PI_LO)
                    ts("dve", tb_[:], ta[:], 0.25, None, ALU.add, None, ["ta"], ["tb_"])
                    ts("dve", ta[:], tb_[:], 0.5, None, ALU.is_gt, None, ["tb_", "sinT"], ["ta"])
                    tt("dve", tb_[:], tb_[:], ta[:], ALU.subtract, ["ta", "tb_"], ["tb_"])
                    act(cosT[:, sl], tb_[:], AF.Sin, ["tb_"], ["cosT"], scale=TWO_PI_LO)

                bankc = {"n": 0}
                pend = []

                def flush(keep):
                    while len(pend) > keep:
                        pend.pop(0)()

                def next_bank():
                    b = bankc["n"] % 3
                    bankc["n"] += 1
                    return b

                def proj_fm(wb, half, tbk, bk):
                    for kc in range(KC):
                        mm(ps[bk][:, :], wr[wb][:, kc, half * 128:(half + 1) * 128], hT[:, kc, tbk * 512:(tbk + 1) * 512],
                           kc == 0, kc == KC - 1, [f"wr{wb}", "hT"], [PS[bk]])

                pi = 0
                for h in range(8):
                    for dst, dname in ((qT, "qT"), (kT, "kT")):
                        wb = pi % 3
                        prefetch(pi + 2)
                        for tbk in range(4):
                            sl = slice(tbk * 512, (tbk + 1) * 512)
                            bA = next_bank()
                            proj_fm(wb, 0, tbk, bA)
                            bB = next_bank()
                            proj_fm(wb, 1, tbk, bB)
                            tt("dve", r[0][:], ps[bA][:, :], cosT[:, sl], ALU.mult, [PS[bA], "cosT"], ["r0"])
                            tt("dve", r[1][:], ps[bB][:, :], sinT[:, sl], ALU.mult, [PS[bB], "sinT"], ["r1"])
                            tt("dve", r[2][:], ps[bB][:, :], cosT[:, sl], ALU.mult, [PS[bB], "cosT"], ["r2"])
                            tt("dve", r[3][:], ps[bA][:, :], sinT[:, sl], ALU.mult, [PS[bA], "sinT"], ["r3"])
                            tt("pool", dst[:, 0, sl], r[0][:], r[1][:], ALU.subtract, ["r0", "r1"], [dname])
                            tt("pool", dst[:, 1, sl], r[2][:], r[3][:], ALU.add, ["r2", "r3"], [dname])
                        pi += 1
                    for vp in range(2):
                        wb = pi % 3
                        prefetch(pi + 2)
                        for t in range(NT):
                            bk = next_bank()
                            for kc in range(KC):
                                mm(ps[bk][:, 0:256], hT[:, kc, t * 128:(t + 1) * 128], wr[wb][:, kc, :],
                                   kc == 0, kc == KC - 1, [f"wr{wb}", "hT"], [PS[bk]])
                            act(v[:, t, vp * 256:(vp + 1) * 256], ps[bk][:, 0:256], AF.Identity, [PS[bk]], ["v"])
                        pi += 1
                    wb_g = [pi % 3, (pi + 1) % 3]
                    prefetch(pi + 2)
                    pi += 2
                    ggroups = [(tbk, gp, half) for tbk in range(4) for gp in range(2) for half in range(2)]
                    gstate = {"n": 0}

                    def emit_g(idx, incore=True):
                        tbk, gp, half = ggroups[idx]
                        bk = 1 if incore else next_bank()
                        wbg = wb_g[gp]
                        for kc in range(KC):
                            mm(ps[bk][:, :], wr[wbg][:, kc, half * 128:(half + 1) * 128], hT[:, kc, tbk * 512:(tbk + 1) * 512],
                               kc == 0, kc == KC - 1, [f"wr{wbg}", "hT"], [PS[bk]])
                        act(gT[:, gp * 2 + half, tbk * 512:(tbk + 1) * 512], ps[bk][:, :], AF.Silu, [PS[bk]], ["gT"])

                    for idx in range(4):
                        emit_g(idx, incore=False)
                    for c in range(16):
                        cs = slice(c * 128, (c + 1) * 128)
                        bf = c % 2
                        ob_ = 5 if c % 2 == 0 else 2
                        if c < 15:
                            for half in range(2):
                                tr(psb[3][:, half * 128:(half + 1) * 128], kT[:, half, cs], ident_b[:],
                                   ["kT", "ident_b"], [PS[3]])
                            act(kd[bf][:], psb[3][:, 0:256], AF.Identity, [PS[3], "kdec"], [f"kd{bf}"],
                                scale=kdec[:, h:h + 1])
                        for half in range(2):
                            mm(ps[4][:, 0:128], kT[:, half, cs], qT[:, half, cs], half == 0, half == 1,
                               ["kT", "qT"], [PS[4]])
                        tt("dve", sT[bf][:], ps[4][:, 0:128], mask[:, h, :], ALU.mult, [PS[4], "mask"], [f"sT{bf}"])
                        if c > 0:
                            for half in range(2):
                                tt("pool", qd[bf][:, half, :], qT[:, half, cs], qdec[:, h, :], ALU.mult,
                                   ["qT", "qdec"], [f"qd{bf}"])
                        if c < 12:
                            emit_g(4 + c)
                        mm(ps[ob_][:, :], sT[bf][:], v[:, c, :], True, c == 0, [f"sT{bf}", "v"], [PS[ob_]])
                        if c > 0:
                            for half in range(2):
                                mm(ps[ob_][:, :], qd[bf][:, half, :], st_b[:, half, :], False, half == 1,
                                   [f"qd{bf}", "st_b"], [PS[ob_]])
                        if c < 15:
                            for half in range(2):
                                bk = 6 + half
                                mm(ps[bk][:, :], kd[bf][:, half * 128:(half + 1) * 128], v[:, c, :], True, True,
                                   [f"kd{bf}", "v"], [PS[bk]])
                                if c == 0:
                                    cp("dve", st_f[:, half, :], ps[bk][:, :], [PS[bk]], [f"st_f{half}"])
                                else:
                                    S.op("dve", lambda e, half=half, bk=bk, cd=cdec[h]: e.scalar_tensor_tensor(
                                        out=st_f[:, half, :], in0=st_f[:, half, :], scalar=cd, in1=ps[bk][:, :],
                                        op0=ALU.mult, op1=ALU.add), [PS[bk], f"st_f{half}"], [f"st_f{half}"])
                                act(st_b[:, half, :], st_f[:, half, :], AF.Identity, [f"st_f{half}"], ["st_b"])
                        S.op("dve", lambda e, ob_=ob_: e.bn_stats(out=stats[:], in_=ps[ob_][:, :]), [PS[ob_]], ["stats"])
                        S.op("dve", lambda e: e.bn_aggr(out=mv[:], in_=stats[:]), ["stats"], ["mv"])
                        act(lnv[:], mv[:, 1:2], AF.Ln, ["mv", "eps_t"], ["lnv"], bias=eps_t[:, 0:1])
                        act(rstd[:], lnv[:], AF.Exp, ["lnv"], ["rstd"], scale=-0.5)
                        ts("dve", on_[bf][:], ps[ob_][:, :], mv[:, 0:1], rstd[:, 0:1], ALU.subtract, ALU.mult,
                           [PS[ob_], "mv", "rstd"], [f"on{bf}"])

                        def fin(c=c, cs=cs, bf=bf, h=h):
                            for fc in range(4):
                                tr(psb[0][:, fc * 128:(fc + 1) * 128], on_[bf][:, fc * 128:(fc + 1) * 128], ident_b[:],
                                   [f"on{bf}", "ident_b"], [PS[0]])
                            yb = c % 4
                            tt("dve", yst[yb][:], psb[0][:, 0:512].rearrange("p (f t) -> p f t", t=128), gT[:, :, cs], ALU.mult,
                               [PS[0], "gT"], [f"yst{yb}"])
                            dma("sp", yT[c, :, h * 4:(h + 1) * 4, :], yst[yb][:], [f"yst{yb}"], ["yT_d"], f"d_yst{yb}")

                        pend.append(fin)
                        flush(1)
                    flush(0)
                S.finish()

        def phase2_moba(hT):
            w_in = dr["moba_w_in"].rearrange("(c p) n -> p c n", p=128)
            yT = yT_d[1]
            scale = 128.0 ** -0.5
            with contextlib.ExitStack() as ph:
                wr = [sb(ph, f"wr{i}", [128, 16, 256], BF16) for i in range(3)]
                qtm = [sb(ph, f"qtm{i}", [128, 256], BF16) for i in range(3)]
                qT = sb(ph, "qT", [128, 2, SEQ], BF16)
                kT = sb(ph, "kT", [128, 2, SEQ], BF16)
                vext = sb(ph, "vext", [128, 16, 2, 130], BF16)
                gT = sb(ph, "gT", [128, 2, SEQ], BF16)
                PT = [sb(ph, f"PT{i}", [128, 16, 512], BF16) for i in range(2)]
                pos_i = sb(ph, "pos_i", [128, 16], I32)
                posf = sb(ph, "posf", [128, 16], F32)
                minv = sb(ph, "minv", [128, 16], F32)
                ta = sb(ph, "ta", [128, 16, 16], F32)
                tb_ = sb(ph, "tb_", [128, 16, 16], F32)
                ti = sb(ph, "ti", [128, 16, 16], I32)
                cos2 = sb(ph, "cos2", [128, 16, 2, 16], F32)
                sin2 = sb(ph, "sin2", [128, 16, 2, 16], F32)
                r = [[sb(ph, f"r{s_}_{i}", [128, 2, 16], F32) for i in range(4)] for s_ in range(2)]
                kmf = sb(ph, "kmf", [128, 2, 8], F32)
                kmb = sb(ph, "kmb", [128, 2, 8], BF16)
                gm = [sb(ph, f"gm{i}", [128, 8, 8], F32) for i in range(2)]
                top8 = [sb(ph, f"top8{i}", [128, 8, 8], F32) for i in range(2)]
                negmask = sb(ph, "negmask", [128, 8, 8], F32)
                selb = [sb(ph, f"selb{i}", [128, 16, 32], BF16) for i in range(2)]
                selbT = [sb(ph, f"selbT{i}", [128, SEQ], BF16) for i in range(2)]
                blockind = sb(ph, "blockind", [128, 8, 128], BF16)
                tri = sb(ph, "tri", [128, 128], BF16)
                o_n = [sb(ph, f"o_n{i}", [128, 4, 128], BF16) for i in range(2)]
                rden = [sb(ph, f"rden{i}", [128, 1], F32) for i in range(2)]
                yst = [sb(ph, f"yst{i}", [128, 512], BF16) for i in range(2)]
                dma("sp", negmask[:], dr["negmask"][:, :, :], [], ["negmask"], "d_c0")
                dma("sp", blockind[:], dr["blockind"][:, :, :], [], ["blockind"], "d_c2")
                dma("sp", tri[:], dr["tri"][:, :], [], ["tri"], "d_c3")
                dma("sp", minv[:], dr["moba_inv"][:, :], [], ["minv"], "d_c4")
                dma("sp", pos_i[:], dr["post"][:, :], [], ["pos_i"], "d_c5")
                for hd in range(2):
                    dma("sp", selb[hd][:], dr["selb_init"][:, :, :], [], [f"selb{hd}"], "d_c1" if hd == 0 else "d_c7")
                    S.op("dve", lambda e, hd=hd: e.memset(selbT[hd][:, 1024:2048], 0.0), [], [f"selbT_d{hd}"])
                    dma("sp", selbT[hd][:, 0:1024], dr["selbT_static"][:, :], [], [f"selbT_s{hd}"], "d_c6" if hd == 0 else "d_pos")
                S.op("dve", lambda e: e.memset(vext[:, :, :, 128:130], 1.0), [], ["vext1"])

                pieces = []
                for g in range(8):
                    pieces += [("q", g, g * 256), ("k", g, 2048 + g * 256), ("v", g, 4096 + g * 256), ("g", g, 6144 + g * 256)]
                state = {"next": 0}

                def prefetch(upto):
                    while state["next"] <= upto and state["next"] < len(pieces):
                        i = state["next"]
                        b = i % 3
                        col = pieces[i][2]
                        dma("pool", wr[b][:], w_in[:, :, col:col + 256], [], [f"wr{b}"], f"d_wr{b}")
                        state["next"] += 1

                prefetch(1)

                cp("dve", posf[:], pos_i[:], ["pos_i"], ["posf"])
                for t in range(16):
                    ts("dve", ta[:, t, :], minv[:], posf[:, t:t + 1], None, ALU.mult, None, ["minv", "posf"], ["ta"])
                cp("dve", ti[:], ta[:], ["ta"], ["ti"])
                cp("dve", tb_[:], ti[:], ["ti"], ["tb_"])
                tt("dve", ta[:], ta[:], tb_[:], ALU.subtract, ["ta", "tb_"], ["ta"])
                ts("dve", tb_[:], ta[:], 0.5, None, ALU.is_gt, None, ["ta"], ["tb_"])
                tt("dve", ta[:], ta[:], tb_[:], ALU.subtract, ["ta", "tb_"], ["ta"])
                for hd in range(2):
                    act(sin2[:, :, hd, :], ta[:], AF.Sin, ["ta"], ["sin2"], scale=TWO_PI_LO)
                ts("dve", tb_[:], ta[:], 0.25, None, ALU.add, None, ["ta"], ["tb_"])
                ts("dve", ta[:], tb_[:], 0.5, None, ALU.is_gt, None, ["tb_", "sin2"], ["ta"])
                tt("dve", tb_[:], tb_[:], ta[:], ALU.subtract, ["ta", "tb_"], ["tb_"])
                for hd in range(2):
                    act(cos2[:, :, hd, :], tb_[:], AF.Sin, ["tb_"], ["cos2"], scale=TWO_PI_LO)

                bankc = {"n": 0, "att": 0, "po": 0, "t": 0, "qk": 0}
                pend = []

                def flush(keep):
                    while len(pend) > keep:
                        pend.pop(0)()

                def next_bank():
                    b = bankc["n"] % 2
                    bankc["n"] += 1
                    return b

                pi = 0
                for g in range(8):
                    for dst, dname in ((qT, "qT"), (kT, "kT")):
                        wb = pi % 3
                        prefetch(pi + 2)
                        for t in range(NT):
                            bk = next_bank()
                            qb_ = bankc["t"] % 3
                            rs = r[bankc["t"] % 2]
                            rn = ["r%d_%d" % (bankc["t"] % 2, i_) for i_ in range(4)]
                            bankc["t"] += 1
                            for kc in range(KC):
                                mm(ps[bk][:, 0:256], hT[:, kc, t * 128:(t + 1) * 128], wr[wb][:, kc, :],
                                   kc == 0, kc == KC - 1, [f"wr{wb}", "hT"], [PS[bk]])
                            psv = ps[bk][:, 0:256].rearrange("p (h d) -> p h d", d=128)
                            qv = qtm[qb_][:].rearrange("p (h d) -> p h d", d=128)
                            act(qv[:, :, 32:128], psv[:, :, 32:128], AF.Identity, [PS[bk]], [f"qtm{qb_}"])
                            tt("dve", rs[0][:], psv[:, :, 0:16], cos2[:, t, :, :], ALU.mult, [PS[bk], "cos2"], [rn[0]])
                            tt("dve", rs[1][:], psv[:, :, 16:32], sin2[:, t, :, :], ALU.mult, [PS[bk], "sin2"], [rn[1]])
                            tt("dve", rs[2][:], psv[:, :, 16:32], cos2[:, t, :, :], ALU.mult, [PS[bk], "cos2"], [rn[2]])
                            tt("dve", rs[3][:], psv[:, :, 0:16], sin2[:, t, :, :], ALU.mult, [PS[bk], "sin2"], [rn[3]])
                            tt("pool", qv[:, :, 0:16], rs[0][:], rs[1][:], ALU.subtract, [rn[0], rn[1]], [f"qtm{qb_}"])
                            tt("pool", qv[:, :, 16:32], rs[2][:], rs[3][:], ALU.add, [rn[2], rn[3]], [f"qtm{qb_}"])

                            def fin(t=t, qb_=qb_, dst=dst, dname=dname):
                                for hd in range(2):
                                    tr(psb[3][:, hd * 128:(hd + 1) * 128], qtm[qb_][:, hd * 128:(hd + 1) * 128], ident_b[:],
                                       [f"qtm{qb_}", "ident_b"], [PS[3]])
                                act(dst[:, :, t * 128:(t + 1) * 128], psb[3][:, 0:256].rearrange("p (h d) -> p h d", d=128),
                                    AF.Identity, [PS[3]], [dname])

                            pend.append(fin)
                            flush(2)
                        pi += 1
                    wb = pi % 3
                    prefetch(pi + 2)
                    for t in range(NT):
                        bk = next_bank()
                        for kc in range(KC):
                            mm(ps[bk][:, 0:256], hT[:, kc, t * 128:(t + 1) * 128], wr[wb][:, kc, :],
                               kc == 0, kc == KC - 1, [f"wr{wb}", "hT"], [PS[bk]])
                        act(vext[:, t, :, 0:128], ps[bk][:, 0:256].rearrange("p (h d) -> p h d", d=128), AF.Identity,
                            [PS[bk]], ["vext"])
                        if t == 1:
                            flush(0)
                        if t == 3:
                            S.op("dve", lambda e: e.tensor_reduce(out=kmf[:], in_=kT[:].rearrange("p h (n k) -> p h n k", k=256),
                                                                  axis=AX.X, op=ALU.add), ["kT"], ["kmf"])
                            ts("dve", kmb[:], kmf[:], 1.0 / 256.0, None, ALU.mult, None, ["kmf"], ["kmb"])
                            for hd in range(2):
                                for i8 in range(8):
                                    i = 8 + i8
                                    mm(ps[4][:, hd * 64 + i8 * 8:hd * 64 + (i8 + 1) * 8], qT[:, hd, i * 128:(i + 1) * 128],
                                       kmb[:, hd, :], True, True, ["qT", "kmb"], [PS[4]])
                        if t == 5:
                            for hd in range(2):
                                tt("dve", gm[hd][:].rearrange("p a b -> p (a b)"), ps[4][:, hd * 64:(hd + 1) * 64],
                                   negmask[:].rearrange("p a b -> p (a b)"), ALU.add, [PS[4], "negmask"], [f"gm{hd}"])
                                for i8 in range(8):
                                    S.op("dve", lambda e, i8=i8, hd=hd: e.max(out=top8[hd][:, i8, :], in_=gm[hd][:, i8, :]),
                                         [f"gm{hd}"], [f"top8{hd}"])
                                for i8 in range(8):
                                    i = 8 + i8
                                    B = i // 2
                                    ts("dve", selb[hd][:, i, 0:B], gm[hd][:, i8, 0:B], top8[hd][:, i8, 2:3], -BIG, ALU.is_lt, ALU.mult,
                                       [f"gm{hd}", f"top8{hd}"], [f"selb{hd}"])
                        if t == 9 or t == 11:
                            hd = 0 if t == 9 else 1
                            for i8 in range(8):
                                tr(psb[3][0:32, i8 * 128:(i8 + 1) * 128], selb[hd][:, 8 + i8, :], ident_b[:],
                                   [f"selb{hd}", "ident_b"], [PS[3]])
                            cp("dve", selbT[hd][0:32, 1024:2048], psb[3][0:32, 0:1024], [PS[3]], [f"selbT_d{hd}"])
                    pi += 1
                    wb = pi % 3
                    prefetch(pi + 2)
                    for hd in range(2):
                        for tbk in range(4):
                            bk = next_bank()
                            for kc in range(KC):
                                mm(ps[bk][:, :], wr[wb][:, kc, hd * 128:(hd + 1) * 128], hT[:, kc, tbk * 512:(tbk + 1) * 512],
                                   kc == 0, kc == KC - 1, [f"wr{wb}", "hT"], [PS[bk]])
                            act(gT[:, hd, tbk * 512:(tbk + 1) * 512], ps[bk][:, :], AF.Silu, [PS[bk]], ["gT"])
                    pi += 1
                    for hd in range(2):
                        for qb in range(4):
                            pb = bankc["att"] % 2
                            bankc["att"] += 1
                            qs = slice(qb * 512, (qb + 1) * 512)
                            nkt = 4 * (qb + 1)
                            for p2 in range(nkt // 2):
                                j0 = 2 * p2
                                bset = bankc["qk"] % 2
                                bankc["qk"] += 1
                                b0 = 4 + 2 * bset
                                c0 = 256 if j0 == nkt - 2 else 0
                                ncol = 512 - c0
                                qsl = slice(qb * 512 + c0, (qb + 1) * 512)
                                for jj in range(2):
                                    j = j0 + jj
                                    need_mask = (qb >= 2) and (j < 4 * qb + 2)
                                    mm(ps[b0 + jj][:, 0:ncol], kT[:, hd, j * 128:(j + 1) * 128], qT[:, hd, qsl], True, not need_mask,
                                       ["kT", "qT"], [PS[b0 + jj]])
                                    if need_mask:
                                        mm(ps[b0 + jj][:, 0:ncol], blockind[:, j // 2, :], selbT[hd][:, qsl], False, True,
                                           ["blockind", f"selbT_s{hd}", f"selbT_d{hd}"], [PS[b0 + jj]])
                                src = pall[:, b0 * 512:(b0 + 2) * 512].rearrange("p (a n) -> p a n", a=2)[:, :, 0:ncol]
                                act(PT[pb][:, j0:j0 + 2, c0:512], src, AF.Exp, [PS[b0], PS[b0 + 1]], [f"PT{pb}"], scale=scale)
                                for jj in range(2):
                                    j = j0 + jj
                                    if j >= 4 * qb:
                                        il = j - 4 * qb
                                        tt("pool", PT[pb][:, j, il * 128:(il + 1) * 128], PT[pb][:, j, il * 128:(il + 1) * 128],
                                           tri[:], ALU.mult, [f"PT{pb}", "tri"], [f"PT{pb}"])
                                if p2 == 0:
                                    flush(0)
                            ob = bankc["att"] % 2
                            for il in range(4):
                                i = 4 * qb + il
                                pk = bankc["po"] % 2
                                rb = bankc["po"] % 2
                                bankc["po"] += 1
                                for j in range(i + 1):
                                    mm(ps[pk][:, 0:129], PT[pb][:, j, il * 128:(il + 1) * 128], vext[:, j, hd, 0:129],
                                       j == 0, j == i, [f"PT{pb}", "vext", "vext1"], [PS[pk]])
                                S.op("dve", lambda e, pk=pk, rb=rb: e.reciprocal(out=rden[rb][:], in_=ps[pk][:, 128:129]),
                                     [PS[pk]], [f"rden{rb}"])
                                ts("dve", o_n[ob][:, il, :], ps[pk][:, 0:128], rden[rb][:, 0:1], None, ALU.mult, None,
                                   [PS[pk], f"rden{rb}"], [f"o_n{ob}"])

                                def fin_tr(il=il, ob=ob):
                                    tr(psb[2][:, il * 128:(il + 1) * 128], o_n[ob][:, il, :], ident_b[:],
                                       [f"o_n{ob}", "ident_b"], [PS[2]])

                                pend.append(fin_tr)
                                flush(2)

                            def fin_y(ob=ob, hd=hd, qs=qs, hg=g * 2 + hd, qb=qb):
                                tt("dve", yst[ob][:], psb[2][:, 0:512], gT[:, hd, qs], ALU.mult, [PS[2], "gT"], [f"yst{ob}"])
                                dma("sp", yT[qb * 4:(qb + 1) * 4, :, hg, :].rearrange("t p k -> p t k"),
                                    yst[ob][:].rearrange("p (t k) -> p t k", k=128), [f"yst{ob}"], ["yT_d"], f"d_yst{ob}")

                            pend.append(fin_y)
                    flush(0)
                S.finish()

        def phase3(l, xsrc, dst, w_out, kco):
            wv = w_out.rearrange("(c p) n -> p c n", p=128)
            yTs = yT_d[l]
            with contextlib.ExitStack() as ph:
                wres = sb(ph, "wres", [128, kco, D], BF16)
                yTt = [sb(ph, f"yTt{i}", [128, kco, 128], BF16) for i in range(2)]
                z = [sb(ph, f"z{i}", [128, D], F32) for i in range(2)]
                xt = [sb(ph, f"xt{i}", [128, D], F32) for i in range(2)]
                grow = sb(ph, "grow", [128, D], F32)
                lng_t = sb(ph, "lng_t", [128, D], F32)
                lnb_t = sb(ph, "lnb_t", [128, D], F32)
                stats = sb(ph, "stats", [128, 4, 6], F32)
                mv = sb(ph, "mv", [128, 2], F32)
                lnv = sb(ph, "lnv", [128, 1], F32)
                rstd = sb(ph, "rstd", [128, 1], F32)
                nmr = sb(ph, "nmr", [128, 1], F32)
                dma("sp", yTt[0][:], yTs[0], [], ["yTt0"], "d_yTb")
                for kc in range(kco):
                    dma("pool", wres[:, kc, :], wv[:, kc, :], [], [f"wres{kc}"], "d_wk%d" % kc, max_dma_last_dim=8192)
                dma("sp", grow[:], gate_d[l, :, :], [], ["grow"], "d_c0")
                dma("sp", lng_t[:], dr["lng"][:, l, :], [], ["lng_t"], "d_c1")
                dma("sp", lnb_t[:], dr["lnb"][:, l, :], [], ["lnb_t"], "d_c2")

                def ln_tile(t):
                    zb = t % 2
                    zn = f"z{zb}"
                    xb = t % 2
                    row = t * 128
                    dma("sp", xt[xb][:], xsrc[row:row + 128, :], [], [f"xt{xb}"], f"d_xt{xb}")
                    S.op("dve", lambda e: e.scalar_tensor_tensor(
                        out=z[zb][:], in0=xt[xb][:], scalar=ALPHA, in1=z[zb][:], op0=ALU.mult, op1=ALU.add),
                        [f"xt{xb}", zn], [zn])
                    for q4 in range(4):
                        S.op("dve", lambda e, q4=q4: e.bn_stats(out=stats[:, q4, :], in_=z[zb][:, q4 * 512:(q4 + 1) * 512]),
                             [zn], ["stats"])
                    S.op("dve", lambda e: e.bn_aggr(out=mv[:], in_=stats[:].rearrange("p a b -> p (a b)")), ["stats"], ["mv"])
                    act(lnv[:], mv[:, 1:2], AF.Ln, ["mv", "eps_t"], ["lnv"], bias=eps_t[:, 0:1])
                    act(rstd[:], lnv[:], AF.Exp, ["lnv"], ["rstd"], scale=-0.5)
                    ts("dve", nmr[:], mv[:, 0:1], rstd[:, 0:1], -1.0, ALU.mult, ALU.mult, ["mv", "rstd"], ["nmr"])
                    act(z[zb][:], z[zb][:], AF.Identity, [zn, "rstd", "nmr"], [zn], scale=rstd[:, 0:1], bias=nmr[:, 0:1])
                    tt("pool", z[zb][:], z[zb][:], lng_t[:], ALU.mult, [zn, "lng_t"], [zn])
                    tt("pool", z[zb][:], z[zb][:], lnb_t[:], ALU.add, [zn, "lnb_t"], [zn])
                    dma("sp", dst[row:row + 128, :], z[zb][:], [zn], ["dst_d"], f"d_zo{zb}")

                for t in range(NT):
                    yb = t % 2
                    bset = t % 2
                    if t + 1 < NT:
                        dma("sp", yTt[(t + 1) % 2][:], yTs[t + 1], [], [f"yTt{(t + 1) % 2}"], "d_yTb" if (t + 1) % 2 == 0 else "d_g")
                    for kc in range(kco):
                        for j in range(4):
                            bk = bset * 4 + j
                            mm(ps[bk][:, :], yTt[yb][:, kc, :], wres[:, kc, j * 512:(j + 1) * 512], kc == 0, kc == kco - 1,
                               [f"yTt{yb}", f"wres{kc}"], [PS[bk]])
                    for j in range(4):
                        bk = bset * 4 + j
                        tt("dve", z[yb][:, j * 512:(j + 1) * 512], ps[bk][:, :], grow[:, j * 512:(j + 1) * 512], ALU.mult,
                           [PS[bk], "grow"], [f"z{yb}"])
                    ln_tile(t)
                S.finish()
        phase0()
        for l in layers:
            xsrc = dr["x"] if l == layers[0] else x1_d
            dst = out_d if l == layers[-1] else x1_d
            with contextlib.ExitStack() as lay:
                hT = sb(lay, "hT", [128, KC, SEQ], BF16)
                phase1(l, xsrc, hT)
                if l == 0:
                    phase2_ret(hT)
                else:
                    phase2_moba(hT)
            if l == 0:
                phase3(0, xsrc, dst, dr["ret_w_out"], 32)
            else:
                phase3(1, xsrc, dst, dr["moba_w_out"], 16)
        clear_block()
    return nc


_CACHE = {}


def make_in_maps(inputs, consts, xs_override=None):
    x = np.ascontiguousarray(inputs["x"], dtype=np.float32)
    c = np.asarray(inputs["c"], dtype=np.float32)
    pos = np.asarray(inputs["positions"], dtype=np.int32)
    b_ada = np.asarray(inputs["b_ada"], dtype=np.float32)
    b_fm = np.ascontiguousarray(b_ada.reshape(2, 48, 128).transpose(2, 0, 1))
    b_gate = np.ascontiguousarray(np.broadcast_to(b_ada[None, :, 4096:], (128, 2, 2048)))
    lng = np.ascontiguousarray(np.broadcast_to(np.asarray(inputs["ln_g"], np.float32)[None], (128, 2, 2048)))
    lnb = np.ascontiguousarray(np.broadcast_to(np.asarray(inputs["ln_b"], np.float32)[None], (128, 2, 2048)))
    shared = dict(w_ada=np.ascontiguousarray(inputs["w_ada"], dtype=np.float32), b_fm=b_fm, b_gate=b_gate, lng=lng, lnb=lnb,
                  ret_w_in=np.ascontiguousarray(inputs["ret_w_in"][0], dtype=np.float32),
                  ret_w_out=np.ascontiguousarray(inputs["ret_w_out"][0], dtype=np.float32),
                  moba_w_in=np.ascontiguousarray(inputs["moba_w_in"][0], dtype=np.float32),
                  moba_w_out=np.ascontiguousarray(inputs["moba_w_out"][0], dtype=np.float32))
    shared.update(consts)
    maps = []
    for b in range(x.shape[0]):
        m = dict(shared)
        m["x"] = x[b] if xs_override is None else np.ascontiguousarray(xs_override[b], dtype=np.float32)
        m["cT"] = np.ascontiguousarray(c[b].reshape(16, 128).T)
        m["posb"] = np.ascontiguousarray(np.broadcast_to(pos[b][None, :], (128, SEQ)))
        m["post"] = np.ascontiguousarray(pos[b].reshape(16, 128).T)
        maps.append(m)
    return maps


def run_layers(inputs, layers, xs_override=None, trace=False):
    consts, cdec = host_consts()
    key = tuple(layers)
    if key not in _CACHE:
        _CACHE[key] = build(layers=layers, cdec=cdec)
    nc = _CACHE[key]
    maps = make_in_maps(inputs, consts, xs_override)
    n = len(maps)
    res = run_bass_kernel_spmd(nc, maps, core_ids=list(range(n)), trace=trace)
    out = np.stack([np.asarray(r["out"], dtype=np.float32) for r in res.results], axis=0)
    return out, res


def kernel(**inputs):
    out, _ = run_layers(inputs, (0, 1))
    return out
```

```python
import contextlib
import math
import numpy as np
import ml_dtypes
import concourse.bass as bass
import concourse.mybir as mybir
from concourse.bass_utils import run_bass_kernel_spmd

F32 = mybir.dt.float32
BF16 = mybir.dt.bfloat16
I32 = mybir.dt.int32
AF = mybir.ActivationFunctionType
ALU = mybir.AluOpType
AX = mybir.AxisListType

D = 2048
SEQ = 2048
NT = 16
KC = 16
ALPHA = float((2.0 * 2) ** 0.25)
LN_EPS = 1e-5
GN_EPS = 1e-5
BIG = 30000.0
TWO_PI_LO = 6.28318
COMPUTE = ("pe", "dve", "act", "pool")


class Sched:
    def __init__(self, nc, stack):
        self.nc = nc
        self.stack = stack
        self.sems = {}
        self.cum = {}
        self.known = {e: {} for e in ("pe", "dve", "act", "pool", "sp")}
        self.begin()
        for e in COMPUTE:
            self._sem("eng_" + e)

    def begin(self):
        self.q = {e: [] for e in ("pe", "dve", "act", "pool", "sp")}
        self.res = {}

    def _sem(self, key):
        if key not in self.sems:
            assert not getattr(self, "frozen", False), "semaphore %s not pre-declared" % key
            self.sems[key] = self.stack.enter_context(self.nc.semaphore(key))
            self.cum[key] = 0
        return self.sems[key]

    def _need(self, eng, tok, needs, kind):
        if tok is None:
            return
        key, val, teng = tok
        if teng == eng:
            if eng == "pe" or kind != "raw":
                return
        if self.known[eng].get(key, 0) >= val:
            return
        if needs.get(key, 0) < val:
            needs[key] = val

    def op(self, eng, fn, reads=(), writes=(), dma_sem=None):
        needs = {}
        for r in reads:
            st = self.res.get(r)
            if st is None:
                continue
            self._need(eng, st[0], needs, "raw")
            if r.startswith("ps"):
                for t in st[1].values():
                    self._need(eng, t, needs, "war")
        for w in writes:
            st = self.res.get(w)
            if st is None:
                continue
            self._need(eng, st[0], needs, "waw")
            for t in st[1].values():
                self._need(eng, t, needs, "war")
        for k, v in needs.items():
            self.known[eng][k] = v
        if dma_sem is not None:
            self._sem(dma_sem)
            self.cum[dma_sem] += 16
            tok = (dma_sem, self.cum[dma_sem], "dma")
            inc = (dma_sem, 16)
        else:
            key = "eng_" + eng
            self.cum[key] += 1
            tok = (key, self.cum[key], eng)
            inc = (key, 1)
        self.q[eng].append((list(needs.items()), fn, inc))
        for r in reads:
            st = self.res.setdefault(r, [None, {}])
            st[1][tok[0]] = tok
        for w in writes:
            self.res[w] = [tok, {}]
        return tok

    def finish(self):
        nc = self.nc
        waits = [(k, v) for k, v in self.cum.items() if not k.startswith("eng_") and v > 0]
        self.q["sp"].append((waits, None, None))
        handles = {"pe": "tensor", "dve": "vector", "act": "scalar", "pool": "gpsimd", "sp": "sync"}
        with nc.Block() as block:
            for e, hname in handles.items():
                queue = self.q[e]
                if not queue:
                    continue

                def body(engine, queue=queue):
                    for wl, fn, inc in queue:
                        for k, v in wl:
                            engine.wait_ge(self.sems[k], v)
                        if fn is not None:
                            fn(engine).then_inc(self.sems[inc[0]], inc[1])

                getattr(block, hname)(body)
        self.begin()


def host_consts():
    c = {}
    c["ident_f"] = np.eye(128, dtype=np.float32)
    c["ident_b"] = np.eye(128, dtype=np.float32).astype(ml_dtypes.bfloat16)
    lg = np.log(np.float32(1.0) - np.float32(2.0) ** (-5.0 - np.arange(8, dtype=np.float32))).astype(np.float32)
    n = np.arange(128, dtype=np.float32)
    diff = n[:, None] - n[None, :]
    inner = np.where(diff >= 0, np.exp(np.maximum(diff, 0.0)[None] * lg[:, None, None]), 0.0).astype(np.float32)
    c["ret_mask"] = np.ascontiguousarray(inner.transpose(2, 0, 1) / 16.0).astype(np.float32)
    qdec = np.exp((n[None, :] + 1.0) * lg[:, None]).astype(np.float32)
    c["ret_qdec"] = np.ascontiguousarray(np.broadcast_to(qdec[None], (128, 8, 128))).astype(np.float32)
    kdec = np.exp((127.0 - n[None, :]) * lg[:, None]).astype(np.float32)
    c["ret_kdec"] = np.ascontiguousarray(kdec.T / 16.0).astype(np.float32)
    cdec = [float(np.exp(np.float32(128.0) * lg[h])) for h in range(8)]
    i128 = np.arange(128, dtype=np.float64)
    c["ret_inv"] = ((10000.0 ** (-(i128 * 2.0 / 256.0))).astype(np.float32).astype(np.float64) / (2 * math.pi)).astype(np.float32).reshape(128, 1)
    i16 = np.arange(16, dtype=np.float64)
    minv = ((500000.0 ** (-(i16 * 2.0 / 32.0))).astype(np.float32).astype(np.float64) / (2 * math.pi)).astype(np.float32)
    c["moba_inv"] = np.ascontiguousarray(np.broadcast_to(minv[None], (128, 16))).astype(np.float32)
    m = np.arange(128)
    c["tri"] = (m[:, None] <= m[None, :]).astype(np.float32).astype(ml_dtypes.bfloat16)
    bi = np.zeros((128, 8, 128), np.float32)
    for b in range(8):
        bi[b, b, :] = 1.0
    c["blockind"] = bi.astype(ml_dtypes.bfloat16)
    negm = np.zeros((128, 8, 8), np.float32)
    for i8 in range(8):
        B = (8 + i8) // 2
        negm[:, i8, B:] = -1e30
    c["negmask"] = negm
    sbi = np.zeros((128, 16, 32), np.float32)
    for i in range(16):
        B = i // 2
        sbi[:, i, B + 1:8] = -BIG
    c["selb_init"] = sbi.astype(ml_dtypes.bfloat16)
    sst = np.zeros((128, 1024), np.float32)
    for t in range(8):
        B = t // 2
        sst[B + 1:8, t * 128:(t + 1) * 128] = -BIG
    c["selbT_static"] = sst.astype(ml_dtypes.bfloat16)
    return c, cdec


CONST_SPECS = [("ident_f", [128, 128], F32), ("ident_b", [128, 128], BF16), ("ret_mask", [128, 8, 128], F32),
               ("ret_qdec", [128, 8, 128], F32), ("ret_kdec", [128, 8], F32), ("ret_inv", [128, 1], F32),
               ("moba_inv", [128, 16], F32), ("tri", [128, 128], BF16), ("blockind", [128, 8, 128], BF16),
               ("negmask", [128, 8, 8], F32), ("selb_init", [128, 16, 32], BF16), ("selbT_static", [128, 1024], BF16)]


def build(layers=(0, 1), cdec=None):
    nc = bass.Bass("TRN2", target_bir_lowering=False)
    dr = {}

    def din(name, shape, dt):
        dr[name] = nc.dram_tensor(name, shape, dt, kind="ExternalInput").ap()

    din("x", [SEQ, D], F32)
    din("cT", [128, 16], F32)
    din("posb", [128, SEQ], I32)
    din("post", [128, 16], I32)
    din("w_ada", [2, D, 3 * D], F32)
    din("b_fm", [128, 2, 48], F32)
    din("b_gate", [128, 2, D], F32)
    din("lng", [128, 2, D], F32)
    din("lnb", [128, 2, D], F32)
    din("ret_w_in", [D, 12288], F32)
    din("ret_w_out", [4096, D], F32)
    din("moba_w_in", [D, 8192], F32)
    din("moba_w_out", [D, D], F32)
    for name, shape, dt in CONST_SPECS:
        din(name, shape, dt)
    out_d = nc.dram_tensor("out", [SEQ, D], F32, kind="ExternalOutput").ap()
    yT_d = [nc.dram_tensor("yT0", [16, 128, 32, 128], BF16, kind="Internal").ap(),
            nc.dram_tensor("yT1", [16, 128, 16, 128], BF16, kind="Internal").ap()]
    x1_d = nc.dram_tensor("x1s", [SEQ, D], F32, kind="Internal").ap()
    gate_d = nc.dram_tensor("gates", [2, 128, D], F32, kind="Internal").ap()

    with contextlib.ExitStack() as outer:
        S = Sched(nc, outer)

        uniq = {"n": 0}

        def sb(stack, name, shape, dt):
            uniq["n"] += 1
            return stack.enter_context(nc.sbuf_tensor("s%d_%s" % (uniq["n"], name), shape, dt))

        pall = outer.enter_context(nc.psum_tensor("pall", [128, 4096], F32))
        pall_b = pall[:].bitcast(BF16)
        ps = [pall[:, i * 512:(i + 1) * 512] for i in range(8)]
        psb = [pall_b[:, i * 1024:(i + 1) * 1024] for i in range(8)]
        PS = [f"ps{i}" for i in range(8)]

        ident_f = sb(outer, "ident_f", [128, 128], F32)
        ident_b = sb(outer, "ident_b", [128, 128], BF16)
        mod_fm = sb(outer, "mod_fm", [128, 2, 48], F32)
        s1p = sb(outer, "s1p", [128, 2, 16], F32)
        eps_t = sb(outer, "eps_t", [128, 1], F32)

        def mm(out, lhsT, rhs, start, stop, reads, writes, skip=False):
            S.op("pe", lambda e: e.matmul(out, lhsT=lhsT, rhs=rhs, start=start, stop=stop, skip_group_check=skip),
                 reads, writes)

        def tr(out, in_, ident, reads, writes):
            S.op("pe", lambda e: e.transpose(out=out, in_=in_, identity=ident), reads, writes)

        def act(out, in_, func, reads, writes, scale=1.0, bias=0.0):
            S.op("act", lambda e: e.activation(out=out, in_=in_, func=func, bias=bias, scale=scale), reads, writes)

        def ts(eng, out, in0, s1, s2, op0, op1, reads, writes):
            if op1 is None:
                S.op(eng, lambda e: e.tensor_scalar(out=out, in0=in0, scalar1=s1, scalar2=None, op0=op0), reads, writes)
            else:
                S.op(eng, lambda e: e.tensor_scalar(out=out, in0=in0, scalar1=s1, scalar2=s2, op0=op0, op1=op1),
                     reads, writes)

        def tt(eng, out, in0, in1, op, reads, writes):
            S.op(eng, lambda e: e.tensor_tensor(out=out, in0=in0, in1=in1, op=op), reads, writes)

        def cp(eng, out, in_, reads, writes):
            S.op(eng, lambda e: e.tensor_copy(out=out, in_=in_), reads, writes)

        def dma(q, out, in_, reads, writes, sem, **kw):
            S.op(q, lambda e: e.dma_start(out=out, in_=in_, **kw), reads, writes, dma_sem=sem)

        dma_sem_names = (["d_c%d" % i for i in range(8)] + ["d_wa0", "d_wa1", "d_wr0", "d_wr1", "d_wr2", "d_xs0", "d_xs1",
                         "d_yst0", "d_yst1", "d_yst2", "d_yst3", "d_yTb", "d_wk0", "d_wk1", "d_wk2", "d_wk3", "d_wk4", "d_wk5", "d_wk6", "d_wk7", "d_wk8", "d_wk9", "d_wk10", "d_wk11", "d_wk12", "d_wk13", "d_wk14", "d_wk15", "d_wk16", "d_wk17", "d_wk18", "d_wk19", "d_wk20", "d_wk21", "d_wk22", "d_wk23", "d_wk24", "d_wk25", "d_wk26", "d_wk27", "d_wk28", "d_wk29", "d_wk30", "d_wk31", "d_zo0_0", "d_zo0_1", "d_zo0_2", "d_zo0_3", "d_zo1_0", "d_zo1_1", "d_zo1_2", "d_zo1_3", "d_wo0", "d_wo1", "d_xt0", "d_xt1", "d_zo0", "d_zo1", "d_g", "d_pos"])
        for nme in dma_sem_names:
            S._sem(nme)
        allsems = list(S.sems.values())
        S.frozen = True

        def clear_block():
            with nc.Block() as blk:
                @blk.gpsimd
                def _(g):
                    for s_ in allsems:
                        g.sem_clear(s_)

        clear_block()

        def phase0():
            with contextlib.ExitStack() as ph:
                wa = [sb(ph, f"wa{i}", [128, 3 * D], BF16) for i in range(2)]
                c_f = sb(ph, "c_f", [128, 16], F32)
                c_b = sb(ph, "c_b", [128, 16], BF16)
                ones_b = sb(ph, "ones_b", [128, 128], BF16)
                c_rep = sb(ph, "c_rep", [128, 16, 128], BF16)
                bfm = sb(ph, "bfm", [128, 2, 48], F32)
                bgate = sb(ph, "bgate", [128, 2, D], F32)
                grow = sb(ph, "grow0", [128, D], F32)
                dma("sp", ident_f[:], dr["ident_f"][:, :], [], ["ident_f"], "d_c0")
                dma("sp", ident_b[:], dr["ident_b"][:, :], [], ["ident_b"], "d_c1")
                dma("sp", c_f[:], dr["cT"][:, :], [], ["c_f"], "d_c2")
                dma("sp", bfm[:], dr["b_fm"][:, :, :], [], ["bfm"], "d_c3")
                dma("sp", bgate[:], dr["b_gate"][:, :, :], [], ["bgate"], "d_c4")
                S.op("dve", lambda e: e.memset(eps_t[:], LN_EPS), [], ["eps_t"])
                S.op("dve", lambda e: e.memset(ones_b[:], 1.0), [], ["ones_b"])
                cp("dve", c_b[:], c_f[:], ["c_f"], ["c_b"])
                for ch in range(16):
                    ts("dve", c_rep[:, ch, :], ones_b[:], c_f[:, ch:ch + 1], None, ALU.mult, None,
                       ["ones_b", "c_f"], ["c_rep"])
                cnt = 0
                for l in range(2):
                    for ch in range(16):
                        b = cnt % 2
                        cnt += 1
                        dma("pool", wa[b][:], dr["w_ada"][l, ch * 128:(ch + 1) * 128, :], [], [f"wa{b}"], f"d_wa{b}",
                            max_dma_last_dim=8192)
                        for jt in range(48):
                            mm(ps[0][:, jt:jt + 1], wa[b][:, jt * 128:(jt + 1) * 128], c_b[:, ch:ch + 1],
                               (ch == 0 and jt == 0), (ch == 15), [f"wa{b}", "c_b"], [PS[0]], skip=True)
                        for n_ in range(4):
                            mm(ps[1 + n_][:, :], c_rep[:, ch, :], wa[b][:, 4096 + n_ * 512:4096 + (n_ + 1) * 512],
                               ch == 0, ch == 15, [f"wa{b}", "c_rep"], [PS[1 + n_]])
                    tt("dve", mod_fm[:, l, :], ps[0][:, 0:48], bfm[:, l, :], ALU.add, [PS[0], "bfm"], ["mod_fm"])
                    ts("dve", s1p[:, l, :], mod_fm[:, l, 16:32], 1.0, None, ALU.add, None, ["mod_fm"], ["s1p"])
                    for n_ in range(4):
                        tt("dve", grow[:, n_ * 512:(n_ + 1) * 512], ps[1 + n_][:, :], bgate[:, l, n_ * 512:(n_ + 1) * 512],
                           ALU.add, [PS[1 + n_], "bgate"], ["grow"])
                    dma("sp", gate_d[l, :, :], grow[:], ["grow"], ["gate_d"], "d_g")
                S.finish()

        def phase1(l, xsrc, hT):
            with contextlib.ExitStack() as ph:
                xs = [sb(ph, f"xs{i}", [128, 4, D], F32) for i in range(2)]
                for g in range(4):
                    b = g % 2
                    dma("sp", xs[b][:], xsrc[g * 512:(g + 1) * 512, :].rearrange("(t p) d -> p t d", p=128),
                        [], [f"xs{b}"], f"d_xs{b}")
                    for c in range(16):
                        bk = c % 4
                        for t in range(4):
                            tr(ps[bk][:, t * 128:(t + 1) * 128], xs[b][:, t, c * 128:(c + 1) * 128], ident_f[:],
                               [f"xs{b}", "ident_f"], [PS[bk]])
                        act(hT[:, c, g * 512:(g + 1) * 512], ps[bk][:, :], AF.Identity, [PS[bk], "s1p", "mod_fm"], ["hT"],
                            scale=s1p[:, l, c:c + 1], bias=mod_fm[:, l, c:c + 1])
                S.finish()

        def phase2_ret(hT):
            w_in = dr["ret_w_in"].rearrange("(c p) n -> p c n", p=128)
            yT = yT_d[0]
            with contextlib.ExitStack() as ph:
                wr = [sb(ph, f"wr{i}", [128, 16, 256], BF16) for i in range(3)]
                cosT = sb(ph, "cosT", [128, SEQ], F32)
                sinT = sb(ph, "sinT", [128, SEQ], F32)
                pos_i = sb(ph, "pos_i", [128, 512], I32)
                ta = sb(ph, "ta", [128, 512], F32)
                tb_ = sb(ph, "tb_", [128, 512], F32)
                ti = sb(ph, "ti", [128, 512], I32)
                qT = sb(ph, "qT", [128, 2, SEQ], BF16)
                kT = sb(ph, "kT", [128, 2, SEQ], BF16)
                v = sb(ph, "v", [128, 16, 512], BF16)
                gT = sb(ph, "gT", [128, 4, SEQ], BF16)
                r = [sb(ph, f"r{i}", [128, 512], F32) for i in range(4)]
                sT = [sb(ph, f"sT{i}", [128, 128], BF16) for i in range(2)]
                qd = [sb(ph, f"qd{i}", [128, 2, 128], BF16) for i in range(2)]
                kd = [sb(ph, f"kd{i}", [128, 256], BF16) for i in range(2)]
                st_f = sb(ph, "st_f", [128, 2, 512], F32)
                st_b = sb(ph, "st_b", [128, 2, 512], BF16)
                on_ = [sb(ph, f"on{i}", [128, 512], BF16) for i in range(2)]
                stats = sb(ph, "stats", [128, 6], F32)
                mv = sb(ph, "mv", [128, 2], F32)
                lnv = sb(ph, "lnv", [128, 1], F32)
                rstd = sb(ph, "rstd", [128, 1], F32)
                yst = [sb(ph, f"yst{i}", [128, 4, 128], BF16) for i in range(4)]
                mask = sb(ph, "mask", [128, 8, 128], F32)
                qdec = sb(ph, "qdec", [128, 8, 128], F32)
                kdec = sb(ph, "kdec", [128, 8], F32)
                inv = sb(ph, "inv", [128, 1], F32)
                dma("sp", mask[:], dr["ret_mask"][:, :, :], [], ["mask"], "d_c0")
                dma("sp", qdec[:], dr["ret_qdec"][:, :, :], [], ["qdec"], "d_c1")
                dma("sp", kdec[:], dr["ret_kdec"][:, :], [], ["kdec"], "d_c2")
                dma("sp", inv[:], dr["ret_inv"][:, :], [], ["inv"], "d_c3")

                pieces = []
                for h in range(8):
                    pieces += [("q", h, h * 256), ("k", h, 2048 + h * 256), ("v0", h, 4096 + h * 512),
                               ("v1", h, 4096 + h * 512 + 256), ("g0", h, 8192 + h * 512), ("g1", h, 8192 + h * 512 + 256)]
                state = {"next": 0}

                def prefetch(upto):
                    while state["next"] <= upto and state["next"] < len(pieces):
                        i = state["next"]
                        b = i % 3
                        col = pieces[i][2]
                        dma("pool", wr[b][:], w_in[:, :, col:col + 256], [], [f"wr{b}"], f"d_wr{b}")
                        state["next"] += 1

                prefetch(1)

                for pc in range(4):
                    sl = slice(pc * 512, (pc + 1) * 512)
                    dma("sp", pos_i[:], dr["posb"][:, sl], [], ["pos_i"], "d_pos")
                    cp("dve", ta[:], pos_i[:], ["pos_i"], ["ta"])
                    ts("dve", ta[:], ta[:], inv[:, 0:1], None, ALU.mult, None, ["ta", "inv"], ["ta"])
                    cp("dve", ti[:], ta[:], ["ta"], ["ti"])
                    cp("dve", tb_[:], ti[:], ["ti"], ["tb_"])
                    tt("dve", ta[:], ta[:], tb_[:], ALU.subtract, ["ta", "tb_"], ["ta"])
                    ts("dve", tb_[:], ta[:], 0.5, None, ALU.is_gt, None, ["ta"], ["tb_"])
                    tt("dve", ta[:], ta[:], tb_[:], ALU.subtract, ["ta", "tb_"], ["ta"])
                    act(sinT[:, sl], ta[:], AF.Sin, ["ta"], ["sinT"], scale=TWO_PI_LO)
                    ts("dve", tb_[:], ta[:], 0.25, None, ALU.add, None, ["ta"], ["tb_"])
                    ts("dve", ta[:], tb_[:], 0.5, None, ALU.is_gt, None, ["tb_", "sinT"], ["ta"])
                    tt("dve", tb_[:], tb_[:], ta[:], ALU.subtract, ["ta", "tb_"], ["tb_"])
                    act(cosT[:, sl], tb_[:], AF.Sin, ["tb_"], ["cosT"], scale=TWO_PI_LO)

                bankc = {"n": 0}
                pend = []

                def flush(keep):
                    while len(pend) > keep:
                        pend.pop(0)()

                def next_bank():
                    b = bankc["n"] % 3
                    bankc["n"] += 1
                    return b

                def proj_fm(wb, half, tbk, bk):
                    for kc in range(KC):
                        mm(ps[bk][:, :], wr[wb][:, kc, half * 128:(half + 1) * 128], hT[:, kc, tbk * 512:(tbk + 1) * 512],
                           kc == 0, kc == KC - 1, [f"wr{wb}", "hT"], [PS[bk]])

                pi = 0
                for h in range(8):
                    for dst, dname in ((qT, "qT"), (kT, "kT")):
                        wb = pi % 3
                        prefetch(pi + 2)
                        for tbk in range(4):
                            sl = slice(tbk * 512, (tbk + 1) * 512)
                            bA = next_bank()
                            proj_fm(wb, 0, tbk, bA)
                            bB = next_bank()
                            proj_fm(wb, 1, tbk, bB)
                            tt("dve", r[0][:], ps[bA][:, :], cosT[:, sl], ALU.mult, [PS[bA], "cosT"], ["r0"])
                            tt("dve", r[1][:], ps[bB][:, :], sinT[:, sl], ALU.mult, [PS[bB], "sinT"], ["r1"])
                            tt("dve", r[2][:], ps[bB][:, :], cosT[:, sl], ALU.mult, [PS[bB], "cosT"], ["r2"])
                            tt("dve", r[3][:], ps[bA][:, :], sinT[:, sl], ALU.mult, [PS[bA], "sinT"], ["r3"])
                            tt("pool", dst[:, 0, sl], r[0][:], r[1][:], ALU.subtract, ["r0", "r1"], [dname])
                            tt("pool", dst[:, 1, sl], r[2][:], r[3][:], ALU.add, ["r2", "r3"], [dname])
                        pi += 1
                    for vp in range(2):
                        wb = pi % 3
                        prefetch(pi + 2)
                        for t in range(NT):
                            bk = next_bank()
                            for kc in range(KC):
                                mm(ps[bk][:, 0:256], hT[:, kc, t * 128:(t + 1) * 128], wr[wb][:, kc, :],
                                   kc == 0, kc == KC - 1, [f"wr{wb}", "hT"], [PS[bk]])
                            act(v[:, t, vp * 256:(vp + 1) * 256], ps[bk][:, 0:256], AF.Identity, [PS[bk]], ["v"])
                        pi += 1
                    wb_g = [pi % 3, (pi + 1) % 3]
                    prefetch(pi + 2)
                    pi += 2
                    ggroups = [(tbk, gp, half) for tbk in range(4) for gp in range(2) for half in range(2)]
                    gstate = {"n": 0}

                    def emit_g(idx, incore=True):
                        tbk, gp, half = ggroups[idx]
                        bk = 1 if incore else next_bank()
                        wbg = wb_g[gp]
                        for kc in range(KC):
                            mm(ps[bk][:, :], wr[wbg][:, kc, half * 128:(half + 1) * 128], hT[:, kc, tbk * 512:(tbk + 1) * 512],
                               kc == 0, kc == KC - 1, [f"wr{wbg}", "hT"], [PS[bk]])
                        act(gT[:, gp * 2 + half, tbk * 512:(tbk + 1) * 512], ps[bk][:, :], AF.Silu, [PS[bk]], ["gT"])

                    for idx in range(4):
                        emit_g(idx, incore=False)
                    for c in range(16):
                        cs = slice(c * 128, (c + 1) * 128)
                        bf = c % 2
                        ob_ = 5 if c % 2 == 0 else 2
                        if c < 15:
                            for half in range(2):
                                tr(psb[3][:, half * 128:(half + 1) * 128], kT[:, half, cs], ident_b[:],
                                   ["kT", "ident_b"], [PS[3]])
                            act(kd[bf][:], psb[3][:, 0:256], AF.Identity, [PS[3], "kdec"], [f"kd{bf}"],
                                scale=kdec[:, h:h + 1])
                        for half in range(2):
                            mm(ps[4][:, 0:128], kT[:, half, cs], qT[:, half, cs], half == 0, half == 1,
                               ["kT", "qT"], [PS[4]])
                        tt("dve", sT[bf][:], ps[4][:, 0:128], mask[:, h, :], ALU.mult, [PS[4], "mask"], [f"sT{bf}"])
                        if c > 0:
                            for half in range(2):
                                tt("pool", qd[bf][:, half, :], qT[:, half, cs], qdec[:, h, :], ALU.mult,
                                   ["qT", "qdec"], [f"qd{bf}"])
                        if c < 12:
                            emit_g(4 + c)
                        mm(ps[ob_][:, :], sT[bf][:], v[:, c, :], True, c == 0, [f"sT{bf}", "v"], [PS[ob_]])
                        if c > 0:
                            for half in range(2):
                                mm(ps[ob_][:, :], qd[bf][:, half, :], st_b[:, half, :], False, half == 1,
                                   [f"qd{bf}", "st_b"], [PS[ob_]])
                        if c < 15:
                            for half in range(2):
                                bk = 6 + half
                                mm(ps[bk][:, :], kd[bf][:, half * 128:(half + 1) * 128], v[:, c, :], True, True,
                                   [f"kd{bf}", "v"], [PS[bk]])
                                if c == 0:
                                    cp("dve", st_f[:, half, :], ps[bk][:, :], [PS[bk]], [f"st_f{half}"])
                                else:
                                    S.op("dve", lambda e, half=half, bk=bk, cd=cdec[h]: e.scalar_tensor_tensor(
                                        out=st_f[:, half, :], in0=st_f[:, half, :], scalar=cd, in1=ps[bk][:, :],
                                        op0=ALU.mult, op1=ALU.add), [PS[bk], f"st_f{half}"], [f"st_f{half}"])
                                act(st_b[:, half, :], st_f[:, half, :], AF.Identity, [f"st_f{half}"], ["st_b"])
                        S.op("dve", lambda e, ob_=ob_: e.bn_stats(out=stats[:], in_=ps[ob_][:, :]), [PS[ob_]], ["stats"])
                        S.op("dve", lambda e: e.bn_aggr(out=mv[:], in_=stats[:]), ["stats"], ["mv"])
                        act(lnv[:], mv[:, 1:2], AF.Ln, ["mv", "eps_t"], ["lnv"], bias=eps_t[:, 0:1])
                        act(rstd[:], lnv[:], AF.Exp, ["lnv"], ["rstd"], scale=-0.5)
                        ts("dve", on_[bf][:], ps[ob_][:, :], mv[:, 0:1], rstd[:, 0:1], ALU.subtract, ALU.mult,
                           [PS[ob_], "mv", "rstd"], [f"on{bf}"])

                        def fin(c=c, cs=cs, bf=bf, h=h):
                            for fc in range(4):
                                tr(psb[0][:, fc * 128:(fc + 1) * 128], on_[bf][:, fc * 128:(fc + 1) * 128], ident_b[:],
                                   [f"on{bf}", "ident_b"], [PS[0]])
                            yb = c % 4
                            tt("dve", yst[yb][:], psb[0][:, 0:512].rearrange("p (f t) -> p f t", t=128), gT[:, :, cs], ALU.mult,
                               [PS[0], "gT"], [f"yst{yb}"])
                            dma("sp", yT[c, :, h * 4:(h + 1) * 4, :], yst[yb][:], [f"yst{yb}"], ["yT_d"], f"d_yst{yb}")

                        pend.append(fin)
                        flush(1)
                    flush(0)
                S.finish()

        def phase2_moba(hT):
            w_in = dr["moba_w_in"].rearrange("(c p) n -> p c n", p=128)
            yT = yT_d[1]
            scale = 128.0 ** -0.5
            with contextlib.ExitStack() as ph:
                wr = [sb(ph, f"wr{i}", [128, 16, 256], BF16) for i in range(3)]
                qtm = [sb(ph, f"qtm{i}", [128, 256], BF16) for i in range(3)]
                qT2 = [sb(ph, f"qT{i}", [128, 2, SEQ], BF16) for i in range(2)]
                kT2 = [sb(ph, f"kT{i}", [128, 2, SEQ], BF16) for i in range(2)]
                vext2 = [sb(ph, f"vext{i}", [128, 16, 2, 130], BF16) for i in range(2)]
                gT2 = [sb(ph, f"gT{i}", [128, 2, SEQ], BF16) for i in range(2)]
                PT = sb(ph, "PT", [128, 16, 512], BF16)
                pos_i = sb(ph, "pos_i", [128, 16], I32)
                posf = sb(ph, "posf", [128, 16], F32)
                minv = sb(ph, "minv", [128, 16], F32)
                ta = sb(ph, "ta", [128, 16, 16], F32)
                tb_ = sb(ph, "tb_", [128, 16, 16], F32)
                ti = sb(ph, "ti", [128, 16, 16], I32)
                cos2 = sb(ph, "cos2", [128, 16, 2, 16], F32)
                sin2 = sb(ph, "sin2", [128, 16, 2, 16], F32)
                r = [[sb(ph, f"r{s_}_{i}", [128, 2, 16], F32) for i in range(4)] for s_ in range(2)]
                kmf = sb(ph, "kmf", [128, 2, 8], F32)
                kmb = sb(ph, "kmb", [128, 2, 8], BF16)
                gm = [sb(ph, f"gm{i}", [128, 8, 8], F32) for i in range(2)]
                top8 = [sb(ph, f"top8{i}", [128, 8, 8], F32) for i in range(2)]
                negmask = sb(ph, "negmask", [128, 8, 8], F32)
                selb = [sb(ph, f"selb{i}", [128, 16, 32], BF16) for i in range(2)]
                selbT = [[sb(ph, f"selbT{s_}_{i}", [128, SEQ], BF16) for i in range(2)] for s_ in range(2)]
                blockind = sb(ph, "blockind", [128, 8, 128], BF16)
                tri = sb(ph, "tri", [128, 128], BF16)
                o_n = [sb(ph, f"o_n{i}", [128, 4, 128], BF16) for i in range(2)]
                rden = [sb(ph, f"rden{i}", [128, 1], F32) for i in range(2)]
                yst = [sb(ph, f"yst{i}", [128, 512], BF16) for i in range(2)]
                dma("sp", negmask[:], dr["negmask"][:, :, :], [], ["negmask"], "d_c0")
                dma("sp", blockind[:], dr["blockind"][:, :, :], [], ["blockind"], "d_c2")
                dma("sp", tri[:], dr["tri"][:, :], [], ["tri"], "d_c3")
                dma("sp", minv[:], dr["moba_inv"][:, :], [], ["minv"], "d_c4")
                dma("sp", pos_i[:], dr["post"][:, :], [], ["pos_i"], "d_c5")
                sidx = 0
                for hd in range(2):
                    dma("sp", selb[hd][:], dr["selb_init"][:, :, :], [], [f"selb{hd}"], "d_c1" if hd == 0 else "d_c7")
                    for s_ in range(2):
                        S.op("dve", lambda e, hd=hd, s_=s_: e.memset(selbT[s_][hd][:, 1024:2048], 0.0), [], [f"selbT_d{s_}_{hd}"])
                        dma("sp", selbT[s_][hd][:, 0:1024], dr["selbT_static"][:, :], [], [f"selbT_s{s_}_{hd}"],
                            ["d_c6", "d_pos", "d_xs0", "d_xs1"][sidx])
                        sidx += 1
                for s_ in range(2):
                    S.op("dve", lambda e, s_=s_: e.memset(vext2[s_][:, :, :, 128:130], 1.0), [], [f"vext1_{s_}"])

                pieces = []
                for g in range(8):
                    pieces += [("q", g, g * 256), ("k", g, 2048 + g * 256), ("v", g, 4096 + g * 256), ("g", g, 6144 + g * 256)]
                state = {"next": 0}

                def prefetch(upto):
                    while state["next"] <= upto and state["next"] < len(pieces):
                        i = state["next"]
                        b = i % 3
                        col = pieces[i][2]
                        dma("pool", wr[b][:], w_in[:, :, col:col + 256], [], [f"wr{b}"], f"d_wr{b}")
                        state["next"] += 1

                prefetch(1)

                cp("dve", posf[:], pos_i[:], ["pos_i"], ["posf"])
                for t in range(16):
                    ts("dve", ta[:, t, :], minv[:], posf[:, t:t + 1], None, ALU.mult, None, ["minv", "posf"], ["ta"])
                cp("dve", ti[:], ta[:], ["ta"], ["ti"])
                cp("dve", tb_[:], ti[:], ["ti"], ["tb_"])
                tt("dve", ta[:], ta[:], tb_[:], ALU.subtract, ["ta", "tb_"], ["ta"])
                ts("dve", tb_[:], ta[:], 0.5, None, ALU.is_gt, None, ["ta"], ["tb_"])
                tt("dve", ta[:], ta[:], tb_[:], ALU.subtract, ["ta", "tb_"], ["ta"])
                for hd in range(2):
                    act(sin2[:, :, hd, :], ta[:], AF.Sin, ["ta"], ["sin2"], scale=TWO_PI_LO)
                ts("dve", tb_[:], ta[:], 0.25, None, ALU.add, None, ["ta"], ["tb_"])
                ts("dve", ta[:], tb_[:], 0.5, None, ALU.is_gt, None, ["tb_", "sin2"], ["ta"])
                tt("dve", tb_[:], tb_[:], ta[:], ALU.subtract, ["ta", "tb_"], ["tb_"])
                for hd in range(2):
                    act(cos2[:, :, hd, :], tb_[:], AF.Sin, ["tb_"], ["cos2"], scale=TWO_PI_LO)

                bankc = {"n": 0, "po": 0, "t": 0, "pi": 0, "ob": 0}

                def next_bank():
                    b = bankc["n"] % 2
                    bankc["n"] += 1
                    return b

                def inproj_units(g):
                    st_ = g % 2
                    qT, kT, vext, gT = qT2[st_], kT2[st_], vext2[st_], gT2[st_]
                    qn, kn, vn, gn = f"qT{st_}", f"kT{st_}", f"vext{st_}", f"gT{st_}"
                    pend = []

                    def flush(keep):
                        while len(pend) > keep:
                            pend.pop(0)()

                    for dst, dname in ((qT, qn), (kT, kn)):
                        pi = bankc["pi"]
                        wb = pi % 3
                        prefetch(pi + 2)
                        for t in range(NT):
                            bk = next_bank()
                            qb_ = bankc["t"] % 3
                            rs = r[bankc["t"] % 2]
                            rn = ["r%d_%d" % (bankc["t"] % 2, i_) for i_ in range(4)]
                            bankc["t"] += 1
                            for kc in range(KC):
                                mm(ps[bk][:, 0:256], hT[:, kc, t * 128:(t + 1) * 128], wr[wb][:, kc, :],
                                   kc == 0, kc == KC - 1, [f"wr{wb}", "hT"], [PS[bk]])
                            psv = ps[bk][:, 0:256].rearrange("p (h d) -> p h d", d=128)
                            qv = qtm[qb_][:].rearrange("p (h d) -> p h d", d=128)
                            act(qv[:, :, 32:128], psv[:, :, 32:128], AF.Identity, [PS[bk]], [f"qtm{qb_}"])
                            tt("dve", rs[0][:], psv[:, :, 0:16], cos2[:, t, :, :], ALU.mult, [PS[bk], "cos2"], [rn[0]])
                            tt("dve", rs[1][:], psv[:, :, 16:32], sin2[:, t, :, :], ALU.mult, [PS[bk], "sin2"], [rn[1]])
                            tt("dve", rs[2][:], psv[:, :, 16:32], cos2[:, t, :, :], ALU.mult, [PS[bk], "cos2"], [rn[2]])
                            tt("dve", rs[3][:], psv[:, :, 0:16], sin2[:, t, :, :], ALU.mult, [PS[bk], "sin2"], [rn[3]])
                            tt("pool", qv[:, :, 0:16], rs[0][:], rs[1][:], ALU.subtract, [rn[0], rn[1]], [f"qtm{qb_}"])
                            tt("pool", qv[:, :, 16:32], rs[2][:], rs[3][:], ALU.add, [rn[2], rn[3]], [f"qtm{qb_}"])

                            def fin(t=t, qb_=qb_, dst=dst, dname=dname):
                                for hd in range(2):
                                    tr(psb[3][:, hd * 128:(hd + 1) * 128], qtm[qb_][:, hd * 128:(hd + 1) * 128], ident_b[:],
                                       [f"qtm{qb_}", "ident_b"], [PS[3]])
                                act(dst[:, :, t * 128:(t + 1) * 128], psb[3][:, 0:256].rearrange("p (h d) -> p h d", d=128),
                                    AF.Identity, [PS[3]], [dname])

                            pend.append(fin)
                            flush(2)
                            yield
                        bankc["pi"] += 1
                    pi = bankc["pi"]
                    wb = pi % 3
                    prefetch(pi + 2)
                    for t in range(NT):
                        bk = next_bank()
                        for kc in range(KC):
                            mm(ps[bk][:, 0:256], hT[:, kc, t * 128:(t + 1) * 128], wr[wb][:, kc, :],
                               kc == 0, kc == KC - 1, [f"wr{wb}", "hT"], [PS[bk]])
                        act(vext[:, t, :, 0:128], ps[bk][:, 0:256].rearrange("p (h d) -> p h d", d=128), AF.Identity,
                            [PS[bk]], [vn])
                        if t == 1:
                            flush(0)
                        if t == 3:
                            S.op("dve", lambda e: e.tensor_reduce(out=kmf[:], in_=kT[:].rearrange("p h (n k) -> p h n k", k=256),
                                                                  axis=AX.X, op=ALU.add), [kn], ["kmf"])
                            ts("dve", kmb[:], kmf[:], 1.0 / 256.0, None, ALU.mult, None, ["kmf"], ["kmb"])
                            for hd in range(2):
                                for i8 in range(8):
                                    i = 8 + i8
                                    c_ = 384 + hd * 64 + i8 * 8
                                    mm(ps[7][:, c_:c_ + 8], qT[:, hd, i * 128:(i + 1) * 128],
                                       kmb[:, hd, :], True, True, [qn, "kmb"], [PS[7]])
                        if t == 5:
                            for hd in range(2):
                                tt("dve", gm[hd][:].rearrange("p a b -> p (a b)"), ps[7][:, 384 + hd * 64:384 + (hd + 1) * 64],
                                   negmask[:].rearrange("p a b -> p (a b)"), ALU.add, [PS[7], "negmask"], [f"gm{hd}"])
                                for i8 in range(8):
                                    S.op("dve", lambda e, i8=i8, hd=hd: e.max(out=top8[hd][:, i8, :], in_=gm[hd][:, i8, :]),
                                         [f"gm{hd}"], [f"top8{hd}"])
                                for i8 in range(8):
                                    i = 8 + i8
                                    B = i // 2
                                    ts("dve", selb[hd][:, i, 0:B], gm[hd][:, i8, 0:B], top8[hd][:, i8, 2:3], -BIG, ALU.is_lt, ALU.mult,
                                       [f"gm{hd}", f"top8{hd}"], [f"selb{hd}"])
                        if t == 9 or t == 11:
                            hd = 0 if t == 9 else 1
                            for i8 in range(8):
                                tr(psb[3][0:32, i8 * 128:(i8 + 1) * 128], selb[hd][:, 8 + i8, :], ident_b[:],
                                   [f"selb{hd}", "ident_b"], [PS[3]])
                            cp("dve", selbT[st_][hd][0:32, 1024:2048], psb[3][0:32, 0:1024], [PS[3]], [f"selbT_d{st_}_{hd}"])
                        yield
                    bankc["pi"] += 1
                    pi = bankc["pi"]
                    wb = pi % 3
                    prefetch(pi + 2)
                    for hd in range(2):
                        for tbk in range(4):
                            bk = next_bank()
                            for kc in range(KC):
                                mm(ps[bk][:, :], wr[wb][:, kc, hd * 128:(hd + 1) * 128], hT[:, kc, tbk * 512:(tbk + 1) * 512],
                                   kc == 0, kc == KC - 1, [f"wr{wb}", "hT"], [PS[bk]])
                            act(gT[:, hd, tbk * 512:(tbk + 1) * 512], ps[bk][:, :], AF.Silu, [PS[bk]], [gn])
                            yield
                    bankc["pi"] += 1

                def attention_units(g):
                    st_ = g % 2
                    qT, kT, vext, gT = qT2[st_], kT2[st_], vext2[st_], gT2[st_]
                    qn, kn, vn, gn = f"qT{st_}", f"kT{st_}", f"vext{st_}", f"gT{st_}"
                    for hd in range(2):
                        sT_ = selbT[st_][hd]
                        for qb in range(4):
                            qs = slice(qb * 512, (qb + 1) * 512)
                            nkt = 4 * (qb + 1)
                            for p2 in range(nkt // 2):
                                j0 = 2 * p2
                                b0 = 4
                                c0 = 256 if j0 == nkt - 2 else 0
                                ncol = 512 - c0
                                qsl = slice(qb * 512 + c0, (qb + 1) * 512)
                                for jj in range(2):
                                    j = j0 + jj
                                    need_mask = (qb >= 2) and (j < 4 * qb + 2)
                                    mm(ps[b0 + jj][:, 0:ncol], kT[:, hd, j * 128:(j + 1) * 128], qT[:, hd, qsl], True, not need_mask,
                                       [kn, qn], [PS[b0 + jj]])
                                    if need_mask:
                                        mm(ps[b0 + jj][:, 0:ncol], blockind[:, j // 2, :], sT_[:, qsl], False, True,
                                           ["blockind", f"selbT_s{st_}_{hd}", f"selbT_d{st_}_{hd}"], [PS[b0 + jj]])
                                src = pall[:, b0 * 512:(b0 + 2) * 512].rearrange("p (a n) -> p a n", a=2)[:, :, 0:ncol]
                                act(PT[:, j0:j0 + 2, c0:512], src, AF.Exp, [PS[b0], PS[b0 + 1]], ["PT"], scale=scale)
                                for jj in range(2):
                                    j = j0 + jj
                                    if j >= 4 * qb:
                                        il = j - 4 * qb
                                        tt("pool", PT[:, j, il * 128:(il + 1) * 128], PT[:, j, il * 128:(il + 1) * 128],
                                           tri[:], ALU.mult, ["PT", "tri"], ["PT"])
                                yield
                            ob = bankc["ob"] % 2
                            bankc["ob"] += 1
                            trs = []
                            for il in range(4):
                                i = 4 * qb + il
                                pk = 6 if bankc["po"] % 2 == 0 else 2
                                rb = bankc["po"] % 2
                                bankc["po"] += 1
                                for j in range(i + 1):
                                    mm(ps[pk][:, 0:129], PT[:, j, il * 128:(il + 1) * 128], vext[:, j, hd, 0:129],
                                       j == 0, j == i, ["PT", vn, f"vext1_{st_}"], [PS[pk]])
                                S.op("dve", lambda e, pk=pk, rb=rb: e.reciprocal(out=rden[rb][:], in_=ps[pk][:, 128:129]),
                                     [PS[pk]], [f"rden{rb}"])
                                ts("dve", o_n[ob][:, il, :], ps[pk][:, 0:128], rden[rb][:, 0:1], None, ALU.mult, None,
                                   [PS[pk], f"rden{rb}"], [f"o_n{ob}"])
                                yield
                                if il >= 1:
                                    tr(psb[7][:, (il - 1) * 128:il * 128], o_n[ob][:, il - 1, :], ident_b[:],
                                       [f"o_n{ob}", "ident_b"], [PS[7]])
                            yield
                            tr(psb[7][:, 3 * 128:4 * 128], o_n[ob][:, 3, :], ident_b[:], [f"o_n{ob}", "ident_b"], [PS[7]])
                            yield
                            hg = g * 2 + hd
                            tt("dve", yst[ob][:], psb[7][:, 0:512], gT[:, hd, qs], ALU.mult, [PS[7], gn], [f"yst{ob}"])
                            dma("sp", yT[qb * 4:(qb + 1) * 4, :, hg, :].rearrange("t p k -> p t k"),
                                yst[ob][:].rearrange("p (t k) -> p t k", k=128), [f"yst{ob}"], ["yT_d"], f"d_yst{ob}")

                prev_att = None
                for g in range(8):
                    for _ in inproj_units(g):
                        if prev_att is not None:
                            for _k in range(2):
                                next(prev_att, None)
                    if prev_att is not None:
                        for _ in prev_att:
                            pass
                    prev_att = attention_units(g)
                for _ in prev_att:
                    pass
                S.finish()

        def phase3(l, xsrc, dst, w_out, kco):
            wv = w_out.rearrange("(c p) n -> p c n", p=128)
            yTs = yT_d[l]
            with contextlib.ExitStack() as ph:
                wres = sb(ph, "wres", [128, kco, D], BF16)
                yTt = [sb(ph, f"yTt{i}", [128, kco, 128], BF16) for i in range(2)]
                z = [sb(ph, f"z{i}", [128, D], F32) for i in range(2)]
                xt = [sb(ph, f"xt{i}", [128, D], F32) for i in range(2)]
                grow = sb(ph, "grow", [128, D], F32)
                lng_t = sb(ph, "lng_t", [128, D], F32)
                lnb_t = sb(ph, "lnb_t", [128, D], F32)
                stats = sb(ph, "stats", [128, 4, 6], F32)
                mv = sb(ph, "mv", [128, 2], F32)
                lnv = sb(ph, "lnv", [128, 1], F32)
                rstd = sb(ph, "rstd", [128, 1], F32)
                nmr = sb(ph, "nmr", [128, 1], F32)
                dma("sp", yTt[0][:], yTs[0], [], ["yTt0"], "d_yTb")
                for kc in range(kco):
                    dma("pool", wres[:, kc, :], wv[:, kc, :], [], [f"wres{kc}"], "d_wk%d" % kc, max_dma_last_dim=8192)
                dma("sp", grow[:], gate_d[l, :, :], [], ["grow"], "d_c0")
                dma("sp", lng_t[:], dr["lng"][:, l, :], [], ["lng_t"], "d_c1")
                dma("sp", lnb_t[:], dr["lnb"][:, l, :], [], ["lnb_t"], "d_c2")

                def ln_tile(t):
                    zb = t % 2
                    zn = f"z{zb}"
                    xb = t % 2
                    row = t * 128
                    S.op("dve", lambda e: e.scalar_tensor_tensor(
                        out=z[zb][:], in0=xt[xb][:], scalar=ALPHA, in1=z[zb][:], op0=ALU.mult, op1=ALU.add),
                        [f"xt{xb}", zn], [zn])
                    for q4 in range(4):
                        S.op("dve", lambda e, q4=q4: e.bn_stats(out=stats[:, q4, :], in_=z[zb][:, q4 * 512:(q4 + 1) * 512]),
                             [zn], ["stats"])
                    S.op("dve", lambda e: e.bn_aggr(out=mv[:], in_=stats[:].rearrange("p a b -> p (a b)")), ["stats"], ["mv"])
                    act(lnv[:], mv[:, 1:2], AF.Ln, ["mv", "eps_t"], ["lnv"], bias=eps_t[:, 0:1])
                    act(rstd[:], lnv[:], AF.Exp, ["lnv"], ["rstd"], scale=-0.5)
                    ts("dve", nmr[:], mv[:, 0:1], rstd[:, 0:1], -1.0, ALU.mult, ALU.mult, ["mv", "rstd"], ["nmr"])
                    act(z[zb][:], z[zb][:], AF.Identity, [zn, "rstd", "nmr"], [zn], scale=rstd[:, 0:1], bias=nmr[:, 0:1])
                    tt("pool", z[zb][:], z[zb][:], lng_t[:], ALU.mult, [zn, "lng_t"], [zn])
                    tt("pool", z[zb][:], z[zb][:], lnb_t[:], ALU.add, [zn, "lnb_t"], [zn])
                    dma("sp", dst[row:row + 128, :], z[zb][:], [zn], ["dst_d"], f"d_zo{zb}")

                for t in range(NT):
                    yb = t % 2
                    bset = t % 2
                    dma("sp", xt[t % 2][:], xsrc[t * 128:(t + 1) * 128, :], [], [f"xt{t % 2}"], f"d_xt{t % 2}")
                    if t + 1 < NT:
                        dma("sp", yTt[(t + 1) % 2][:], yTs[t + 1], [], [f"yTt{(t + 1) % 2}"], "d_yTb" if (t + 1) % 2 == 0 else "d_g")
                    for kc in range(kco):
                        for j in range(4):
                            bk = bset * 4 + j
                            mm(ps[bk][:, :], yTt[yb][:, kc, :], wres[:, kc, j * 512:(j + 1) * 512], kc == 0, kc == kco - 1,
                               [f"yTt{yb}", f"wres{kc}"], [PS[bk]])
                    for j in range(4):
                        bk = bset * 4 + j
                        tt("dve", z[yb][:, j * 512:(j + 1) * 512], ps[bk][:, :], grow[:, j * 512:(j + 1) * 512], ALU.mult,
                           [PS[bk], "grow"], [f"z{yb}"])
                    ln_tile(t)
                S.finish()
        phase0()
        for l in layers:
            xsrc = dr["x"] if l == layers[0] else x1_d
            dst = out_d if l == layers[-1] else x1_d
            with contextlib.ExitStack() as lay:
                hT = sb(lay, "hT", [128, KC, SEQ], BF16)
                phase1(l, xsrc, hT)
                if l == 0:
                    phase2_ret(hT)
                else:
                    phase2_moba(hT)
            if l == 0:
                phase3(0, xsrc, dst, dr["ret_w_out"], 32)
            else:
                phase3(1, xsrc, dst, dr["moba_w_out"], 16)
        clear_block()
    return nc


_CACHE = {}


def make_in_maps(inputs, consts, xs_override=None):
    x = np.ascontiguousarray(inputs["x"], dtype=np.float32)
    c = np.asarray(inputs["c"], dtype=np.float32)
    pos = np.asarray(inputs["positions"], dtype=np.int32)
    b_ada = np.asarray(inputs["b_ada"], dtype=np.float32)
    b_fm = np.ascontiguousarray(b_ada.reshape(2, 48, 128).transpose(2, 0, 1))
    b_gate = np.ascontiguousarray(np.broadcast_to(b_ada[None, :, 4096:], (128, 2, 2048)))
    lng = np.ascontiguousarray(np.broadcast_to(np.asarray(inputs["ln_g"], np.float32)[None], (128, 2, 2048)))
    lnb = np.ascontiguousarray(np.broadcast_to(np.asarray(inputs["ln_b"], np.float32)[None], (128, 2, 2048)))
    shared = dict(w_ada=np.ascontiguousarray(inputs["w_ada"], dtype=np.float32), b_fm=b_fm, b_gate=b_gate, lng=lng, lnb=lnb,
                  ret_w_in=np.ascontiguousarray(inputs["ret_w_in"][0], dtype=np.float32),
                  ret_w_out=np.ascontiguousarray(inputs["ret_w_out"][0], dtype=np.float32),
                  moba_w_in=np.ascontiguousarray(inputs["moba_w_in"][0], dtype=np.float32),
                  moba_w_out=np.ascontiguousarray(inputs["moba_w_out"][0], dtype=np.float32))
    shared.update(consts)
    maps = []
    for b in range(x.shape[0]):
        m = dict(shared)
        m["x"] = x[b] if xs_override is None else np.ascontiguousarray(xs_override[b], dtype=np.float32)
        m["cT"] = np.ascontiguousarray(c[b].reshape(16, 128).T)
        m["posb"] = np.ascontiguousarray(np.broadcast_to(pos[b][None, :], (128, SEQ)))
        m["post"] = np.ascontiguousarray(pos[b].reshape(16, 128).T)
        maps.append(m)
    return maps


def run_layers(inputs, layers, xs_override=None, trace=False):
    consts, cdec = host_consts()
    key = tuple(layers)
    if key not in _CACHE:
        _CACHE[key] = build(layers=layers, cdec=cdec)
    nc = _CACHE[key]
    maps = make_in_maps(inputs, consts, xs_override)
    n = len(maps)
    res = run_bass_kernel_spmd(nc, maps, core_ids=list(range(n)), trace=trace)
    out = np.stack([np.asarray(r["out"], dtype=np.float32) for r in res.results], axis=0)
    return out, res


def kernel(**inputs):
    out, _ = run_layers(inputs, (0, 1))
    return out
```
